# Optimizing a Trainium2 kernel written in Bass

```python
import math
import jax, jax.numpy as jnp
from jax import lax
import numpy as np

D_MODEL = 1024
BATCH = 8
SEQ = 8192
DEPTH = 4
DEC_BATCH = 8
DEC_SEQ = 4096
PAST_LEN = 128

HEAD_DIM = 64
CONV_WIDTH = D_MODEL // 2
CONV_K = 3
SWA_HEADS = D_MODEL // 128
SWA_KV_HEADS = SWA_HEADS // 4
WINDOW = 128
AX_HEADS = D_MODEL // 128
AX_KV_HEADS = AX_HEADS // 4
DIFF_HEADS = D_MODEL // 256
MEM_HEADS = 4
N_MEM = 256
GRID_W = 64
Q_BLOCK = 128
ROPE_THETA = 10000.0
EPS = 1e-6
NEG = -1e30

SWA_Q = SWA_HEADS * HEAD_DIM
SWA_KV = SWA_KV_HEADS * HEAD_DIM
AX_Q = AX_HEADS * HEAD_DIM
AX_KV = AX_KV_HEADS * HEAD_DIM
DIFF_QK = 2 * DIFF_HEADS * HEAD_DIM
DIFF_V = DIFF_HEADS * 2 * HEAD_DIM
MEM_WIDTH = MEM_HEADS * HEAD_DIM
MIX_WIDTH = CONV_WIDTH + SWA_Q + MEM_WIDTH
EVEN_SPLITS = (CONV_WIDTH, CONV_WIDTH, CONV_WIDTH, SWA_Q, SWA_KV, SWA_KV, MEM_WIDTH, MIX_WIDTH)
ODD_SPLITS = (AX_Q, AX_KV, AX_KV, DIFF_QK, DIFF_QK, DIFF_V, MEM_WIDTH, MIX_WIDTH)
IN_WIDTH = sum(EVEN_SPLITS)
N_EVEN = (DEPTH + 1) // 2
N_ODD = DEPTH // 2

kernel_name = 'hybrid_conv_swa_axial_diff_encoder'


def rms_norm(x, g):
    xf = x.astype(jnp.float32)
    y = xf * lax.rsqrt(jnp.mean(xf * xf, axis=-1, keepdims=True) + EPS)
    return (y * g.astype(jnp.float32)).astype(x.dtype)


def heads(t, n):
    return t.reshape(t.shape[:-1] + (n, t.shape[-1] // n))


def split_cols(u, sizes):
    idx = [int(i) for i in np.cumsum(sizes)[:-1]]
    return jnp.split(u, idx, axis=-1)


def rope_angles(pos, dim):
    inv = ROPE_THETA ** (-jnp.arange(0, dim, 2, dtype=jnp.float32) / dim)
    return pos.astype(jnp.float32)[:, None] * inv[None, :]


def apply_rope(x, ang):
    cos = jnp.cos(ang)[None, :, None, :].astype(x.dtype)
    sin = jnp.sin(ang)[None, :, None, :].astype(x.dtype)
    x1, x2 = jnp.split(x, 2, axis=-1)
    return jnp.concatenate([x1 * cos - x2 * sin, x2 * cos + x1 * sin], axis=-1)


def axial_rope(x, row_ang, col_ang):
    half = HEAD_DIM // 2
    return jnp.concatenate([apply_rope(x[..., :half], row_ang),
                            apply_rope(x[..., half:], col_ang)], axis=-1)


def sweep_query_blocks(fn, q):
    b, s = q.shape[:2]
    nb = s // Q_BLOCK
    qb = jnp.moveaxis(q.reshape((b, nb, Q_BLOCK) + q.shape[2:]), 1, 0)
    out = jnp.moveaxis(lax.map(fn, qb), 0, 1)
    return out.reshape((b, s) + out.shape[3:])


def short_conv(gb, gc, hc, w):
    inner = gc * hc
    p = jnp.pad(inner, ((0, 0), (1, 1), (0, 0)))
    conv = p[:, :-2] * w[0] + p[:, 1:-1] * w[1] + p[:, 2:] * w[2]
    return gb * conv


def window_attention(q, k, v, sink):
    b, s, h, d = q.shape
    hkv = k.shape[2]
    g = h // hkv
    nb = s // WINDOW
    qb = q.reshape(b, nb, WINDOW, hkv, g, d)

    def band(t):
        tp = jnp.pad(t.reshape(b, nb, WINDOW, hkv, d), ((0, 0), (1, 1), (0, 0), (0, 0), (0, 0)))
        return jnp.concatenate([tp[:, :-2], tp[:, 1:-1], tp[:, 2:]], axis=2)

    kb, vb = band(k), band(v)
    sc = jnp.einsum('bnqkgd,bnjkd->bnkgqj', qb, kb, preferred_element_type=jnp.float32) * (d ** -0.5)
    blk = jnp.arange(nb)[:, None, None]
    qpos = blk * WINDOW + jnp.arange(WINDOW)[None, :, None]
    kpos = (blk - 1) * WINDOW + jnp.arange(3 * WINDOW)[None, None, :]
    mask = (jnp.abs(qpos - kpos) <= WINDOW) & (kpos >= 0) & (kpos < s)
    sc = jnp.where(mask[None, :, None, None], sc, NEG)
    sink_l = jnp.broadcast_to(sink.astype(jnp.float32).reshape(1, 1, hkv, g, 1, 1), sc.shape[:-1] + (1,))
    p = jax.nn.softmax(jnp.concatenate([sc, sink_l], axis=-1), axis=-1)[..., :-1]
    o = jnp.einsum('bnkgqj,bnjkd->bnqkgd', p.astype(v.dtype), vb)
    return o.reshape(b, s, h * d)


def dense_gqa(q, k, v):
    b, s, h, d = q.shape
    hkv = k.shape[2]
    g = h // hkv
    scale = d ** -0.5

    def block(qi):
        sc = jnp.einsum('bqkgd,bskd->bkgqs', qi, k, preferred_element_type=jnp.float32) * scale
        p = jax.nn.softmax(sc, axis=-1).astype(v.dtype)
        return jnp.einsum('bkgqs,bskd->bqkgd', p, v)

    o = sweep_query_blocks(block, q.reshape(b, s, hkv, g, d))
    return o.reshape(b, s, h * d)


def diff_attention(q, k, v, lam_vec, subln_g, layer):
    b, s, h2, d = q.shape
    h = h2 // 2
    lambda_init = 0.8 - 0.6 * math.exp(-0.3 * layer)
    lv = lam_vec.astype(jnp.float32)
    lam = jnp.exp(jnp.sum(lv[0] * lv[1])) - jnp.exp(jnp.sum(lv[2] * lv[3])) + lambda_init
    scale = d ** -0.5

    def block(qi):
        sc = jnp.einsum('bqhd,bshd->bhqs', qi, k, preferred_element_type=jnp.float32) * scale
        p = jax.nn.softmax(sc, axis=-1).reshape(b, h, 2, qi.shape[1], s)
        a = p[:, :, 0] - lam * p[:, :, 1]
        return jnp.einsum('bhqs,bshe->bqhe', a.astype(v.dtype), v)

    o = sweep_query_blocks(block, q)
    o = rms_norm(o, subln_g) * (1.0 - lambda_init)
    return o.reshape(b, s, h * 2 * d)


def memory_attention(q, mk, mv):
    b, s, hm, d = q.shape
    sc = jnp.einsum('bqhd,bmhd->bhqm', q, mk, preferred_element_type=jnp.float32) * (d ** -0.5)
    p = jax.nn.softmax(sc, axis=-1).astype(mv.dtype)
    return jnp.einsum('bhqm,bmhd->bqhd', p, mv).reshape(b, s, hm * d)


def trunk(x, mem, norm_g, w_in, w_out, mem_norm_g, w_mem_kv, mem_qk_g, conv_w, swa_qk_g,
          swa_sink, ax_qk_g, diff_qk_g, diff_lambda, diff_subln_g):
    b, s, _ = x.shape
    rows = s // GRID_W
    ang_1d = rope_angles(jnp.arange(s), HEAD_DIM)
    row_ang = rope_angles(jnp.repeat(jnp.arange(rows), GRID_W), HEAD_DIM // 2)
    col_ang = rope_angles(jnp.tile(jnp.arange(GRID_W), rows), HEAD_DIM // 2)
    for l in range(DEPTH):
        h = rms_norm(x, norm_g[l])
        u = h @ w_in[l]
        mkv = rms_norm(mem, mem_norm_g[l]) @ w_mem_kv[l]
        mk, mv = jnp.split(mkv, 2, axis=-1)
        mk = rms_norm(heads(mk, MEM_HEADS), mem_qk_g[l, 1])
        mv = heads(mv, MEM_HEADS)
        if l % 2 == 0:
            e = l // 2
            gb, gc, hc, q, k, v, mq, z = split_cols(u, EVEN_SPLITS)
            y1 = short_conv(gb, gc, hc, conv_w[e])
            q = apply_rope(rms_norm(heads(q, SWA_HEADS), swa_qk_g[e, 0]), ang_1d)
            k = apply_rope(rms_norm(heads(k, SWA_KV_HEADS), swa_qk_g[e, 1]), ang_1d)
            y2 = window_attention(q, k, heads(v, SWA_KV_HEADS), swa_sink[e])
        else:
            o = l // 2
            q, k, v, dq, dk, dv, mq, z = split_cols(u, ODD_SPLITS)
            q = axial_rope(rms_norm(heads(q, AX_HEADS), ax_qk_g[o, 0]), row_ang, col_ang)
            k = axial_rope(rms_norm(heads(k, AX_KV_HEADS), ax_qk_g[o, 1]), row_ang, col_ang)
            y1 = dense_gqa(q, k, heads(v, AX_KV_HEADS))
            dq = apply_rope(rms_norm(heads(dq, 2 * DIFF_HEADS), diff_qk_g[o, 0]), ang_1d)
            dk = apply_rope(rms_norm(heads(dk, 2 * DIFF_HEADS), diff_qk_g[o, 1]), ang_1d)
            y2 = diff_attention(dq, dk, heads(dv, DIFF_HEADS), diff_lambda[o], diff_subln_g[o], l)
        mq = rms_norm(heads(mq, MEM_HEADS), mem_qk_g[l, 0])
        ym = memory_attention(mq, mk, mv)
        y = jnp.concatenate([y1, y2, ym], axis=-1)
        x = x + (y * jax.nn.silu(z)) @ w_out[l]
    return x


def setup_inputs(seed: int = 0) -> dict:
    key = jax.random.key(seed)
    ks = jax.random.split(key, 18)
    f32 = jnp.float32
    nrm = lambda k, shp, sc: jax.random.normal(k, shp, f32) * sc
    gain = lambda k, shp: 1.0 + 0.02 * jax.random.normal(k, shp, f32)
    return {
        'x_prompt': nrm(ks[0], (BATCH, SEQ, D_MODEL), 1.0),
        'x_sample': nrm(ks[1], (DEC_BATCH, DEC_SEQ, D_MODEL), 1.0),
        'mem_prompt': nrm(ks[2], (BATCH, N_MEM, D_MODEL), 1.0),
        'mem_sample': nrm(ks[3], (DEC_BATCH, N_MEM, D_MODEL), 1.0),
        'norm_g': gain(ks[4], (DEPTH, D_MODEL)),
        'w_in': nrm(ks[5], (DEPTH, D_MODEL, IN_WIDTH), D_MODEL ** -0.5),
        'w_out': nrm(ks[6], (DEPTH, MIX_WIDTH, D_MODEL), MIX_WIDTH ** -0.5),
        'mem_norm_g': gain(ks[7], (DEPTH, D_MODEL)),
        'w_mem_kv': nrm(ks[8], (DEPTH, D_MODEL, 2 * MEM_WIDTH), D_MODEL ** -0.5),
        'mem_qk_g': gain(ks[9], (DEPTH, 2, HEAD_DIM)),
        'conv_w': nrm(ks[10], (N_EVEN, CONV_K, CONV_WIDTH), CONV_K ** -0.5),
        'swa_qk_g': gain(ks[11], (N_EVEN, 2, HEAD_DIM)),
        'swa_sink': nrm(ks[12], (N_EVEN, SWA_HEADS), 0.5),
        'ax_qk_g': gain(ks[13], (N_ODD, 2, HEAD_DIM)),
        'diff_qk_g': gain(ks[14], (N_ODD, 2, HEAD_DIM)),
        'diff_lambda': nrm(ks[15], (N_ODD, 4, HEAD_DIM), 0.1),
        'diff_subln_g': gain(ks[16], (N_ODD, 2 * HEAD_DIM)),
    }


def reference(x_prompt, x_sample, mem_prompt, mem_sample, norm_g, w_in, w_out, mem_norm_g,
              w_mem_kv, mem_qk_g, conv_w, swa_qk_g, swa_sink, ax_qk_g, diff_qk_g,
              diff_lambda, diff_subln_g):
    y_prompt = trunk(x_prompt, mem_prompt, norm_g, w_in, w_out, mem_norm_g, w_mem_kv, mem_qk_g,
                     conv_w, swa_qk_g, swa_sink, ax_qk_g, diff_qk_g, diff_lambda, diff_subln_g)
    y_sample = trunk(x_sample, mem_sample, norm_g, w_in, w_out, mem_norm_g, w_mem_kv, mem_qk_g,
                     conv_w, swa_qk_g, swa_sink, ax_qk_g, diff_qk_g, diff_lambda, diff_subln_g)
    return (y_prompt, y_sample)
```

```python
import math
import numpy as np
import concourse.bass as bass
import concourse.mybir as mybir
from concourse.bass_utils import run_bass_kernel_spmd

F32 = mybir.dt.float32
BF16 = mybir.dt.bfloat16
U8 = mybir.dt.uint8
AF = mybir.ActivationFunctionType
ALU = mybir.AluOpType

D = 1024
INW = 3840
MIXW = 1280
NMEM = 256
EPS = 1e-6
NEGM = -30000.0
ENGS = ("pe", "act", "dve", "pool", "sp")


class Buf:
    __slots__ = ("name", "w", "r")

    def __init__(self, name):
        self.name = name
        self.w = {}
        self.r = {}


class DmaSem:
    __slots__ = ("handle", "count", "name")

    def __init__(self, handle, name):
        self.handle = handle
        self.count = 0
        self.name = name


class Plan:
    def __init__(self, nc):
        self.nc = nc
        self.items = {e: [] for e in ENGS}
        self.count = {e: 0 for e in ENGS}
        self.waited = {e: {} for e in ENGS}
        self.sems = {}
        for e in ENGS:
            if e != "sp":
                self.sems[e] = nc.alloc_semaphore(name=f"s_{e}")
        self.dmasems = []
        self.n_inst = 0

    def dma_sem(self, name):
        s = DmaSem(self.nc.alloc_semaphore(name=f"d_{name}"), name)
        self.dmasems.append(s)
        return s

    def _deps(self, eng, reads, writes):
        deps = {}
        for b in reads:
            for k, v in b.w.items():
                if deps.get(k, 0) < v:
                    deps[k] = v
        for b in writes:
            for k, v in b.w.items():
                if deps.get(k, 0) < v:
                    deps[k] = v
            for k, v in b.r.items():
                if deps.get(k, 0) < v:
                    deps[k] = v
        waits = []
        wd = self.waited[eng]
        for k, v in deps.items():
            if k == "pe" and eng == "pe":
                continue
            if wd.get(k, 0) >= v:
                continue
            wd[k] = v
            waits.append((k, v))
        return waits

    def _mark(self, key, val, reads, writes):
        for b in reads:
            if b.r.get(key, 0) < val:
                b.r[key] = val
        for b in writes:
            if b.r:
                b.w = {}
                b.r = {}
            b.w[key] = val

    def op(self, eng, fn, reads=(), writes=()):
        waits = self._deps(eng, reads, writes)
        self.count[eng] += 1
        self.items[eng].append((waits, fn, (eng, 1)))
        self._mark(eng, self.count[eng], reads, writes)
        self.n_inst += 1

    def dma(self, q, fn, sem, reads=(), writes=()):
        waits = self._deps(q, reads, writes)
        sem.count += 16
        self.items[q].append((waits, fn, (sem, 16)))
        self._mark(sem, sem.count, reads, writes)
        self.n_inst += 1

    def wait_all(self, eng, bufs):
        waits = self._deps(eng, bufs, ())
        self.items[eng].append((waits, None, None))

    def barrier(self):
        snap = [(e, self.count[e]) for e in ENGS if e != "sp"]
        snap += [(s, s.count) for s in self.dmasems]
        for e in ENGS:
            waits = []
            wd = self.waited[e]
            for k, v in snap:
                if v == 0 or k == e:
                    continue
                if wd.get(k, 0) >= v:
                    continue
                wd[k] = v
                waits.append((k, v))
            if waits:
                self.items[e].append((waits, None, None))

    def _semh(self, k):
        return k.handle if isinstance(k, DmaSem) else self.sems[k]

    def replay(self, block):
        plan = self

        def run(engine, name):
            for waits, fn, inc in plan.items[name]:
                for k, v in waits:
                    engine.wait_ge(plan._semh(k), v)
                if fn is None:
                    continue
                inst = fn(engine)
                inst.then_inc(plan._semh(inc[0]), inc[1])

        @block.tensor
        def _(e):
            run(e, "pe")

        @block.scalar
        def _(e):
            run(e, "act")

        @block.vector
        def _(e):
            run(e, "dve")

        @block.gpsimd
        def _(e):
            run(e, "pool")

        @block.sync
        def _(e):
            run(e, "sp")


class Tl:
    __slots__ = ("ap", "buf", "_sem", "plan", "name")

    def __init__(self, plan, name, ap):
        self.plan = plan
        self.name = name
        self.ap = ap
        self.buf = Buf(name)
        self._sem = None

    @property
    def sem(self):
        if self._sem is None:
            self._sem = self.plan.dma_sem(self.name)
        return self._sem


def _const_mats():
    ident = np.eye(128, dtype=np.float32)
    k = np.arange(128)
    bd = (k[:, None] // 64 == k[None, :] // 64).astype(np.float32) / 64.0
    onesm = np.full((128, 128), 1.0 / 128.0, np.float32)
    ones = np.ones((128, 128), np.float32)
    rot1 = np.zeros((128, 128), np.float32)
    rota = np.zeros((128, 128), np.float32)
    for m in range(128):
        if (m % 64) < 32:
            rot1[m + 32, m] = -1.0
        else:
            rot1[m - 32, m] = 1.0
        if (m % 32) < 16:
            rota[m + 16, m] = -1.0
        else:
            rota[m - 16, m] = 1.0
    return np.concatenate([ident, bd, onesm, ones, rot1, rota], axis=1)


def _masks():
    ki = np.arange(128)[:, None]
    qi = np.arange(128)[None, :]
    mp = np.where(qi <= ki, 0.0, NEGM).astype(np.float32)
    mn = np.where(ki <= qi, 0.0, NEGM).astype(np.float32)
    return np.concatenate([mp, mp, mn, mn], axis=1)


def _rope_tables(smax):
    theta = np.float32(10000.0)
    pos = np.arange(smax)
    f = np.arange(128) % 64
    inv1 = (theta ** (-(np.arange(0, 64, 2, dtype=np.float32)) / np.float32(64))).astype(np.float32)
    ang1 = (pos.astype(np.float32)[None, :] * inv1[f % 32][:, None]).astype(np.float32)
    inva = (theta ** (-(np.arange(0, 32, 2, dtype=np.float32)) / np.float32(32))).astype(np.float32)
    prow = (pos // 64).astype(np.float32)
    pcol = (pos % 64).astype(np.float32)
    fa = f % 32
    pa = np.where((f < 32)[:, None], prow[None, :], pcol[None, :]).astype(np.float32)
    anga = (pa * inva[fa % 16][:, None]).astype(np.float32)
    tabs = np.stack([np.cos(ang1.astype(np.float64)), np.sin(ang1.astype(np.float64)),
                     np.cos(anga.astype(np.float64)), np.sin(anga.astype(np.float64))]).astype(np.float32)
    return tabs


def _gpack(depth, norm_g, mem_norm_g, mem_qk_g, conv_w, swa_qk_g, swa_sink, ax_qk_g, diff_qk_g,
           diff_lambda, diff_subln_g):
    cols = {}
    parts = []
    pos = [0]

    def add(name, a):
        a = np.ascontiguousarray(a, dtype=np.float32)
        assert a.shape[0] == 128
        cols[name] = pos[0]
        parts.append(a)
        pos[0] += a.shape[1]

    dup = lambda v: np.tile(v, 2)[:, None]
    for l in range(depth):
        add(f"ng{l}", norm_g[l].reshape(8, 128).T)
        add(f"mng{l}", mem_norm_g[l].reshape(8, 128).T)
        add(f"mqg{l}", dup(mem_qk_g[l, 0]))
        add(f"mkg{l}", dup(mem_qk_g[l, 1]))
        if l % 2 == 0:
            e = l // 2
            add(f"sqg{e}", dup(swa_qk_g[e, 0]))
            add(f"skg{e}", dup(swa_qk_g[e, 1]))
            add(f"cw{e}", conv_w[e].reshape(3, 4, 128).transpose(2, 0, 1).reshape(128, 12))
            add(f"sink{e}", np.broadcast_to(swa_sink[e][None, :], (128, 8)))
        else:
            o = l // 2
            add(f"aqg{o}", dup(ax_qk_g[o, 0]))
            add(f"akg{o}", dup(ax_qk_g[o, 1]))
            add(f"dqg{o}", dup(diff_qk_g[o, 0]))
            add(f"dkg{o}", dup(diff_qk_g[o, 1]))
            add(f"sub{o}", diff_subln_g[o][:, None])
            add(f"lam{o}", np.broadcast_to(diff_lambda[o].reshape(1, 256), (128, 256)))
    return np.concatenate(parts, axis=1), cols


def _gpack_cols(depth):
    z = lambda *s: np.zeros(s, np.float32)
    _, cols = _gpack(depth, z(depth, D), z(depth, D), z(depth, 2, 64), z((depth + 1) // 2, 3, 512),
                     z((depth + 1) // 2, 2, 64), z((depth + 1) // 2, 8), z(max(depth // 2, 1), 2, 64),
                     z(max(depth // 2, 1), 2, 64), z(max(depth // 2, 1), 4, 64), z(max(depth // 2, 1), 128))
    g, _ = _gpack(depth, z(depth, D), z(depth, D), z(depth, 2, 64), z((depth + 1) // 2, 3, 512),
                  z((depth + 1) // 2, 2, 64), z((depth + 1) // 2, 8), z(max(depth // 2, 1), 2, 64),
                  z(max(depth // 2, 1), 2, 64), z(max(depth // 2, 1), 4, 64), z(max(depth // 2, 1), 128))
    return cols, g.shape[1]


def build_program(SP, SS, DEPTH, debug_layers=None):
    nc = bass.Bass("TRN2", target_bir_lowering=False)
    T = SP + SS
    NG = T // 512
    SMAX = max(SP, SS)
    segs = [(0, SP), (SP, SS)]
    gcols, NGC = _gpack_cols(DEPTH)

    x_p = nc.dram_tensor("x_p", [SP, D], F32, kind="ExternalInput").ap()
    x_s = nc.dram_tensor("x_s", [SS, D], F32, kind="ExternalInput").ap()
    mem = nc.dram_tensor("mem", [2 * NMEM, D], F32, kind="ExternalInput").ap()
    w_in = nc.dram_tensor("w_in", [DEPTH, D, INW], F32, kind="ExternalInput").ap()
    w_out = nc.dram_tensor("w_out", [DEPTH, MIXW, D], F32, kind="ExternalInput").ap()
    w_mem = nc.dram_tensor("w_mem", [DEPTH, D, 512], F32, kind="ExternalInput").ap()
    gpack_d = nc.dram_tensor("gpack", [128, NGC], F32, kind="ExternalInput").ap()
    cmat_d = nc.dram_tensor("cmat", [128, 768], F32, kind="ExternalInput").ap()
    masks_d = nc.dram_tensor("masks", [128, 512], F32, kind="ExternalInput").ap()
    tabs_d = nc.dram_tensor("tabs", [4, 128, SMAX], F32, kind="ExternalInput").ap()
    y_p = nc.dram_tensor("y_p", [SP, D], F32, kind="ExternalOutput").ap()
    y_s = nc.dram_tensor("y_s", [SS, D], F32, kind="ExternalOutput").ap()
    FM = nc.dram_tensor("FM", [30, 128, T], BF16, kind="Internal").ap()
    FMm = nc.dram_tensor("FMm", [2, 128, 512], BF16, kind="Internal").ap()
    VM = nc.dram_tensor("VM", [6, 128, T // 128, 128], BF16, kind="Internal").ap()
    VMm = nc.dram_tensor("VMm", [4, 128, 4, 128], BF16, kind="Internal").ap()
    YT = nc.dram_tensor("YT", [10, 128, T], BF16, kind="Internal").ap()

    x_in = [x_p, x_s]
    y_out = [y_p, y_s]

    P = Plan(nc)

    b_FM = [Buf(f"FM{g}") for g in range(NG)]
    b_VM = [Buf(f"VM{g}") for g in range(NG)]
    b_YT = [Buf(f"YT{g}") for g in range(NG)]
    b_Y = [Buf(f"Y{g}") for g in range(NG)]
    b_FMm = Buf("FMm")
    b_VMm = Buf("VMm")

    def seg_of_group(g):
        return 0 if g * 512 < SP else 1

    def groups_of_seg(s):
        t0, ln = segs[s]
        return list(range(t0 // 512, (t0 + ln) // 512))

    def sb(name, shape, dt):
        return Tl(P, name, nc.alloc_sbuf_tensor("sb_" + name, shape, dt).ap())

    G = sb("G", [128, NGC], F32)
    DV = sb("DV", [128, 64], F32)
    cmat_f = sb("cmat_f", [128, 768], F32)
    cmat = sb("cmat", [128, 768], BF16)
    masks_f = sb("masks_f", [128, 512], F32)
    masks = sb("masks", [128, 512], BF16)
    stat = sb("stat", [128, 64], F32)
    ident = cmat.ap[:, 0:128]
    bdm = cmat.ap[:, 128:256]
    onesm = cmat.ap[:, 256:384]
    ones = cmat.ap[:, 384:512]
    rot1 = cmat.ap[:, 512:640]
    rota = cmat.ap[:, 640:768]
    b_c = cmat.buf

    ARENA_BYTES = 172 * 1024
    arena = nc.alloc_sbuf_tensor("arena", [128, ARENA_BYTES], U8).ap()

    class Arena:
        def __init__(self, tag):
            self.off = 0
            self.tag = tag
            self.n = 0

        def take(self, name, shape, dt):
            esz = 4 if dt == F32 else 2
            n = int(np.prod(shape[1:])) * esz
            self.off = (self.off + 63) // 64 * 64
            assert self.off + n <= ARENA_BYTES, (self.tag, name, self.off, n)
            a = arena[0:shape[0], self.off:self.off + n].bitcast(dt)
            self.off += n
            if len(shape) == 3:
                a = a.rearrange("p (a b) -> p a b", a=shape[1])
            elif len(shape) == 4:
                a = a.rearrange("p (a b c) -> p a b c", a=shape[1], b=shape[2])
            return Tl(P, f"{self.tag}_{name}", a)

    psall = nc.alloc_psum_tensor("psall", [128, 4096], F32).ap()
    banks = [Tl(P, f"ps{i}", psall[:, i * 512:(i + 1) * 512]) for i in range(8)]
    ones_f = cmat_f.ap[:, 384:512]

    def MM(out, lhsT, rhs, start, stop, R, W):
        P.op("pe", lambda e: e.matmul(out, lhsT=lhsT, rhs=rhs, start=start, stop=stop), R, W)

    def TRN(out, in_, R, W):
        P.op("pe", lambda e: e.transpose(out, in_, ident), list(R) + [b_c], W)

    def ACT(out, in_, func, R, W, scale=None, bias=None, accum=None):
        kw = {}
        if scale is not None:
            kw["scale"] = scale
        if bias is not None:
            kw["bias"] = bias
        if accum is not None:
            kw["accum_out"] = accum
        P.op("act", lambda e: e.activation(out=out, in_=in_, func=func, **kw), R, W)

    def TT(eng, out, in0, in1, op, R, W):
        P.op(eng, lambda e: e.tensor_tensor(out=out, in0=in0, in1=in1, op=op), R, W)

    def TS(eng, out, in0, s1, op0, R, W, s2=None, op1=None):
        if op1 is None:
            P.op(eng, lambda e: e.tensor_scalar(out=out, in0=in0, scalar1=s1, scalar2=None, op0=op0), R, W)
        else:
            P.op(eng, lambda e: e.tensor_scalar(out=out, in0=in0, scalar1=s1, scalar2=s2, op0=op0, op1=op1), R, W)

    def STT(out, in0, scalar, in1, op0, op1, R, W):
        P.op("dve", lambda e: e.scalar_tensor_tensor(out=out, in0=in0, scalar=scalar, in1=in1, op0=op0, op1=op1), R, W)

    def CP(eng, out, in_, R, W):
        if eng == "act":
            P.op("act", lambda e: e.activation(out=out, in_=in_, func=AF.Copy), R, W)
        else:
            P.op(eng, lambda e: e.tensor_copy(out=out, in_=in_), R, W)

    def RECIP(out, in_, R, W):
        P.op("dve", lambda e: e.reciprocal(out=out, in_=in_), R, W)

    def REDUCE(out, in_, R, W):
        P.op("dve", lambda e: e.tensor_reduce(out=out, in_=in_, axis=mybir.AxisListType.X, op=ALU.add), R, W)

    def MEMSET(eng, ap, val, W):
        P.op(eng, lambda e: e.memset(ap, val), (), W)

    def DMA(out, in_, sem, R, W, q="sp"):
        P.dma(q, lambda e: e.dma_start(out=out, in_=in_), sem, R, W)

    DMA(G.ap, gpack_d, G.sem, [], [G.buf])
    DMA(cmat_f.ap, cmat_d, cmat_f.sem, [], [cmat_f.buf])
    DMA(masks_f.ap, masks_d, masks_f.sem, [], [masks_f.buf])
    CP("dve", cmat.ap, cmat_f.ap, [cmat_f.buf], [cmat.buf])
    CP("dve", masks.ap, masks_f.ap, [masks_f.buf], [masks.buf])
    MEMSET("dve", DV.ap[:, 0:1], EPS, [DV.buf])
    EPSB = DV.ap[:, 0:1]
    dvc = {}
    dvpos = [1]

    def dv_take(name, n=1):
        dvc[name] = dvpos[0]
        dvpos[0] += n
        return DV.ap[:, dvc[name]:dvc[name] + n]

    for l in range(DEPTH):
        if l % 2 == 0:
            e = l // 2
            es = dv_take(f"esink{e}", 8)
            ACT(es, G.ap[:, gcols[f"sink{e}"]:gcols[f"sink{e}"] + 8], AF.Exp, [G.buf], [DV.buf])
        else:
            o = l // 2
            lam_init = 0.8 - 0.6 * math.exp(-0.3 * l)
            lc = gcols[f"lam{o}"]
            tmpv = dv_take(f"lamtmp{o}", 4)
            prod = sb(f"lamprod{o}", [128, 128], F32)
            TT("dve", prod.ap[:, 0:64], G.ap[:, lc:lc + 64], G.ap[:, lc + 64:lc + 128], ALU.mult, [G.buf], [prod.buf])
            TT("dve", prod.ap[:, 64:128], G.ap[:, lc + 128:lc + 192], G.ap[:, lc + 192:lc + 256], ALU.mult,
               [G.buf], [prod.buf])
            REDUCE(tmpv[:, 0:2], prod.ap.rearrange("p (a b) -> p a b", a=2), [prod.buf], [DV.buf])
            ACT(tmpv[:, 2:4], tmpv[:, 0:2], AF.Exp, [DV.buf], [DV.buf])
            nl = dv_take(f"nlam{o}", 1)
            TS("dve", nl, tmpv[:, 3:4], -lam_init, ALU.add, [DV.buf], [DV.buf])
            TT("dve", nl, nl, tmpv[:, 2:3], ALU.subtract, [DV.buf], [DV.buf])
            gsv = dv_take(f"gs{o}", 1)
            TS("dve", gsv, G.ap[:, gcols[f"sub{o}"]:gcols[f"sub{o}"] + 1], 1.0 - lam_init, ALU.mult,
               [G.buf], [DV.buf])

    def gcol(name, n=1, off=0):
        c = gcols[name] + off
        return G.ap[:, c:c + n]

    aP = Arena("P")
    WI = aP.take("WI", [128, 8, INW], BF16)
    WM = aP.take("WM", [128, 8, 512], BF16)
    wst = [aP.take(f"wst{i}", [128, 960], F32) for i in range(2)]
    xt = [aP.take(f"xt{i}", [128, D], F32) for i in range(4)]
    xb = [aP.take(f"xb{i}", [128, D], BF16) for i in range(4)]
    xT = [aP.take(f"xT{i}", [128, 8, 512], BF16) for i in range(2)]
    tabs = [[aP.take(f"tab{s}_{i}", [128, 512], F32) for i in range(4)] for s in range(2)]
    NSL = 3
    u_sb = [aP.take(f"u{i}", [128, 512], F32) for i in range(NSL)]
    sq_sb = [aP.take(f"sq{i}", [128, 512], BF16) for i in range(NSL)]
    rs_sb = [aP.take(f"rs{i}", [128, 512], F32) for i in range(NSL)]
    t_sb = [aP.take(f"t{i}", [128, 512], BF16) for i in range(NSL)]
    a_sb = [aP.take(f"a{i}", [128, 512], F32) for i in range(NSL)]
    b_sb = [aP.take(f"b{i}", [128, 512], F32) for i in range(NSL)]
    oc_sb = [aP.take(f"oc{i}", [128, 512], BF16) for i in range(4)]
    vst = [aP.take(f"vst{i}", [128, 6, 128], BF16) for i in range(2)]
    sst = [aP.take(f"sst{i}", [128, 4], F32) for i in range(4)]

    pP_main = [banks[0], banks[1], banks[2]]
    pP_ms = [banks[3], banks[3]]
    pP_rot = [banks[4], banks[5]]
    pP_tp = Tl(P, "tpw", psall[:, 6 * 512:8 * 512])

    rr = {"cast": 0, "oc": 0, "xt": 0, "xb": 0, "main": 0, "c2": 0, "vst": 0, "ms": 0, "rot": 0, "xT": 0, "tab": 0}

    def load_cast_weights(l):
        engs = ["act", "dve", "pool"]
        for k in range(8):
            for hf in range(4):
                s = wst[rr["cast"] % 2]
                DMA(s.ap, w_in[l, k * 128:(k + 1) * 128, hf * 960:(hf + 1) * 960], s.sem, [], [s.buf])
                eng = engs[rr["cast"] % 3]
                rr["cast"] += 1
                dst = WI.ap[:, k, hf * 960:(hf + 1) * 960]
                sc = gcol(f"ng{l}", 1, k)
                if eng == "act":
                    ACT(dst, s.ap, AF.Copy, [s.buf, G.buf], [WI.buf], scale=sc)
                else:
                    TS(eng, dst, s.ap, sc, ALU.mult, [s.buf, G.buf], [WI.buf])
        for k in range(8):
            s = wst[rr["cast"] % 2]
            DMA(s.ap[:, 0:512], w_mem[l, k * 128:(k + 1) * 128, :], s.sem, [], [s.buf])
            eng = engs[rr["cast"] % 3]
            rr["cast"] += 1
            dst = WM.ap[:, k, :]
            sc = gcol(f"mng{l}", 1, k)
            if eng == "act":
                ACT(dst, s.ap[:, 0:512], AF.Copy, [s.buf, G.buf], [WM.buf], scale=sc)
            else:
                TS(eng, dst, s.ap[:, 0:512], sc, ALU.mult, [s.buf, G.buf], [WM.buf])

    def x_src(l, s, r0, n):
        t0, _ = segs[s]
        src = x_in[s] if l == 0 else y_out[s]
        g = (t0 + r0) // 512
        return src[r0:r0 + n, :], ([] if l == 0 else [b_Y[g]])

    def emit_group_loads(l, gi):
        tiles = []
        for i in range(4):
            t = xt[rr["xt"] % 4]
            rr["xt"] += 1
            if gi == NG:
                DMA(t.ap, mem[i * 128:(i + 1) * 128, :], t.sem, [], [t.buf])
            else:
                s = seg_of_group(gi)
                r0 = gi * 512 - segs[s][0] + i * 128
                ap, rb = x_src(l, s, r0, 128)
                DMA(t.ap, ap, t.sem, rb, [t.buf])
            tiles.append(t)
        tb = None
        if gi < NG:
            s = seg_of_group(gi)
            p0 = gi * 512 - segs[s][0]
            tb = tabs[rr["tab"] % 2]
            rr["tab"] += 1
            which = [0, 1] if l % 2 == 0 else [0, 1, 2, 3]
            for w in which:
                DMA(tb[w].ap, tabs_d[w, :, p0:p0 + 512], tb[w].sem, [], [tb[w].buf])
        return tiles, tb

    def front_steps(l, gi, tiles):
        XT = xT[rr["xT"] % 2]
        rr["xT"] += 1
        steps = []
        for i, t in enumerate(tiles):
            st = sst[(gi * 4 + i) % 4]
            b = xb[(gi * 4 + i) % 4]

            def s1(t=t, st=st, b=b):
                ACT(b.ap, t.ap, AF.Square, [t.buf], [b.buf, st.buf], accum=st.ap[:, 0:1])
                ACT(st.ap[:, 1:2], st.ap[:, 0:1], AF.Ln, [st.buf, DV.buf], [st.buf], scale=1.0 / D, bias=EPSB)
                ACT(st.ap[:, 2:3], st.ap[:, 1:2], AF.Exp, [st.buf], [st.buf], scale=-0.5)
                TS("dve", b.ap, t.ap, st.ap[:, 2:3], ALU.mult, [t.buf, st.buf], [b.buf])

            def s2(i=i, b=b):
                for c in range(8):
                    MM(pP_tp.ap[:, c * 128:(c + 1) * 128], b.ap[:, c * 128:(c + 1) * 128], ident, True, True,
                       [b.buf, b_c], [pP_tp.buf])
                CP("act" if i % 2 == 0 else "dve", XT.ap[:, :, i * 128:(i + 1) * 128],
                   pP_tp.ap.rearrange("p (c t) -> p c t", c=8), [pP_tp.buf], [XT.buf])
            steps.append(s1)
            steps.append(s2)
        return XT, steps

    def fm_store(l, gi, chunk, oc):
        if gi == NG:
            DMA(FMm[chunk, :, :], oc.ap, oc.sem, [oc.buf], [b_FMm])
        else:
            DMA(FM[chunk, :, gi * 512:(gi + 1) * 512], oc.ap, oc.sem, [oc.buf], [b_FM[gi]])

    def next_oc():
        o = oc_sb[rr["oc"] % 4]
        rr["oc"] += 1
        return o

    def proj_group(l, gi, XT, tb, W, jobs, tokjobs, hooks):
        n = len(jobs)
        state = [dict() for _ in range(n)]

        def main(j):
            jb = jobs[j]
            bank = pP_main[rr["main"] % 3]
            rr["main"] += 1
            state[j]["bank"] = bank
            c0 = jb["col"]
            for k in range(8):
                MM(bank.ap, W.ap[:, k, c0:c0 + 128], XT.ap[:, k, :], k == 0, k == 7, [W.buf, XT.buf], [bank.buf])

        def S1(j):
            jb = jobs[j]
            st = state[j]
            bank = st["bank"]
            kind = jb["kind"]
            if kind == "plain":
                oc = next_oc()
                CP("act", oc.ap, bank.ap, [bank.buf], [oc.buf])
                fm_store(l, gi, jb["chunk"], oc)
            elif kind == "silu":
                oc = next_oc()
                ACT(oc.ap, bank.ap, AF.Silu, [bank.buf], [oc.buf])
                fm_store(l, gi, jb["chunk"], oc)
            elif kind == "gc":
                u = u_sb[rr["c2"] % NSL]
                st["u"] = u
                CP("act", u.ap, bank.ap, [bank.buf], [u.buf])
            elif kind == "hc":
                u = state[j - 1]["u"]
                rr["c2"] += 1
                oc = next_oc()
                TT("dve", oc.ap, bank.ap, u.ap, ALU.mult, [bank.buf, u.buf], [oc.buf])
                fm_store(l, gi, jb["chunk"], oc)
            else:
                sl = rr["c2"] % NSL
                rr["c2"] += 1
                st["s"] = sl
                st["ms"] = pP_ms[rr["ms"] % 2]
                rr["ms"] += 1
                u, sq = u_sb[sl], sq_sb[sl]
                CP("act", u.ap, bank.ap, [bank.buf], [u.buf])
                TT("dve", sq.ap, u.ap, u.ap, ALU.mult, [u.buf], [sq.buf])
                MM(st["ms"].ap, bdm, sq.ap, True, True, [b_c, sq.buf], [st["ms"].buf])

        def S2(j):
            jb = jobs[j]
            st = state[j]
            if jb["kind"] not in ("hn", "rope1", "ropeA"):
                return
            rs = rs_sb[st["s"]]
            ACT(rs.ap, st["ms"].ap, AF.Ln, [st["ms"].buf, DV.buf], [rs.buf], bias=EPSB)
            ACT(rs.ap, rs.ap, AF.Exp, [rs.buf], [rs.buf], scale=-0.5)

        def S3(j):
            jb = jobs[j]
            st = state[j]
            if jb["kind"] not in ("hn", "rope1", "ropeA"):
                return
            sl = st["s"]
            u, rs, t = u_sb[sl], rs_sb[sl], t_sb[sl]
            if jb["kind"] == "hn":
                oc = next_oc()
                STT(oc.ap, u.ap, jb["gain"], rs.ap, ALU.mult, ALU.mult, [u.buf, rs.buf, G.buf], [oc.buf])
                fm_store(l, gi, jb["chunk"], oc)
            else:
                STT(t.ap, u.ap, jb["gain"], rs.ap, ALU.mult, ALU.mult, [u.buf, rs.buf, G.buf], [t.buf])
                rm = rot1 if jb["kind"] == "rope1" else rota
                st["rot"] = pP_rot[rr["rot"] % 2]
                rr["rot"] += 1
                MM(st["rot"].ap, rm, t.ap, True, True, [b_c, t.buf], [st["rot"].buf])

        def S4(j):
            jb = jobs[j]
            st = state[j]
            if jb["kind"] not in ("rope1", "ropeA"):
                return
            sl = st["s"]
            t, a, b = t_sb[sl], a_sb[sl], b_sb[sl]
            ct, sn = (tb[0], tb[1]) if jb["kind"] == "rope1" else (tb[2], tb[3])
            TT("pool", a.ap, t.ap, ct.ap, ALU.mult, [t.buf, ct.buf], [a.buf])
            TT("dve", b.ap, st["rot"].ap, sn.ap, ALU.mult, [st["rot"].buf, sn.buf], [b.buf])

        def S5(j):
            jb = jobs[j]
            st = state[j]
            if jb["kind"] not in ("rope1", "ropeA"):
                return
            sl = st["s"]
            a, b = a_sb[sl], b_sb[sl]
            oc = next_oc()
            TT("pool", oc.ap, a.ap, b.ap, ALU.add, [a.buf, b.buf], [oc.buf])
            fm_store(l, gi, jb["chunk"], oc)

        stages = [S1, S2, S3, S4, S5]
        nh = len(hooks)
        done_h = set()
        if n >= 20:
            hook_steps = [2, 11, 4, 13, 6, 15, 8, 17]
        else:
            hook_steps = [0, 1, 0, 1, 1, 2, 1, 2]
        for step in range(n + 5):
            if step < n:
                main(step)
            if nh:
                for hi_, hstep in enumerate(hook_steps):
                    if hstep == step and hi_ < nh:
                        hooks[hi_]()
                        done_h.add(hi_)
            for d in (5, 4, 3, 2, 1):
                if 0 <= step - d < n:
                    stages[d - 1](step - d)
        for hi_ in range(nh):
            if hi_ not in done_h:
                hooks[hi_]()

        for i in range(4):
            vs = vst[rr["vst"] % 2]
            rr["vst"] += 1
            nslots = 0
            for tj_i, tj in enumerate(tokjobs):
                bank = pP_main[rr["main"] % 3]
                rr["main"] += 1
                nco = tj["ncols"]
                for k in range(8):
                    MM(bank.ap[:, 0:nco], XT.ap[:, k, i * 128:(i + 1) * 128], W.ap[:, k, tj["col"]:tj["col"] + nco],
                       k == 0, k == 7, [W.buf, XT.buf], [bank.buf])
                nh = tj["nhead"]
                wd = nco // nh
                dst = vs.ap[:, tj["slot0"]:tj["slot0"] + nh, 0:wd]
                srcv = bank.ap[:, 0:nco].rearrange("p (h d) -> p h d", h=nh)
                CP("dve" if tj_i % 2 == 0 else "act", dst, srcv, [bank.buf], [vs.buf])
                nslots = max(nslots, tj["slot0"] + nh)
            if gi == NG:
                DMA(VMm[:, :, i, :].rearrange("c p d -> p c d"), vs.ap[:, 0:4, :], vs.sem, [vs.buf], [b_VMm])
            else:
                j = gi * 4 + i
                DMA(VM[0:nslots, :, j, :].rearrange("c p d -> p c d"), vs.ap[:, 0:nslots, :], vs.sem,
                    [vs.buf], [b_VM[gi]])

    def phase_P(l):
        even = (l % 2 == 0)
        load_cast_weights(l)
        for v in vst:
            MEMSET("pool", v.ap, 1.0, [v.buf])
        if even:
            e = l // 2
            jobs = []
            for c in range(4):
                jobs.append(dict(col=c * 128, kind="plain", chunk=c))
            for c in range(4):
                jobs.append(dict(col=512 + c * 128, kind="gc", chunk=None))
                jobs.append(dict(col=1024 + c * 128, kind="hc", chunk=4 + c))
            for c in range(4):
                jobs.append(dict(col=1536 + c * 128, kind="rope1", chunk=12 + c, gain=gcol(f"sqg{e}")))
            jobs.append(dict(col=2048, kind="rope1", chunk=16, gain=gcol(f"skg{e}")))
            for c in range(2):
                jobs.append(dict(col=2304 + c * 128, kind="hn", chunk=18 + c, gain=gcol(f"mqg{l}")))
            for c in range(10):
                jobs.append(dict(col=2560 + c * 128, kind="silu", chunk=20 + c))
            tokjobs = [dict(col=2176, ncols=128, slot0=0, nhead=2)]
        else:
            o = l // 2
            jobs = []
            for c in range(4):
                jobs.append(dict(col=c * 128, kind="ropeA", chunk=c, gain=gcol(f"aqg{o}")))
            jobs.append(dict(col=512, kind="ropeA", chunk=4, gain=gcol(f"akg{o}")))
            for c in range(4):
                jobs.append(dict(col=768 + c * 128, kind="rope1", chunk=6 + c, gain=gcol(f"dqg{o}")))
            for c in range(4):
                jobs.append(dict(col=1280 + c * 128, kind="rope1", chunk=10 + c, gain=gcol(f"dkg{o}")))
            for c in range(2):
                jobs.append(dict(col=2304 + c * 128, kind="hn", chunk=18 + c, gain=gcol(f"mqg{l}")))
            for c in range(10):
                jobs.append(dict(col=2560 + c * 128, kind="silu", chunk=20 + c))
            tokjobs = [dict(col=640, ncols=128, slot0=0, nhead=2), dict(col=1792, ncols=512, slot0=2, nhead=4)]
        memjobs = [dict(col=c * 128, kind="hn", chunk=c, gain=gcol(f"mkg{l}")) for c in range(2)]
        memtok = [dict(col=256, ncols=256, slot0=0, nhead=4)]

        order = [NG] + list(range(NG))
        tiles, tb = emit_group_loads(l, order[0])
        XT, steps = front_steps(l, order[0], tiles)
        for st_ in steps:
            st_()
        for idx, gi in enumerate(order):
            cur_XT, cur_tb = XT, tb
            hooks = []
            if idx + 1 < len(order):
                tiles, tb = emit_group_loads(l, order[idx + 1])
                XT, hooks = front_steps(l, order[idx + 1], tiles)
            if gi == NG:
                proj_group(l, gi, cur_XT, cur_tb, WM, memjobs, memtok, hooks)
                for v in vst:
                    MEMSET("pool", v.ap, 1.0, [v.buf])
            else:
                proj_group(l, gi, cur_XT, cur_tb, WI, jobs, tokjobs, hooks)

    aA = Arena("A")
    NJ = SMAX // 128
    KTa = aA.take("KTa", [128, SMAX], BF16)
    KTb = aA.take("KTb", [128, SMAX], BF16)
    VA = [aA.take(f"VA{i}", [128, NJ, 128], BF16) for i in range(2)]
    Qs = [aA.take(f"Qs{i}", [128, 2, 512], BF16) for i in range(2)]
    pt = [aA.take(f"pt{i}", [128, 1024], BF16) for i in range(4)]
    rz = [aA.take(f"rz{i}", [128, 512], F32) for i in range(2)]
    zs = [aA.take(f"zs{i}", [128, 512], F32) for i in range(2)]
    ys = [aA.take(f"ys{i}", [128, 512], BF16) for i in range(4)]
    ysw = [aA.take(f"ysw{i}", [64, 4, 512], BF16) for i in range(2)]
    d_t = [aA.take(f"dt{i}", [128, 512], F32) for i in range(2)]
    d_o = aA.take("do", [128, 512], F32)
    d_sq = aA.take("dsq", [128, 512], BF16)
    d_ln = aA.take("dln", [128, 512], F32)
    d_rs = aA.take("drs", [128, 512], F32)
    accD = [aA.take(f"accD{i}", [128, 1024], F32) for i in range(2)]
    accP = [aA.take(f"accP{i}", [128, 1024], F32) for i in range(2)]
    zsD = [aA.take(f"zsD{i}", [128, 512], F32) for i in range(2)]
    zsP = [aA.take(f"zsP{i}", [128, 512], F32) for i in range(2)]
    ESR = aA.take("ESR", [128, 4, 256], F32)

    pA_sc = [banks[0], banks[1], banks[2]]
    pA_O = [banks[3], banks[4]]
    pA_acc = [banks[5], banks[6]]
    pA_Zb = banks[7]
    pA_ms = banks[7]
    scw = pA_sc
    ra = {"sc": 0, "pt": 0, "O": 0, "ys": 0, "Qs": 0, "rz": 0, "ysw": 0, "acc": 0}

    def load_KT(src_a, src_b, n, rb):
        MEMSET("pool", KTa.ap[64:128, 0:n], 0.0, [KTa.buf])
        MEMSET("pool", KTb.ap[0:64, 0:n], 0.0, [KTb.buf])
        DMA(KTa.ap[0:64, 0:n], src_a, KTa.sem, rb, [KTa.buf])
        DMA(KTb.ap[64:128, 0:n], src_b, KTb.sem, rb, [KTb.buf])

    def dense_units(maps, nkt, qts, mode, l, recip):
        units = [(qi, mi, j) for qi in range(len(qts)) for mi in range(len(maps)) for j in range(nkt)]
        L = 2
        qtile = {}
        cur = {}
        pending = []

        def front(u):
            qi, mi, j = units[u]
            m = maps[mi]
            if mi == 0 and j == 0:
                q = Qs[ra["Qs"] % 2]
                ra["Qs"] += 1
                qts[qi]["load"](q)
                qtile[qi] = q
            q = qtile[qi]
            sc = pA_sc[ra["sc"] % 3]
            ra["sc"] += 1
            pp = pt[ra["pt"] % 4]
            ra["pt"] += 1
            kt = KTa if m["half"] == 0 else KTb
            MM(sc.ap, kt.ap[:, j * 128:(j + 1) * 128], q.ap[:, m["qc"], :], True, True, [kt.buf, q.buf], [sc.buf])
            ACT(pp.ap[:, 0:512], sc.ap, AF.Exp, [sc.buf], [pp.buf], scale=0.125)
            cur[u] = pp

        def back(u):
            qi, mi, j = units[u]
            m = maps[mi]
            pp = cur.pop(u)
            if j == 0:
                m["_O"] = pA_O[ra["O"] % 2]
                ra["O"] += 1
                if mode == "diff":
                    k = ra["acc"] % 2
                    ra["acc"] += 1
                    m["_acc"] = {"dve": pA_acc[k], "pool": accP[k]}
                    m["_zs"] = zsD[k]
                    m["_init"] = {"dve": False, "pool": False}
            O = m["_O"]
            va = VA[m["va"]]
            MM(O.ap, va.ap[:, j, :], pp.ap[:, 0:512], j == 0, j == nkt - 1, [va.buf, pp.buf], [O.buf])
            if mode == "diff":
                eng = "pool" if (j % 4) == 2 else "dve"
                acc = m["_acc"][eng]
                aap = acc.ap[:, 0:512]
                if not m["_init"][eng]:
                    CP(eng, aap, pp.ap[:, 0:512], [pp.buf], [acc.buf])
                    m["_init"][eng] = True
                else:
                    TT(eng, aap, aap, pp.ap[:, 0:512], ALU.add, [acc.buf, pp.buf], [acc.buf])
            if j == nkt - 1:
                if mode == "aug":
                    epi_aug(qi, mi)
                else:
                    epi_diff(qi, mi)

        def epi_aug(qi, mi):
            m = maps[mi]
            O = m["_O"]
            tok0 = qts[qi]["tok0"]
            g = tok0 // 512
            r = rz[ra["rz"] % 2]
            ra["rz"] += 1
            y = ys[ra["ys"] % 4]
            ra["ys"] += 1
            if recip == "dve":
                RECIP(r.ap[64:128, :], O.ap[64:128, :], [O.buf], [r.buf])
            else:
                ACT(r.ap[64:128, :], O.ap[64:128, :], AF.Ln, [O.buf], [r.buf])
                ACT(r.ap[64:128, :], r.ap[64:128, :], AF.Exp, [r.buf], [r.buf], scale=-1.0)
            TT("dve", y.ap[0:64, :], O.ap[0:64, :], r.ap[64:128, :], ALU.mult, [O.buf, r.buf], [y.buf])
            ch, hf = m["out"]
            DMA(YT[ch, hf * 64:(hf + 1) * 64, tok0:tok0 + 512], y.ap[0:64, :], y.sem, [y.buf], [b_YT[g]])

        def epi_diff(qi, mi):
            m = maps[mi]
            O = m["_O"]
            acc = m["_acc"]
            sD = m["_zs"]
            tok0 = qts[qi]["tok0"]
            g = tok0 // 512
            o = l // 2
            t = d_t[mi]
            r = rz[ra["rz"] % 2]
            ra["rz"] += 1
            has_pool = m["_init"]["pool"]
            CP("dve", sD.ap, acc["dve"].ap[:, 0:512], [acc["dve"].buf], [sD.buf])

            def st1():
                MM(pA_Zb.ap, ones_f, sD.ap, True, not has_pool, [cmat_f.buf, sD.buf], [pA_Zb.buf])
                if has_pool:
                    MM(pA_Zb.ap, ones_f, acc["pool"].ap[:, 0:512], False, True, [cmat_f.buf, acc["pool"].buf],
                       [pA_Zb.buf])
                ACT(r.ap, pA_Zb.ap, AF.Ln, [pA_Zb.buf], [r.buf])
                ACT(r.ap, r.ap, AF.Exp, [r.buf], [r.buf], scale=-1.0)
                TT("dve", t.ap, O.ap, r.ap, ALU.mult, [O.buf, r.buf], [t.buf])

            def st2():
                nl = DV.ap[:, dvc[f"nlam{o}"]:dvc[f"nlam{o}"] + 1]
                STT(d_o.ap, d_t[1].ap, nl, d_t[0].ap, ALU.mult, ALU.add, [d_t[0].buf, d_t[1].buf, DV.buf], [d_o.buf])
                ACT(d_sq.ap, d_o.ap, AF.Square, [d_o.buf], [d_sq.buf])

            def st3():
                gsv = DV.ap[:, dvc[f"gs{o}"]:dvc[f"gs{o}"] + 1]
                MM(pA_ms.ap, onesm, d_sq.ap, True, True, [b_c, d_sq.buf], [pA_ms.buf])
                ACT(d_ln.ap, pA_ms.ap, AF.Ln, [pA_ms.buf, DV.buf], [d_ln.buf], bias=EPSB)
                ACT(d_rs.ap, d_ln.ap, AF.Exp, [d_ln.buf], [d_rs.buf], scale=-0.5)
                y = ys[ra["ys"] % 4]
                ra["ys"] += 1
                STT(y.ap, d_o.ap, gsv, d_rs.ap, ALU.mult, ALU.mult, [d_o.buf, d_rs.buf, DV.buf], [y.buf])
                DMA(YT[m["out"], :, tok0:tok0 + 512], y.ap, y.sem, [y.buf], [b_YT[g]])

            pending.append([3, st1])
            if mi == 1:
                pending.append([6, st2])
                pending.append([9, st3])

        def tick(flush=False):
            keep = []
            for it in pending:
                it[0] -= 1
                if it[0] <= 0 or flush:
                    it[1]()
                else:
                    keep.append(it)
            pending[:] = keep

        n = len(units)
        for i in range(n + L):
            if i < n:
                front(i)
            if i - L >= 0:
                back(i - L)
            tick()
        while pending:
            tick(flush=True)

    def q_loader(chunks, tok0, rb):
        def load(q):
            for ci, ch in enumerate(chunks):
                DMA(q.ap[:, ci, :], FM[ch, :, tok0:tok0 + 512], q.sem, rb, [q.buf])
        return load

    def phase_A_mem(l):
        for s in range(2):
            t0, ln = segs[s]
            gl = groups_of_seg(s)
            for pr in range(2):
                load_KT(FMm[pr, 0:64, s * 256:(s + 1) * 256], FMm[pr, 64:128, s * 256:(s + 1) * 256], 256, [b_FMm])
                for hh in range(2):
                    DMA(VA[hh].ap[:, 0:2, :], VMm[2 * pr + hh, :, 2 * s:2 * s + 2, :], VA[hh].sem,
                        [b_VMm], [VA[hh].buf])
                maps = [dict(qc=0, half=0, va=0, out=(8 + pr, 0)), dict(qc=0, half=1, va=1, out=(8 + pr, 1))]
                qts = [dict(load=q_loader([18 + pr], g * 512, [b_FM[g]]), tok0=g * 512) for g in gl]
                dense_units(maps, 2, qts, "aug", l, "act")

    def phase_A_odd(l):
        for s in range(2):
            t0, ln = segs[s]
            gl = groups_of_seg(s)
            rbs = [b_FM[g] for g in gl]
            rvs = [b_VM[g] for g in gl]
            nkt = ln // 128
            j0 = t0 // 128
            for kv in range(2):
                srck = FM[4, kv * 64:(kv + 1) * 64, t0:t0 + ln]
                load_KT(srck, srck, ln, rbs)
                DMA(VA[0].ap[:, 0:nkt, :], VM[kv, :, j0:j0 + nkt, :], VA[0].sem, rvs, [VA[0].buf])
                maps = []
                for hh in range(4):
                    h = 4 * kv + hh
                    maps.append(dict(qc=hh // 2, half=hh % 2, va=0, out=(h // 2, h % 2)))
                qts = [dict(load=q_loader([2 * kv, 2 * kv + 1], g * 512, [b_FM[g]]), tok0=g * 512) for g in gl]
                dense_units(maps, nkt, qts, "aug", l, "dve")
            for h in range(4):
                load_KT(FM[10 + h, 0:64, t0:t0 + ln], FM[10 + h, 64:128, t0:t0 + ln], ln, rbs)
                DMA(VA[0].ap[:, 0:nkt, :], VM[2 + h, :, j0:j0 + nkt, :], VA[0].sem, rvs, [VA[0].buf])
                maps = [dict(qc=0, half=0, va=0, out=4 + h), dict(qc=0, half=1, va=0, out=4 + h)]
                qts = [dict(load=q_loader([6 + h], g * 512, [b_FM[g]]), tok0=g * 512) for g in gl]
                dense_units(maps, nkt, qts, "diff", l, "act")

    def phase_A_even(l):
        e = l // 2
        esc = dvc[f"esink{e}"]
        maskP = masks.ap[:, 0:256]
        maskN = masks.ap[:, 256:512]
        MEMSET("dve", ESR.ap, 0.0, [ESR.buf])
        for kv in range(2):
            for hf in range(2):
                for ci in range(2):
                    h = 4 * kv + 2 * ci + hf
                    dst = ESR.ap[:, kv * 2 + hf, ci * 128:(ci + 1) * 128]
                    TS("dve", dst, dst, DV.ap[:, esc + h:esc + h + 1], ALU.add, [ESR.buf, DV.buf], [ESR.buf])
        for s in range(2):
            t0, ln = segs[s]
            gl = groups_of_seg(s)
            rbs = [b_FM[g] for g in gl]
            rvs = [b_VM[g] for g in gl]
            nb = ln // 128
            j0 = t0 // 128
            for kv in range(2):
                srck = FM[16, kv * 64:(kv + 1) * 64, t0:t0 + ln]
                load_KT(srck, srck, ln, rbs)
                DMA(VA[0].ap[:, 0:nb, :], VM[kv, :, j0:j0 + nb, :], VA[0].sem, rvs, [VA[0].buf])
                units = []
                for g in gl:
                    for bi in range(4):
                        i = (g * 512 - t0) // 128 + bi
                        for hf in range(2):
                            kbs = [jj for jj in (i - 1, i, i + 1) if 0 <= jj < nb]
                            for jj in kbs:
                                units.append((g, bi, i, hf, jj, jj == kbs[0], jj == kbs[-1]))
                qtile = {}
                cur = {}
                ost = {}

                def front(u, kv=kv):
                    g, bi, i, hf, jj, first, last = units[u]
                    if bi == 0 and hf == 0 and first:
                        q = Qs[ra["Qs"] % 2]
                        ra["Qs"] += 1
                        for ci in range(2):
                            DMA(q.ap[:, ci, :], FM[12 + 2 * kv + ci, :, g * 512:(g + 1) * 512], q.sem,
                                [b_FM[g]], [q.buf])
                        qtile[g] = q
                    q = qtile[g]
                    sc = pA_sc[ra["sc"] % 3]
                    ra["sc"] += 1
                    p = pt[ra["pt"] % 4]
                    ra["pt"] += 1
                    kt = KTa if hf == 0 else KTb
                    scv = sc.ap[:, 0:256]
                    rhs = q.ap[:, :, bi * 128:(bi + 1) * 128]
                    MM(scv, kt.ap[:, jj * 128:(jj + 1) * 128], rhs, True, jj == i, [kt.buf, q.buf], [sc.buf])
                    if jj != i:
                        MM(scv, ident, maskP if jj < i else maskN, False, True, [b_c, masks.buf], [sc.buf])
                    ACT(p.ap[:, 0:256], scv, AF.Exp, [sc.buf], [p.buf], scale=0.125)
                    cur[u] = p

                def back(u, kv=kv):
                    g, bi, i, hf, jj, first, last = units[u]
                    p = cur.pop(u)
                    if first:
                        ost[(i, hf)] = pA_O[ra["O"] % 2]
                        ra["O"] += 1
                    O = ost[(i, hf)]
                    Ov = O.ap[:, 0:256]
                    MM(Ov, VA[0].ap[:, jj, :], p.ap[:, 0:256], first, last, [VA[0].buf, p.buf], [O.buf])
                    if last:
                        if bi == 0 and hf == 0:
                            ost["ysw"] = ysw[ra["ysw"] % 2]
                            ra["ysw"] += 1
                        yw = ost["ysw"]
                        z = zs[ra["rz"] % 2]
                        ra["rz"] += 1
                        TT("dve", z.ap[64:128, 0:256], O.ap[64:128, 0:256], ESR.ap[64:128, kv * 2 + hf, :], ALU.add,
                           [O.buf, ESR.buf], [z.buf])
                        ACT(z.ap[64:128, 0:256], z.ap[64:128, 0:256], AF.Ln, [z.buf], [z.buf])
                        ACT(z.ap[64:128, 0:256], z.ap[64:128, 0:256], AF.Exp, [z.buf], [z.buf], scale=-1.0)
                        TT("dve", yw.ap[:, 2 * hf:2 * hf + 2, bi * 128:(bi + 1) * 128],
                           O.ap[0:64, 0:256].rearrange("p (c q) -> p c q", c=2),
                           z.ap[64:128, 0:256].rearrange("p (c q) -> p c q", c=2), ALU.mult,
                           [O.buf, z.buf], [yw.buf])
                        if bi == 3 and hf == 1:
                            for hf2 in range(2):
                                for ci in range(2):
                                    h = 4 * kv + 2 * ci + hf2
                                    DMA(YT[4 + h // 2, (h % 2) * 64:(h % 2) * 64 + 64, g * 512:(g + 1) * 512],
                                        yw.ap[:, 2 * hf2 + ci, :], yw.sem, [yw.buf], [b_YT[g]])

                n = len(units)
                L = 2
                for ii in range(n + L):
                    if ii < n:
                        front(ii)
                    if ii - L >= 0:
                        back(ii - L)

    aC = Arena("C")
    WO = aC.take("WO", [128, 10, D], BF16)
    wso = [aC.take(f"wso{i}", [128, D], F32) for i in range(2)]
    Yb = [aC.take(f"Yb{i}", [128, 10, 512], BF16) for i in range(2)]
    SZ = [aC.take(f"SZ{i}", [128, 10, 512], BF16) for i in range(2)]
    YZ = [aC.take(f"YZ{i}", [128, 10, 512], BF16) for i in range(2)]
    INb = [aC.take(f"IN{i}", [128, 4, 520], BF16) for i in range(2)]
    GBb = [aC.take(f"GB{i}", [128, 4, 512], BF16) for i in range(2)]
    ctmp = [aC.take(f"ct{i}", [128, 512], F32) for i in range(2)]
    xc = [aC.take(f"xc{i}", [128, D], F32) for i in range(8)]
    xn = [aC.take(f"xn{i}", [128, D], F32) for i in range(3)]
    rc = {"w": 0, "xc": 0, "xn": 0, "bank": 0, "ct": 0}

    def phase_C(l):
        even = (l % 2 == 0)
        engs = ["act", "dve", "pool"]
        for c in range(10):
            s = wso[rc["w"] % 2]
            DMA(s.ap, w_out[l, c * 128:(c + 1) * 128, :], s.sem, [], [s.buf])
            CP(engs[rc["w"] % 3], WO.ap[:, c, :], s.ap, [s.buf], [WO.buf])
            rc["w"] += 1

        def loads(g):
            s = seg_of_group(g)
            t0, ln = segs[s]
            k = g % 2
            Y, Z = Yb[k], SZ[k]
            c0 = 4 if even else 0
            DMA(Y.ap[:, c0:10, :], YT[c0:10, :, g * 512:(g + 1) * 512].rearrange("c p t -> p c t"), Y.sem,
                [b_YT[g]], [Y.buf])
            DMA(Z.ap, FM[20:30, :, g * 512:(g + 1) * 512].rearrange("c p t -> p c t"), Z.sem, [b_FM[g]], [Z.buf])
            if even:
                I_, Gb = INb[k], GBb[k]
                lo = g * 512 - 1
                hi = g * 512 + 513
                a0 = 0
                if lo < t0:
                    MEMSET("pool", I_.ap[:, :, 0:1], 0.0, [I_.buf])
                    lo += 1
                    a0 = 1
                a1 = 514
                if hi > t0 + ln:
                    MEMSET("pool", I_.ap[:, :, 513:514], 0.0, [I_.buf])
                    hi -= 1
                    a1 = 513
                rb = [b_FM[g]]
                if g - 1 >= 0:
                    rb.append(b_FM[g - 1])
                if g + 1 < NG:
                    rb.append(b_FM[g + 1])
                DMA(I_.ap[:, :, a0:a1], FM[4:8, :, lo:hi].rearrange("c p t -> p c t"), I_.sem, rb, [I_.buf])
                DMA(Gb.ap, FM[0:4, :, g * 512:(g + 1) * 512].rearrange("c p t -> p c t"), Gb.sem, [b_FM[g]], [Gb.buf])
            xs = []
            for i in range(4):
                t = xc[rc["xc"] % 8]
                rc["xc"] += 1
                r0 = g * 512 - t0 + i * 128
                ap, rb = x_src(l, s, r0, 128)
                DMA(t.ap, ap, t.sem, rb, [t.buf])
                xs.append(t)
            return xs

        pend = loads(0)
        for g in range(NG):
            s = seg_of_group(g)
            t0, ln = segs[s]
            k = g % 2
            xs = pend
            Y, Z, yz = Yb[k], SZ[k], YZ[k]
            if even:
                e = l // 2
                I_, Gb = INb[k], GBb[k]
                cw = gcols[f"cw{e}"]
                for c in range(4):
                    ct = ctmp[rc["ct"] % 2]
                    rc["ct"] += 1
                    w0 = G.ap[:, cw + 0 * 4 + c:cw + 0 * 4 + c + 1]
                    w1 = G.ap[:, cw + 1 * 4 + c:cw + 1 * 4 + c + 1]
                    w2 = G.ap[:, cw + 2 * 4 + c:cw + 2 * 4 + c + 1]
                    TS("dve", ct.ap, I_.ap[:, c, 0:512], w0, ALU.mult, [I_.buf, G.buf], [ct.buf])
                    STT(ct.ap, I_.ap[:, c, 1:513], w1, ct.ap, ALU.mult, ALU.add, [I_.buf, G.buf, ct.buf], [ct.buf])
                    STT(ct.ap, I_.ap[:, c, 2:514], w2, ct.ap, ALU.mult, ALU.add, [I_.buf, G.buf, ct.buf], [ct.buf])
                    TT("dve", Y.ap[:, c, :], ct.ap, Gb.ap[:, c, :], ALU.mult, [ct.buf, Gb.buf], [Y.buf])
            for hf in range(2):
                TT("dve", yz.ap[:, 5 * hf:5 * hf + 5, :], Y.ap[:, 5 * hf:5 * hf + 5, :], Z.ap[:, 5 * hf:5 * hf + 5, :],
                   ALU.mult, [Y.buf, Z.buf], [yz.buf])
            if g + 1 < NG:
                pend = loads(g + 1)
            for i in range(4):
                xo = xn[rc["xn"] % 3]
                rc["xn"] += 1
                for nh in range(2):
                    bank = banks[rc["bank"] % 4]
                    rc["bank"] += 1
                    for c in range(10):
                        MM(bank.ap, yz.ap[:, c, i * 128:(i + 1) * 128], WO.ap[:, c, nh * 512:(nh + 1) * 512],
                           c == 0, c == 9, [yz.buf, WO.buf], [bank.buf])
                    TT("dve", xo.ap[:, nh * 512:(nh + 1) * 512], bank.ap, xs[i].ap[:, nh * 512:(nh + 1) * 512], ALU.add,
                       [bank.buf, xs[i].buf], [xo.buf])
                r0 = g * 512 - t0 + i * 128
                DMA(y_out[s][r0:r0 + 128, :], xo.ap, xo.sem, [xo.buf], [b_Y[g]])

    P.marks = []

    def mark(name):
        P.marks.append((name, dict(P.count)))

    for l in range(DEPTH):
        P.barrier()
        mark(f"L{l} P")
        phase_P(l)
        P.barrier()
        mark(f"L{l} Amem")
        phase_A_mem(l)
        mark(f"L{l} A")
        if l % 2 == 0:
            phase_A_even(l)
        else:
            phase_A_odd(l)
        P.barrier()
        mark(f"L{l} C")
        phase_C(l)
    mark("end")
    P.wait_all("sp", b_Y)
    P.barrier()

    with nc.Block() as block:
        P.replay(block)
    return nc, P


_CONST_CACHE = {}


def _consts(smax):
    if smax not in _CONST_CACHE:
        _CONST_CACHE[smax] = (_const_mats(), _masks(), _rope_tables(smax))
    return _CONST_CACHE[smax]


def run(x_prompt, x_sample, mem_prompt, mem_sample, norm_g, w_in, w_out, mem_norm_g, w_mem_kv, mem_qk_g,
        conv_w, swa_qk_g, swa_sink, ax_qk_g, diff_qk_g, diff_lambda, diff_subln_g, depth=None, n_cores=8):
    f = lambda a: np.ascontiguousarray(np.asarray(a), dtype=np.float32)
    x_prompt, x_sample, mem_prompt, mem_sample = f(x_prompt), f(x_sample), f(mem_prompt), f(mem_sample)
    B, SP, _ = x_prompt.shape
    SS = x_sample.shape[1]
    DEPTH = int(depth if depth is not None else np.asarray(norm_g).shape[0])
    nc, P = build_program(SP, SS, DEPTH)
    cm, mk, tabs = _consts(max(SP, SS))
    gp, _ = _gpack(DEPTH, f(norm_g), f(mem_norm_g), f(mem_qk_g), f(conv_w), f(swa_qk_g), f(swa_sink),
                   f(ax_qk_g), f(diff_qk_g), f(diff_lambda), f(diff_subln_g))
    w_in, w_out, w_mem = f(w_in)[:DEPTH], f(w_out)[:DEPTH], f(w_mem_kv)[:DEPTH]
    in_maps = []
    for c in range(n_cores):
        in_maps.append({
            "x_p": x_prompt[c], "x_s": x_sample[c],
            "mem": np.ascontiguousarray(np.concatenate([mem_prompt[c], mem_sample[c]], axis=0)),
            "w_in": w_in, "w_out": w_out, "w_mem": w_mem, "gpack": gp, "cmat": cm, "masks": mk, "tabs": tabs,
        })
    res = run_bass_kernel_spmd(nc, in_maps, core_ids=list(range(n_cores)))
    yp = np.stack([np.asarray(r["y_p"], dtype=np.float32) for r in res.results])
    ysm = np.stack([np.asarray(r["y_s"], dtype=np.float32) for r in res.results])
    return yp, ysm


def kernel(x_prompt, x_sample, mem_prompt, mem_sample, norm_g, w_in, w_out, mem_norm_g, w_mem_kv, mem_qk_g,
           conv_w, swa_qk_g, swa_sink, ax_qk_g, diff_qk_g, diff_lambda, diff_subln_g):
    return run(x_prompt, x_sample, mem_prompt, mem_sample, norm_g, w_in, w_out, mem_norm_g, w_mem_kv, mem_qk_g,
               conv_w, swa_qk_g, swa_sink, ax_qk_g, diff_qk_g, diff_lambda, diff_subln_g)
```

```python
import math
import numpy as np
import concourse.bass as bass
import concourse.mybir as mybir
from concourse.bass_utils import run_bass_kernel_spmd

F32 = mybir.dt.float32
BF16 = mybir.dt.bfloat16
U8 = mybir.dt.uint8
AF = mybir.ActivationFunctionType
ALU = mybir.AluOpType

D = 1024
INW = 3840
MIXW = 1280
NMEM = 256
EPS = 1e-6
NEGM = -30000.0
ENGS = ("pe", "act", "dve", "pool", "sp")


class Buf:
    __slots__ = ("name", "w", "r")

    def __init__(self, name):
        self.name = name
        self.w = {}
        self.r = {}


class DmaSem:
    __slots__ = ("handle", "count", "name")

    def __init__(self, handle, name):
        self.handle = handle
        self.count = 0
        self.name = name


class Plan:
    def __init__(self, nc):
        self.nc = nc
        self.items = {e: [] for e in ENGS}
        self.count = {e: 0 for e in ENGS}
        self.waited = {e: {} for e in ENGS}
        self.sems = {}
        for e in ENGS:
            if e != "sp":
                self.sems[e] = nc.alloc_semaphore(name=f"s_{e}")
        self.dmasems = []
        self.n_inst = 0

    def dma_sem(self, name):
        s = DmaSem(self.nc.alloc_semaphore(name=f"d_{name}"), name)
        self.dmasems.append(s)
        return s

    def _deps(self, eng, reads, writes):
        deps = {}
        for b in reads:
            for k, v in b.w.items():
                if deps.get(k, 0) < v:
                    deps[k] = v
        for b in writes:
            for k, v in b.w.items():
                if deps.get(k, 0) < v:
                    deps[k] = v
            for k, v in b.r.items():
                if deps.get(k, 0) < v:
                    deps[k] = v
        waits = []
        wd = self.waited[eng]
        for k, v in deps.items():
            if k == "pe" and eng == "pe":
                continue
            if wd.get(k, 0) >= v:
                continue
            wd[k] = v
            waits.append((k, v))
        return waits

    def _mark(self, key, val, reads, writes):
        for b in reads:
            if b.r.get(key, 0) < val:
                b.r[key] = val
        for b in writes:
            if b.r:
                b.w = {}
                b.r = {}
            b.w[key] = val

    def op(self, eng, fn, reads=(), writes=()):
        waits = self._deps(eng, reads, writes)
        self.count[eng] += 1
        self.items[eng].append((waits, fn, (eng, 1)))
        self._mark(eng, self.count[eng], reads, writes)
        self.n_inst += 1

    def dma(self, q, fn, sem, reads=(), writes=()):
        waits = self._deps(q, reads, writes)
        sem.count += 16
        self.items[q].append((waits, fn, (sem, 16)))
        self._mark(sem, sem.count, reads, writes)
        self.n_inst += 1

    def wait_all(self, eng, bufs):
        waits = self._deps(eng, bufs, ())
        self.items[eng].append((waits, None, None))

    def barrier(self):
        snap = [(e, self.count[e]) for e in ENGS if e != "sp"]
        snap += [(s, s.count) for s in self.dmasems]
        for e in ENGS:
            waits = []
            wd = self.waited[e]
            for k, v in snap:
                if v == 0 or k == e:
                    continue
                if wd.get(k, 0) >= v:
                    continue
                wd[k] = v
                waits.append((k, v))
            if waits:
                self.items[e].append((waits, None, None))

    def _semh(self, k):
        return k.handle if isinstance(k, DmaSem) else self.sems[k]

    def replay(self, block):
        plan = self

        def run(engine, name):
            for waits, fn, inc in plan.items[name]:
                for k, v in waits:
                    engine.wait_ge(plan._semh(k), v)
                if fn is None:
                    continue
                inst = fn(engine)
                inst.then_inc(plan._semh(inc[0]), inc[1])

        @block.tensor
        def _(e):
            run(e, "pe")

        @block.scalar
        def _(e):
            run(e, "act")

        @block.vector
        def _(e):
            run(e, "dve")

        @block.gpsimd
        def _(e):
            run(e, "pool")

        @block.sync
        def _(e):
            run(e, "sp")


class Tl:
    __slots__ = ("ap", "buf", "_sem", "plan", "name")

    def __init__(self, plan, name, ap):
        self.plan = plan
        self.name = name
        self.ap = ap
        self.buf = Buf(name)
        self._sem = None

    @property
    def sem(self):
        if self._sem is None:
            self._sem = self.plan.dma_sem(self.name)
        return self._sem


def _const_mats():
    ident = np.eye(128, dtype=np.float32)
    k = np.arange(128)
    bd = (k[:, None] // 64 == k[None, :] // 64).astype(np.float32) / 64.0
    onesm = np.full((128, 128), 1.0 / 128.0, np.float32)
    ones = np.ones((128, 128), np.float32)
    rot1 = np.zeros((128, 128), np.float32)
    rota = np.zeros((128, 128), np.float32)
    for m in range(128):
        if (m % 64) < 32:
            rot1[m + 32, m] = -1.0
        else:
            rot1[m - 32, m] = 1.0
        if (m % 32) < 16:
            rota[m + 16, m] = -1.0
        else:
            rota[m - 16, m] = 1.0
    return np.concatenate([ident, bd, onesm, ones, rot1, rota], axis=1)


def _masks():
    ki = np.arange(128)[:, None]
    qi = np.arange(128)[None, :]
    mp = np.where(qi <= ki, 0.0, NEGM).astype(np.float32)
    mn = np.where(ki <= qi, 0.0, NEGM).astype(np.float32)
    return np.concatenate([mp, mp, mn, mn], axis=1)


def _rope_tables(smax):
    theta = np.float32(10000.0)
    pos = np.arange(smax)
    f = np.arange(128) % 64
    inv1 = (theta ** (-(np.arange(0, 64, 2, dtype=np.float32)) / np.float32(64))).astype(np.float32)
    ang1 = (pos.astype(np.float32)[None, :] * inv1[f % 32][:, None]).astype(np.float32)
    inva = (theta ** (-(np.arange(0, 32, 2, dtype=np.float32)) / np.float32(32))).astype(np.float32)
    prow = (pos // 64).astype(np.float32)
    pcol = (pos % 64).astype(np.float32)
    fa = f % 32
    pa = np.where((f < 32)[:, None], prow[None, :], pcol[None, :]).astype(np.float32)
    anga = (pa * inva[fa % 16][:, None]).astype(np.float32)
    tabs = np.stack([np.cos(ang1.astype(np.float64)), np.sin(ang1.astype(np.float64)),
                     np.cos(anga.astype(np.float64)), np.sin(anga.astype(np.float64))]).astype(np.float32)
    return tabs


def _gpack(depth, norm_g, mem_norm_g, mem_qk_g, conv_w, swa_qk_g, swa_sink, ax_qk_g, diff_qk_g,
           diff_lambda, diff_subln_g):
    cols = {}
    parts = []
    pos = [0]

    def add(name, a):
        a = np.ascontiguousarray(a, dtype=np.float32)
        assert a.shape[0] == 128
        cols[name] = pos[0]
        parts.append(a)
        pos[0] += a.shape[1]

    dup = lambda v: np.tile(v, 2)[:, None]
    for l in range(depth):
        add(f"ng{l}", norm_g[l].reshape(8, 128).T)
        add(f"mng{l}", mem_norm_g[l].reshape(8, 128).T)
        add(f"mqg{l}", dup(mem_qk_g[l, 0]))
        add(f"mkg{l}", dup(mem_qk_g[l, 1]))
        if l % 2 == 0:
            e = l // 2
            add(f"sqg{e}", dup(swa_qk_g[e, 0]))
            add(f"skg{e}", dup(swa_qk_g[e, 1]))
            add(f"cw{e}", conv_w[e].reshape(3, 4, 128).transpose(2, 0, 1).reshape(128, 12))
            add(f"sink{e}", np.broadcast_to(swa_sink[e][None, :], (128, 8)))
        else:
            o = l // 2
            add(f"aqg{o}", dup(ax_qk_g[o, 0]))
            add(f"akg{o}", dup(ax_qk_g[o, 1]))
            add(f"dqg{o}", dup(diff_qk_g[o, 0]))
            add(f"dkg{o}", dup(diff_qk_g[o, 1]))
            add(f"sub{o}", diff_subln_g[o][:, None])
            add(f"lam{o}", np.broadcast_to(diff_lambda[o].reshape(1, 256), (128, 256)))
    return np.concatenate(parts, axis=1), cols


def _gpack_cols(depth):
    z = lambda *s: np.zeros(s, np.float32)
    _, cols = _gpack(depth, z(depth, D), z(depth, D), z(depth, 2, 64), z((depth + 1) // 2, 3, 512),
                     z((depth + 1) // 2, 2, 64), z((depth + 1) // 2, 8), z(max(depth // 2, 1), 2, 64),
                     z(max(depth // 2, 1), 2, 64), z(max(depth // 2, 1), 4, 64), z(max(depth // 2, 1), 128))
    g, _ = _gpack(depth, z(depth, D), z(depth, D), z(depth, 2, 64), z((depth + 1) // 2, 3, 512),
                  z((depth + 1) // 2, 2, 64), z((depth + 1) // 2, 8), z(max(depth // 2, 1), 2, 64),
                  z(max(depth // 2, 1), 2, 64), z(max(depth // 2, 1), 4, 64), z(max(depth // 2, 1), 128))
    return cols, g.shape[1]


def build_program(SP, SS, DEPTH, debug_layers=None):
    nc = bass.Bass("TRN2", target_bir_lowering=False)
    T = SP + SS
    NG = T // 512
    SMAX = max(SP, SS)
    segs = [(0, SP), (SP, SS)]
    gcols, NGC = _gpack_cols(DEPTH)

    x_p = nc.dram_tensor("x_p", [SP, D], F32, kind="ExternalInput").ap()
    x_s = nc.dram_tensor("x_s", [SS, D], F32, kind="ExternalInput").ap()
    mem = nc.dram_tensor("mem", [2 * NMEM, D], F32, kind="ExternalInput").ap()
    w_in = nc.dram_tensor("w_in", [DEPTH, D, INW], F32, kind="ExternalInput").ap()
    w_out = nc.dram_tensor("w_out", [DEPTH, MIXW, D], F32, kind="ExternalInput").ap()
    w_mem = nc.dram_tensor("w_mem", [DEPTH, D, 512], F32, kind="ExternalInput").ap()
    gpack_d = nc.dram_tensor("gpack", [128, NGC], F32, kind="ExternalInput").ap()
    cmat_d = nc.dram_tensor("cmat", [128, 768], F32, kind="ExternalInput").ap()
    masks_d = nc.dram_tensor("masks", [128, 512], F32, kind="ExternalInput").ap()
    tabs_d = nc.dram_tensor("tabs", [4, 128, SMAX], F32, kind="ExternalInput").ap()
    y_p = nc.dram_tensor("y_p", [SP, D], F32, kind="ExternalOutput").ap()
    y_s = nc.dram_tensor("y_s", [SS, D], F32, kind="ExternalOutput").ap()
    FM = nc.dram_tensor("FM", [30, 128, T], BF16, kind="Internal").ap()
    FMm = nc.dram_tensor("FMm", [2, 128, 512], BF16, kind="Internal").ap()
    VM = nc.dram_tensor("VM", [6, 128, T // 128, 128], BF16, kind="Internal").ap()
    VMm = nc.dram_tensor("VMm", [4, 128, 4, 128], BF16, kind="Internal").ap()
    YT = nc.dram_tensor("YT", [10, 128, T], BF16, kind="Internal").ap()

    x_in = [x_p, x_s]
    y_out = [y_p, y_s]

    P = Plan(nc)

    b_FM = [Buf(f"FM{g}") for g in range(NG)]
    b_VM = [Buf(f"VM{g}") for g in range(NG)]
    b_YT = [Buf(f"YT{g}") for g in range(NG)]
    b_Y = [Buf(f"Y{g}") for g in range(NG)]
    b_FMm = Buf("FMm")
    b_VMm = Buf("VMm")

    def seg_of_group(g):
        return 0 if g * 512 < SP else 1

    def groups_of_seg(s):
        t0, ln = segs[s]
        return list(range(t0 // 512, (t0 + ln) // 512))

    def sb(name, shape, dt):
        return Tl(P, name, nc.alloc_sbuf_tensor("sb_" + name, shape, dt).ap())

    G = sb("G", [128, NGC], F32)
    DV = sb("DV", [128, 64], F32)
    cmat_f = sb("cmat_f", [128, 768], F32)
    cmat = sb("cmat", [128, 768], BF16)
    masks_f = sb("masks_f", [128, 512], F32)
    masks = sb("masks", [128, 512], BF16)
    stat = sb("stat", [128, 64], F32)
    ident = cmat.ap[:, 0:128]
    bdm = cmat.ap[:, 128:256]
    onesm = cmat.ap[:, 256:384]
    ones = cmat.ap[:, 384:512]
    rot1 = cmat.ap[:, 512:640]
    rota = cmat.ap[:, 640:768]
    b_c = cmat.buf

    ARENA_BYTES = 172 * 1024
    arena = nc.alloc_sbuf_tensor("arena", [128, ARENA_BYTES], U8).ap()

    class Arena:
        def __init__(self, tag):
            self.off = 0
            self.tag = tag
            self.n = 0

        def take(self, name, shape, dt):
            esz = 4 if dt == F32 else 2
            n = int(np.prod(shape[1:])) * esz
            self.off = (self.off + 63) // 64 * 64
            assert self.off + n <= ARENA_BYTES, (self.tag, name, self.off, n)
            a = arena[0:shape[0], self.off:self.off + n].bitcast(dt)
            self.off += n
            if len(shape) == 3:
                a = a.rearrange("p (a b) -> p a b", a=shape[1])
            elif len(shape) == 4:
                a = a.rearrange("p (a b c) -> p a b c", a=shape[1], b=shape[2])
            return Tl(P, f"{self.tag}_{name}", a)

    psall = nc.alloc_psum_tensor("psall", [128, 4096], F32).ap()
    banks = [Tl(P, f"ps{i}", psall[:, i * 512:(i + 1) * 512]) for i in range(8)]
    ones_f = cmat_f.ap[:, 384:512]

    def MM(out, lhsT, rhs, start, stop, R, W):
        P.op("pe", lambda e: e.matmul(out, lhsT=lhsT, rhs=rhs, start=start, stop=stop), R, W)

    def TRN(out, in_, R, W):
        P.op("pe", lambda e: e.transpose(out, in_, ident), list(R) + [b_c], W)

    def ACT(out, in_, func, R, W, scale=None, bias=None, accum=None):
        kw = {}
        if scale is not None:
            kw["scale"] = scale
        if bias is not None:
            kw["bias"] = bias
        if accum is not None:
            kw["accum_out"] = accum
        P.op("act", lambda e: e.activation(out=out, in_=in_, func=func, **kw), R, W)

    def TT(eng, out, in0, in1, op, R, W):
        P.op(eng, lambda e: e.tensor_tensor(out=out, in0=in0, in1=in1, op=op), R, W)

    def TS(eng, out, in0, s1, op0, R, W, s2=None, op1=None):
        if op1 is None:
            P.op(eng, lambda e: e.tensor_scalar(out=out, in0=in0, scalar1=s1, scalar2=None, op0=op0), R, W)
        else:
            P.op(eng, lambda e: e.tensor_scalar(out=out, in0=in0, scalar1=s1, scalar2=s2, op0=op0, op1=op1), R, W)

    def STT(out, in0, scalar, in1, op0, op1, R, W):
        P.op("dve", lambda e: e.scalar_tensor_tensor(out=out, in0=in0, scalar=scalar, in1=in1, op0=op0, op1=op1), R, W)

    def CP(eng, out, in_, R, W):
        if eng == "act":
            P.op("act", lambda e: e.activation(out=out, in_=in_, func=AF.Copy), R, W)
        else:
            P.op(eng, lambda e: e.tensor_copy(out=out, in_=in_), R, W)

    def RECIP(out, in_, R, W):
        P.op("dve", lambda e: e.reciprocal(out=out, in_=in_), R, W)

    def REDUCE(out, in_, R, W):
        P.op("dve", lambda e: e.tensor_reduce(out=out, in_=in_, axis=mybir.AxisListType.X, op=ALU.add), R, W)

    def MEMSET(eng, ap, val, W):
        P.op(eng, lambda e: e.memset(ap, val), (), W)

    def DMA(out, in_, sem, R, W, q="sp"):
        P.dma(q, lambda e: e.dma_start(out=out, in_=in_), sem, R, W)

    DMA(G.ap, gpack_d, G.sem, [], [G.buf])
    DMA(cmat_f.ap, cmat_d, cmat_f.sem, [], [cmat_f.buf])
    DMA(masks_f.ap, masks_d, masks_f.sem, [], [masks_f.buf])
    CP("dve", cmat.ap, cmat_f.ap, [cmat_f.buf], [cmat.buf])
    CP("dve", masks.ap, masks_f.ap, [masks_f.buf], [masks.buf])
    MEMSET("dve", DV.ap[:, 0:1], EPS, [DV.buf])
    EPSB = DV.ap[:, 0:1]
    dvc = {}
    dvpos = [1]

    def dv_take(name, n=1):
        dvc[name] = dvpos[0]
        dvpos[0] += n
        return DV.ap[:, dvc[name]:dvc[name] + n]

    for l in range(DEPTH):
        if l % 2 == 0:
            e = l // 2
            es = dv_take(f"esink{e}", 8)
            ACT(es, G.ap[:, gcols[f"sink{e}"]:gcols[f"sink{e}"] + 8], AF.Exp, [G.buf], [DV.buf])
        else:
            o = l // 2
            lam_init = 0.8 - 0.6 * math.exp(-0.3 * l)
            lc = gcols[f"lam{o}"]
            tmpv = dv_take(f"lamtmp{o}", 4)
            prod = sb(f"lamprod{o}", [128, 128], F32)
            TT("dve", prod.ap[:, 0:64], G.ap[:, lc:lc + 64], G.ap[:, lc + 64:lc + 128], ALU.mult, [G.buf], [prod.buf])
            TT("dve", prod.ap[:, 64:128], G.ap[:, lc + 128:lc + 192], G.ap[:, lc + 192:lc + 256], ALU.mult,
               [G.buf], [prod.buf])
            REDUCE(tmpv[:, 0:2], prod.ap.rearrange("p (a b) -> p a b", a=2), [prod.buf], [DV.buf])
            ACT(tmpv[:, 2:4], tmpv[:, 0:2], AF.Exp, [DV.buf], [DV.buf])
            nl = dv_take(f"nlam{o}", 1)
            TS("dve", nl, tmpv[:, 3:4], -lam_init, ALU.add, [DV.buf], [DV.buf])
            TT("dve", nl, nl, tmpv[:, 2:3], ALU.subtract, [DV.buf], [DV.buf])
            gsv = dv_take(f"gs{o}", 1)
            TS("dve", gsv, G.ap[:, gcols[f"sub{o}"]:gcols[f"sub{o}"] + 1], 1.0 - lam_init, ALU.mult,
               [G.buf], [DV.buf])

    def gcol(name, n=1, off=0):
        c = gcols[name] + off
        return G.ap[:, c:c + n]

    aP = Arena("P")
    WI = aP.take("WI", [128, 8, INW], BF16)
    WM = aP.take("WM", [128, 8, 512], BF16)
    wst = [aP.take(f"wst{i}", [128, 960], F32) for i in range(2)]
    xt = [aP.take(f"xt{i}", [128, D], F32) for i in range(4)]
    xb = [aP.take(f"xb{i}", [128, D], BF16) for i in range(4)]
    xT = [aP.take(f"xT{i}", [128, 8, 512], BF16) for i in range(2)]
    tabs = [[aP.take(f"tab{s}_{i}", [128, 512], F32) for i in range(4)] for s in range(2)]
    u_sb = [aP.take(f"u{i}", [128, 512], F32) for i in range(4)]
    sq_sb = [aP.take(f"sq{i}", [128, 512], BF16) for i in range(2)]
    rs_sb = [aP.take(f"rs{i}", [128, 512], F32) for i in range(2)]
    t_sb = [aP.take(f"t{i}", [128, 512], BF16) for i in range(2)]
    a_sb = [aP.take(f"a{i}", [128, 512], F32) for i in range(3)]
    b_sb = [aP.take(f"b{i}", [128, 512], F32) for i in range(2)]
    oc_sb = [aP.take(f"oc{i}", [128, 512], BF16) for i in range(4)]
    vst = [aP.take(f"vst{i}", [128, 6, 128], BF16) for i in range(2)]
    sst = [aP.take(f"sst{i}", [128, 4], F32) for i in range(4)]

    pP_main = [banks[0], banks[1], banks[2]]
    pP_ms = [banks[3], banks[4]]
    pP_rot = [banks[5], banks[6]]
    pP_tp = banks[7]
    tp_bf = pP_tp.ap.bitcast(BF16)

    rr = {"cast": 0, "oc": 0, "xt": 0, "xb": 0, "main": 0, "c2": 0, "vst": 0, "ms": 0, "rot": 0, "xT": 0, "tab": 0, "u": 0, "sq": 0, "rs": 0, "t": 0, "a": 0, "b": 0}

    def load_cast_weights(l):
        engs = ["act", "dve", "pool"]
        for k in range(8):
            for hf in range(4):
                s = wst[rr["cast"] % 2]
                DMA(s.ap, w_in[l, k * 128:(k + 1) * 128, hf * 960:(hf + 1) * 960], s.sem, [], [s.buf])
                eng = engs[rr["cast"] % 3]
                rr["cast"] += 1
                dst = WI.ap[:, k, hf * 960:(hf + 1) * 960]
                sc = gcol(f"ng{l}", 1, k)
                if eng == "act":
                    ACT(dst, s.ap, AF.Copy, [s.buf, G.buf], [WI.buf], scale=sc)
                else:
                    TS(eng, dst, s.ap, sc, ALU.mult, [s.buf, G.buf], [WI.buf])
        for k in range(8):
            s = wst[rr["cast"] % 2]
            DMA(s.ap[:, 0:512], w_mem[l, k * 128:(k + 1) * 128, :], s.sem, [], [s.buf])
            eng = engs[rr["cast"] % 3]
            rr["cast"] += 1
            dst = WM.ap[:, k, :]
            sc = gcol(f"mng{l}", 1, k)
            if eng == "act":
                ACT(dst, s.ap[:, 0:512], AF.Copy, [s.buf, G.buf], [WM.buf], scale=sc)
            else:
                TS(eng, dst, s.ap[:, 0:512], sc, ALU.mult, [s.buf, G.buf], [WM.buf])

    def x_src(l, s, r0, n):
        t0, _ = segs[s]
        src = x_in[s] if l == 0 else y_out[s]
        g = (t0 + r0) // 512
        return src[r0:r0 + n, :], ([] if l == 0 else [b_Y[g]])

    def emit_group_loads(l, gi):
        tiles = []
        for i in range(4):
            t = xt[rr["xt"] % 4]
            rr["xt"] += 1
            if gi == NG:
                DMA(t.ap, mem[i * 128:(i + 1) * 128, :], t.sem, [], [t.buf])
            else:
                s = seg_of_group(gi)
                r0 = gi * 512 - segs[s][0] + i * 128
                ap, rb = x_src(l, s, r0, 128)
                DMA(t.ap, ap, t.sem, rb, [t.buf])
            tiles.append(t)
        tb = None
        if gi < NG:
            s = seg_of_group(gi)
            p0 = gi * 512 - segs[s][0]
            tb = tabs[rr["tab"] % 2]
            rr["tab"] += 1
            which = [0, 1] if l % 2 == 0 else [0, 1, 2, 3]
            for w in which:
                DMA(tb[w].ap, tabs_d[w, :, p0:p0 + 512], tb[w].sem, [], [tb[w].buf])
        return tiles, tb

    def front_steps(l, gi, tiles):
        XT = xT[rr["xT"] % 2]
        rr["xT"] += 1
        steps = []
        for i, t in enumerate(tiles):
            st = sst[(gi * 4 + i) % 4]
            b = xb[(gi * 4 + i) % 4]

            def s1(t=t, st=st, b=b):
                ACT(b.ap, t.ap, AF.Square, [t.buf], [b.buf, st.buf], accum=st.ap[:, 0:1])
                ACT(st.ap[:, 1:2], st.ap[:, 0:1], AF.Ln, [st.buf, DV.buf], [st.buf], scale=1.0 / D, bias=EPSB)
                ACT(st.ap[:, 2:3], st.ap[:, 1:2], AF.Exp, [st.buf], [st.buf], scale=-0.5)
                TS("dve", b.ap, t.ap, st.ap[:, 2:3], ALU.mult, [t.buf, st.buf], [b.buf])

            def s2(i=i, b=b):
                for c in range(8):
                    TRN(tp_bf[:, c * 128:(c + 1) * 128], b.ap[:, c * 128:(c + 1) * 128], [b.buf], [pP_tp.buf])
                CP("act" if i % 2 == 0 else "dve", XT.ap[:, :, i * 128:(i + 1) * 128],
                   tp_bf.rearrange("p (c t) -> p c t", c=8), [pP_tp.buf], [XT.buf])
            steps.append(s1)
            steps.append(s2)
        return XT, steps

    def fm_store(l, gi, chunk, oc):
        if gi == NG:
            DMA(FMm[chunk, :, :], oc.ap, oc.sem, [oc.buf], [b_FMm])
        else:
            DMA(FM[chunk, :, gi * 512:(gi + 1) * 512], oc.ap, oc.sem, [oc.buf], [b_FM[gi]])

    def next_oc():
        o = oc_sb[rr["oc"] % 4]
        rr["oc"] += 1
        return o

    def proj_group(l, gi, XT, tb, W, jobs, tokjobs, hooks):
        n = len(jobs)
        state = [dict() for _ in range(n)]

        def main(j):
            jb = jobs[j]
            bank = pP_main[rr["main"] % 3]
            rr["main"] += 1
            state[j]["bank"] = bank
            c0 = jb["col"]
            for k in range(8):
                MM(bank.ap, W.ap[:, k, c0:c0 + 128], XT.ap[:, k, :], k == 0, k == 7, [W.buf, XT.buf], [bank.buf])

        NORM = ("hn", "rope1", "ropeA")
        ROPE = ("rope1", "ropeA")

        def T1(j):
            jb = jobs[j]
            st = state[j]
            bank = st["bank"]
            kind = jb["kind"]
            if kind == "plain":
                oc = next_oc()
                CP("act", oc.ap, bank.ap, [bank.buf], [oc.buf])
                fm_store(l, gi, jb["chunk"], oc)
            elif kind == "silu":
                oc = next_oc()
                ACT(oc.ap, bank.ap, AF.Silu, [bank.buf], [oc.buf])
                fm_store(l, gi, jb["chunk"], oc)
            elif kind == "gc":
                u = u_sb[rr["u"] % 4]
                rr["u"] += 1
                st["u"] = u
                CP("act", u.ap, bank.ap, [bank.buf], [u.buf])
            elif kind == "hc":
                u = state[j - 1]["u"]
                oc = next_oc()
                TT("dve", oc.ap, bank.ap, u.ap, ALU.mult, [bank.buf, u.buf], [oc.buf])
                fm_store(l, gi, jb["chunk"], oc)
            else:
                sq = sq_sb[rr["sq"] % 2]
                rr["sq"] += 1
                st["sq"] = sq
                ACT(sq.ap, bank.ap, AF.Square, [bank.buf], [sq.buf])

        def T2(j):
            jb, st = jobs[j], state[j]
            if jb["kind"] not in NORM:
                return
            bank = st["bank"]
            u = u_sb[rr["u"] % 4]
            rr["u"] += 1
            st["u"] = u
            CP("dve", u.ap, bank.ap, [bank.buf, st["sq"].buf], [u.buf])
            st["ms"] = pP_ms[rr["ms"] % 2]
            rr["ms"] += 1
            MM(st["ms"].ap, bdm, st["sq"].ap, True, True, [b_c, st["sq"].buf], [st["ms"].buf])

        def T3(j):
            jb, st = jobs[j], state[j]
            if jb["kind"] not in NORM:
                return
            rs = rs_sb[rr["rs"] % 2]
            rr["rs"] += 1
            st["rs"] = rs
            ACT(rs.ap, st["ms"].ap, AF.Ln, [st["ms"].buf, DV.buf], [rs.buf], bias=EPSB)
            ACT(rs.ap, rs.ap, AF.Exp, [rs.buf], [rs.buf], scale=-0.5)

        def T4(j):
            jb, st = jobs[j], state[j]
            if jb["kind"] not in NORM:
                return
            u, rs = st["u"], st["rs"]
            if jb["kind"] == "hn":
                oc = next_oc()
                STT(oc.ap, u.ap, jb["gain"], rs.ap, ALU.mult, ALU.mult, [u.buf, rs.buf, G.buf], [oc.buf])
                fm_store(l, gi, jb["chunk"], oc)
            else:
                t = t_sb[rr["t"] % 2]
                rr["t"] += 1
                st["t"] = t
                STT(t.ap, u.ap, jb["gain"], rs.ap, ALU.mult, ALU.mult, [u.buf, rs.buf, G.buf], [t.buf])

        def T5(j):
            jb, st = jobs[j], state[j]
            if jb["kind"] not in ROPE:
                return
            t = st["t"]
            rm = rot1 if jb["kind"] == "rope1" else rota
            st["rot"] = pP_rot[rr["rot"] % 2]
            rr["rot"] += 1
            MM(st["rot"].ap, rm, t.ap, True, True, [b_c, t.buf], [st["rot"].buf])
            a = a_sb[rr["a"] % 3]
            rr["a"] += 1
            st["a"] = a
            ct = tb[0] if jb["kind"] == "rope1" else tb[2]
            TT("pool", a.ap, t.ap, ct.ap, ALU.mult, [t.buf, ct.buf], [a.buf])

        def T6(j):
            jb, st = jobs[j], state[j]
            if jb["kind"] not in ROPE:
                return
            b = b_sb[rr["b"] % 2]
            rr["b"] += 1
            st["b"] = b
            sn = tb[1] if jb["kind"] == "rope1" else tb[3]
            TT("dve", b.ap, st["rot"].ap, sn.ap, ALU.mult, [st["rot"].buf, sn.buf], [b.buf])

        def T7(j):
            jb, st = jobs[j], state[j]
            if jb["kind"] not in ROPE:
                return
            oc = next_oc()
            TT("pool", oc.ap, st["a"].ap, st["b"].ap, ALU.add, [st["a"].buf, st["b"].buf], [oc.buf])
            fm_store(l, gi, jb["chunk"], oc)

        stages = [T1, T2, T3, T4, T5, T6, T7]
        NST = len(stages)
        nh = len(hooks)
        done_h = set()
        if n >= 20:
            hook_steps = [2, 11, 4, 13, 6, 15, 8, 17]
        else:
            hook_steps = [0, 1, 0, 1, 1, 2, 1, 2]
        for step in range(n + NST):
            if step < n:
                main(step)
            if nh:
                for hi_, hstep in enumerate(hook_steps):
                    if hstep == step and hi_ < nh:
                        hooks[hi_]()
                        done_h.add(hi_)
            for d in range(NST, 0, -1):
                if 0 <= step - d < n:
                    stages[d - 1](step - d)
        for hi_ in range(nh):
            if hi_ not in done_h:
                hooks[hi_]()

        for i in range(4):
            vs = vst[rr["vst"] % 2]
            rr["vst"] += 1
            nslots = 0
            for tj_i, tj in enumerate(tokjobs):
                bank = pP_main[rr["main"] % 3]
                rr["main"] += 1
                nco = tj["ncols"]
                for k in range(8):
                    MM(bank.ap[:, 0:nco], XT.ap[:, k, i * 128:(i + 1) * 128], W.ap[:, k, tj["col"]:tj["col"] + nco],
                       k == 0, k == 7, [W.buf, XT.buf], [bank.buf])
                nh = tj["nhead"]
                wd = nco // nh
                dst = vs.ap[:, tj["slot0"]:tj["slot0"] + nh, 0:wd]
                srcv = bank.ap[:, 0:nco].rearrange("p (h d) -> p h d", h=nh)
                CP("dve" if tj_i % 2 == 0 else "act", dst, srcv, [bank.buf], [vs.buf])
                nslots = max(nslots, tj["slot0"] + nh)
            if gi == NG:
                DMA(VMm[:, :, i, :].rearrange("c p d -> p c d"), vs.ap[:, 0:4, :], vs.sem, [vs.buf], [b_VMm])
            else:
                j = gi * 4 + i
                DMA(VM[0:nslots, :, j, :].rearrange("c p d -> p c d"), vs.ap[:, 0:nslots, :], vs.sem,
                    [vs.buf], [b_VM[gi]])

    def phase_P(l):
        even = (l % 2 == 0)
        load_cast_weights(l)
        for v in vst:
            MEMSET("pool", v.ap, 1.0, [v.buf])
        if even:
            e = l // 2
            jobs = []
            for c in range(4):
                jobs.append(dict(col=c * 128, kind="plain", chunk=c))
            for c in range(4):
                jobs.append(dict(col=512 + c * 128, kind="gc", chunk=None))
                jobs.append(dict(col=1024 + c * 128, kind="hc", chunk=4 + c))
            for c in range(4):
                jobs.append(dict(col=1536 + c * 128, kind="rope1", chunk=12 + c, gain=gcol(f"sqg{e}")))
            jobs.append(dict(col=2048, kind="rope1", chunk=16, gain=gcol(f"skg{e}")))
            for c in range(2):
                jobs.append(dict(col=2304 + c * 128, kind="hn", chunk=18 + c, gain=gcol(f"mqg{l}")))
            for c in range(10):
                jobs.append(dict(col=2560 + c * 128, kind="silu", chunk=20 + c))
            tokjobs = [dict(col=2176, ncols=128, slot0=0, nhead=2)]
        else:
            o = l // 2
            jobs = []
            for c in range(4):
                jobs.append(dict(col=c * 128, kind="ropeA", chunk=c, gain=gcol(f"aqg{o}")))
            jobs.append(dict(col=512, kind="ropeA", chunk=4, gain=gcol(f"akg{o}")))
            for c in range(4):
                jobs.append(dict(col=768 + c * 128, kind="rope1", chunk=6 + c, gain=gcol(f"dqg{o}")))
            for c in range(4):
                jobs.append(dict(col=1280 + c * 128, kind="rope1", chunk=10 + c, gain=gcol(f"dkg{o}")))
            for c in range(2):
                jobs.append(dict(col=2304 + c * 128, kind="hn", chunk=18 + c, gain=gcol(f"mqg{l}")))
            for c in range(10):
                jobs.append(dict(col=2560 + c * 128, kind="silu", chunk=20 + c))
            tokjobs = [dict(col=640, ncols=128, slot0=0, nhead=2), dict(col=1792, ncols=512, slot0=2, nhead=4)]
        memjobs = [dict(col=c * 128, kind="hn", chunk=c, gain=gcol(f"mkg{l}")) for c in range(2)]
        memtok = [dict(col=256, ncols=256, slot0=0, nhead=4)]

        order = [NG] + list(range(NG))
        tiles, tb = emit_group_loads(l, order[0])
        XT, steps = front_steps(l, order[0], tiles)
        for st_ in steps:
            st_()
        for idx, gi in enumerate(order):
            cur_XT, cur_tb = XT, tb
            hooks = []
            if idx + 1 < len(order):
                tiles, tb = emit_group_loads(l, order[idx + 1])
                XT, hooks = front_steps(l, order[idx + 1], tiles)
            if gi == NG:
                proj_group(l, gi, cur_XT, cur_tb, WM, memjobs, memtok, hooks)
                for v in vst:
                    MEMSET("pool", v.ap, 1.0, [v.buf])
            else:
                proj_group(l, gi, cur_XT, cur_tb, WI, jobs, tokjobs, hooks)

    aA = Arena("A")
    NJ = SMAX // 128
    KTa = aA.take("KTa", [128, SMAX], BF16)
    KTb = aA.take("KTb", [128, SMAX], BF16)
    VA = [aA.take(f"VA{i}", [128, NJ, 128], BF16) for i in range(2)]
    Qs = [aA.take(f"Qs{i}", [128, 2, 512], BF16) for i in range(2)]
    pt = [aA.take(f"pt{i}", [128, 512], BF16) for i in range(8)]
    rz = [aA.take(f"rz{i}", [128, 512], F32) for i in range(2)]
    zs = [aA.take(f"zs{i}", [128, 512], F32) for i in range(2)]
    ys = [aA.take(f"ys{i}", [128, 512], BF16) for i in range(4)]
    ysw = [aA.take(f"ysw{i}", [64, 4, 512], BF16) for i in range(2)]
    d_t = [aA.take(f"dt{i}", [128, 512], F32) for i in range(2)]
    d_o = aA.take("do", [128, 512], F32)
    d_sq = aA.take("dsq", [128, 512], BF16)
    d_ln = aA.take("dln", [128, 512], F32)
    d_rs = aA.take("drs", [128, 512], F32)
    accD = [aA.take(f"accD{i}", [128, 1024], F32) for i in range(2)]
    accP = [aA.take(f"accP{i}", [128, 1024], F32) for i in range(2)]
    zsD = [aA.take(f"zsD{i}", [128, 512], F32) for i in range(2)]
    zsP = [aA.take(f"zsP{i}", [128, 512], F32) for i in range(2)]
    ESR = aA.take("ESR", [128, 4, 256], F32)

    pA_sc = [banks[0], banks[1], banks[2]]
    pA_O = [banks[3], banks[4]]
    pA_acc = [banks[5], banks[5]]
    pA_Zb = banks[6]
    pA_ms = banks[7]
    scw = pA_sc
    ra = {"sc": 0, "pt": 0, "O": 0, "ys": 0, "Qs": 0, "rz": 0, "ysw": 0, "acc": 0}

    def load_KT(src_a, src_b, n, rb):
        MEMSET("pool", KTa.ap[64:128, 0:n], 0.0, [KTa.buf])
        MEMSET("pool", KTb.ap[0:64, 0:n], 0.0, [KTb.buf])
        DMA(KTa.ap[0:64, 0:n], src_a, KTa.sem, rb, [KTa.buf])
        DMA(KTb.ap[64:128, 0:n], src_b, KTb.sem, rb, [KTb.buf])

    def dense_units(maps, nkt, qts, mode, l, recip):
        units = [(qi, mi, j) for qi in range(len(qts)) for mi in range(len(maps)) for j in range(nkt)]
        L = 2
        qtile = {}
        cur = {}
        pending = []

        def front(u):
            qi, mi, j = units[u]
            m = maps[mi]
            if mi == 0 and j == 0:
                q = Qs[ra["Qs"] % 2]
                ra["Qs"] += 1
                qts[qi]["load"](q)
                qtile[qi] = q
            q = qtile[qi]
            sc = pA_sc[ra["sc"] % 3]
            ra["sc"] += 1
            pp = pt[ra["pt"] % 8]
            ra["pt"] += 1
            kt = KTa if m["half"] == 0 else KTb
            MM(sc.ap, kt.ap[:, j * 128:(j + 1) * 128], q.ap[:, m["qc"], :], True, True, [kt.buf, q.buf], [sc.buf])
            ACT(pp.ap[:, 0:512], sc.ap, AF.Exp, [sc.buf], [pp.buf], scale=0.125)
            cur[u] = pp

        def back(u):
            qi, mi, j = units[u]
            m = maps[mi]
            pp = cur.pop(u)
            if j == 0:
                m["_O"] = pA_O[ra["O"] % 2]
                ra["O"] += 1
                if mode == "diff":
                    k = ra["acc"] % 2
                    ra["acc"] += 1
                    m["_acc"] = {"dve": pA_acc[k], "pool": accP[k]}
                    m["_zs"] = zsD[k]
                    m["_init"] = {"dve": False, "pool": False, "pe": False}
            O = m["_O"]
            va = VA[m["va"]]
            MM(O.ap, va.ap[:, j, :], pp.ap[:, 0:512], j == 0, j == nkt - 1, [va.buf, pp.buf], [O.buf])
            if mode == "diff":
                r8 = j % 8
                eng = "pe" if r8 == 7 else ("pool" if r8 in (1, 3, 5) else "dve")
                if eng == "pe":
                    MM(pA_Zb.ap, ones, pp.ap[:, 0:512], not m["_init"]["pe"], False, [b_c, pp.buf], [pA_Zb.buf])
                    m["_init"]["pe"] = True
                else:
                    acc = m["_acc"][eng]
                    aap = acc.ap[:, 0:512]
                    if not m["_init"][eng]:
                        CP(eng, aap, pp.ap[:, 0:512], [pp.buf], [acc.buf])
                        m["_init"][eng] = True
                    else:
                        TT(eng, aap, aap, pp.ap[:, 0:512], ALU.add, [acc.buf, pp.buf], [acc.buf])
            if j == nkt - 1:
                if mode == "aug":
                    epi_aug(qi, mi)
                else:
                    epi_diff(qi, mi)

        def epi_aug(qi, mi):
            m = maps[mi]
            O = m["_O"]
            tok0 = qts[qi]["tok0"]
            g = tok0 // 512
            r = rz[ra["rz"] % 2]
            ra["rz"] += 1
            y = ys[ra["ys"] % 4]
            ra["ys"] += 1
            if recip == "dve":
                RECIP(r.ap[64:128, :], O.ap[64:128, :], [O.buf], [r.buf])
            else:
                ACT(r.ap[64:128, :], O.ap[64:128, :], AF.Ln, [O.buf], [r.buf])
                ACT(r.ap[64:128, :], r.ap[64:128, :], AF.Exp, [r.buf], [r.buf], scale=-1.0)
            TT("dve", y.ap[0:64, :], O.ap[0:64, :], r.ap[64:128, :], ALU.mult, [O.buf, r.buf], [y.buf])
            ch, hf = m["out"]
            DMA(YT[ch, hf * 64:(hf + 1) * 64, tok0:tok0 + 512], y.ap[0:64, :], y.sem, [y.buf], [b_YT[g]])

        def epi_diff(qi, mi):
            m = maps[mi]
            O = m["_O"]
            acc = m["_acc"]
            sD = m["_zs"]
            tok0 = qts[qi]["tok0"]
            g = tok0 // 512
            o = l // 2
            t = d_t[mi]
            r = rz[ra["rz"] % 2]
            ra["rz"] += 1
            has_pool = m["_init"]["pool"]
            has_pe = m["_init"]["pe"]
            CP("dve", sD.ap, acc["dve"].ap[:, 0:512], [acc["dve"].buf], [sD.buf])

            def st1():
                MM(pA_Zb.ap, ones_f, sD.ap, not has_pe, not has_pool, [cmat_f.buf, sD.buf], [pA_Zb.buf])
                if has_pool:
                    MM(pA_Zb.ap, ones_f, acc["pool"].ap[:, 0:512], False, True, [cmat_f.buf, acc["pool"].buf],
                       [pA_Zb.buf])
                ACT(r.ap, pA_Zb.ap, AF.Ln, [pA_Zb.buf], [r.buf])
                ACT(r.ap, r.ap, AF.Exp, [r.buf], [r.buf], scale=-1.0)
                TT("dve", t.ap, O.ap, r.ap, ALU.mult, [O.buf, r.buf], [t.buf])

            def st2():
                nl = DV.ap[:, dvc[f"nlam{o}"]:dvc[f"nlam{o}"] + 1]
                STT(d_o.ap, d_t[1].ap, nl, d_t[0].ap, ALU.mult, ALU.add, [d_t[0].buf, d_t[1].buf, DV.buf], [d_o.buf])
                ACT(d_sq.ap, d_o.ap, AF.Square, [d_o.buf], [d_sq.buf])

            def st3():
                gsv = DV.ap[:, dvc[f"gs{o}"]:dvc[f"gs{o}"] + 1]
                MM(pA_ms.ap, onesm, d_sq.ap, True, True, [b_c, d_sq.buf], [pA_ms.buf])
                ACT(d_ln.ap, pA_ms.ap, AF.Ln, [pA_ms.buf, DV.buf], [d_ln.buf], bias=EPSB)
                ACT(d_rs.ap, d_ln.ap, AF.Exp, [d_ln.buf], [d_rs.buf], scale=-0.5)
                y = ys[ra["ys"] % 4]
                ra["ys"] += 1
                STT(y.ap, d_o.ap, gsv, d_rs.ap, ALU.mult, ALU.mult, [d_o.buf, d_rs.buf, DV.buf], [y.buf])
                DMA(YT[m["out"], :, tok0:tok0 + 512], y.ap, y.sem, [y.buf], [b_YT[g]])

            pending.append([3, st1])
            if mi == 1:
                pending.append([6, st2])
                pending.append([9, st3])

        def tick(flush=False):
            keep = []
            for it in pending:
                it[0] -= 1
                if it[0] <= 0 or flush:
                    it[1]()
                else:
                    keep.append(it)
            pending[:] = keep

        n = len(units)
        for i in range(n + L):
            if i < n:
                front(i)
            if i - L >= 0:
                back(i - L)
            tick()
        while pending:
            tick(flush=True)

    def q_loader(chunks, tok0, rb):
        def load(q):
            for ci, ch in enumerate(chunks):
                DMA(q.ap[:, ci, :], FM[ch, :, tok0:tok0 + 512], q.sem, rb, [q.buf])
        return load

    def phase_A_mem(l):
        for s in range(2):
            t0, ln = segs[s]
            gl = groups_of_seg(s)
            for pr in range(2):
                load_KT(FMm[pr, 0:64, s * 256:(s + 1) * 256], FMm[pr, 64:128, s * 256:(s + 1) * 256], 256, [b_FMm])
                for hh in range(2):
                    DMA(VA[hh].ap[:, 0:2, :], VMm[2 * pr + hh, :, 2 * s:2 * s + 2, :], VA[hh].sem,
                        [b_VMm], [VA[hh].buf])
                maps = [dict(qc=0, half=0, va=0, out=(8 + pr, 0)), dict(qc=0, half=1, va=1, out=(8 + pr, 1))]
                qts = [dict(load=q_loader([18 + pr], g * 512, [b_FM[g]]), tok0=g * 512) for g in gl]
                dense_units(maps, 2, qts, "aug", l, "act")

    def phase_A_odd(l):
        for s in range(2):
            t0, ln = segs[s]
            gl = groups_of_seg(s)
            rbs = [b_FM[g] for g in gl]
            rvs = [b_VM[g] for g in gl]
            nkt = ln // 128
            j0 = t0 // 128
            for kv in range(2):
                srck = FM[4, kv * 64:(kv + 1) * 64, t0:t0 + ln]
                load_KT(srck, srck, ln, rbs)
                DMA(VA[0].ap[:, 0:nkt, :], VM[kv, :, j0:j0 + nkt, :], VA[0].sem, rvs, [VA[0].buf])
                maps = []
                for hh in range(4):
                    h = 4 * kv + hh
                    maps.append(dict(qc=hh // 2, half=hh % 2, va=0, out=(h // 2, h % 2)))
                qts = [dict(load=q_loader([2 * kv, 2 * kv + 1], g * 512, [b_FM[g]]), tok0=g * 512) for g in gl]
                dense_units(maps, nkt, qts, "aug", l, "dve")
            for h in range(4):
                load_KT(FM[10 + h, 0:64, t0:t0 + ln], FM[10 + h, 64:128, t0:t0 + ln], ln, rbs)
                DMA(VA[0].ap[:, 0:nkt, :], VM[2 + h, :, j0:j0 + nkt, :], VA[0].sem, rvs, [VA[0].buf])
                maps = [dict(qc=0, half=0, va=0, out=4 + h), dict(qc=0, half=1, va=0, out=4 + h)]
                qts = [dict(load=q_loader([6 + h], g * 512, [b_FM[g]]), tok0=g * 512) for g in gl]
                dense_units(maps, nkt, qts, "diff", l, "act")

    def phase_A_even(l):
        e = l // 2
        esc = dvc[f"esink{e}"]
        maskP = masks.ap[:, 0:256]
        maskN = masks.ap[:, 256:512]
        MEMSET("dve", ESR.ap, 0.0, [ESR.buf])
        for kv in range(2):
            for hf in range(2):
                for ci in range(2):
                    h = 4 * kv + 2 * ci + hf
                    dst = ESR.ap[:, kv * 2 + hf, ci * 128:(ci + 1) * 128]
                    TS("dve", dst, dst, DV.ap[:, esc + h:esc + h + 1], ALU.add, [ESR.buf, DV.buf], [ESR.buf])
        for s in range(2):
            t0, ln = segs[s]
            gl = groups_of_seg(s)
            rbs = [b_FM[g] for g in gl]
            rvs = [b_VM[g] for g in gl]
            nb = ln // 128
            j0 = t0 // 128
            for kv in range(2):
                srck = FM[16, kv * 64:(kv + 1) * 64, t0:t0 + ln]
                load_KT(srck, srck, ln, rbs)
                DMA(VA[0].ap[:, 0:nb, :], VM[kv, :, j0:j0 + nb, :], VA[0].sem, rvs, [VA[0].buf])
                units = []
                for g in gl:
                    for bi in range(4):
                        i = (g * 512 - t0) // 128 + bi
                        for hf in range(2):
                            kbs = [jj for jj in (i - 1, i, i + 1) if 0 <= jj < nb]
                            for jj in kbs:
                                units.append((g, bi, i, hf, jj, jj == kbs[0], jj == kbs[-1]))
                qtile = {}
                cur = {}
                ost = {}

                def front(u, kv=kv):
                    g, bi, i, hf, jj, first, last = units[u]
                    if bi == 0 and hf == 0 and first:
                        q = Qs[ra["Qs"] % 2]
                        ra["Qs"] += 1
                        for ci in range(2):
                            DMA(q.ap[:, ci, :], FM[12 + 2 * kv + ci, :, g * 512:(g + 1) * 512], q.sem,
                                [b_FM[g]], [q.buf])
                        qtile[g] = q
                    q = qtile[g]
                    sc = pA_sc[ra["sc"] % 3]
                    ra["sc"] += 1
                    p = pt[ra["pt"] % 8]
                    ra["pt"] += 1
                    kt = KTa if hf == 0 else KTb
                    scv = sc.ap[:, 0:256]
                    rhs = q.ap[:, :, bi * 128:(bi + 1) * 128]
                    MM(scv, kt.ap[:, jj * 128:(jj + 1) * 128], rhs, True, jj == i, [kt.buf, q.buf], [sc.buf])
                    if jj != i:
                        MM(scv, ident, maskP if jj < i else maskN, False, True, [b_c, masks.buf], [sc.buf])
                    ACT(p.ap[:, 0:256], scv, AF.Exp, [sc.buf], [p.buf], scale=0.125)
                    cur[u] = p

                def back(u, kv=kv):
                    g, bi, i, hf, jj, first, last = units[u]
                    p = cur.pop(u)
                    if first:
                        ost[(i, hf)] = pA_O[ra["O"] % 2]
                        ra["O"] += 1
                    O = ost[(i, hf)]
                    Ov = O.ap[:, 0:256]
                    MM(Ov, VA[0].ap[:, jj, :], p.ap[:, 0:256], first, last, [VA[0].buf, p.buf], [O.buf])
                    if last:
                        if bi == 0 and hf == 0:
                            ost["ysw"] = ysw[ra["ysw"] % 2]
                            ra["ysw"] += 1
                        yw = ost["ysw"]
                        z = zs[ra["rz"] % 2]
                        ra["rz"] += 1
                        TT("dve", z.ap[64:128, 0:256], O.ap[64:128, 0:256], ESR.ap[64:128, kv * 2 + hf, :], ALU.add,
                           [O.buf, ESR.buf], [z.buf])
                        ACT(z.ap[64:128, 0:256], z.ap[64:128, 0:256], AF.Ln, [z.buf], [z.buf])
                        ACT(z.ap[64:128, 0:256], z.ap[64:128, 0:256], AF.Exp, [z.buf], [z.buf], scale=-1.0)
                        TT("dve", yw.ap[:, 2 * hf:2 * hf + 2, bi * 128:(bi + 1) * 128],
                           O.ap[0:64, 0:256].rearrange("p (c q) -> p c q", c=2),
                           z.ap[64:128, 0:256].rearrange("p (c q) -> p c q", c=2), ALU.mult,
                           [O.buf, z.buf], [yw.buf])
                        if bi == 3 and hf == 1:
                            for hf2 in range(2):
                                for ci in range(2):
                                    h = 4 * kv + 2 * ci + hf2
                                    DMA(YT[4 + h // 2, (h % 2) * 64:(h % 2) * 64 + 64, g * 512:(g + 1) * 512],
                                        yw.ap[:, 2 * hf2 + ci, :], yw.sem, [yw.buf], [b_YT[g]])

                n = len(units)
                L = 2
                for ii in range(n + L):
                    if ii < n:
                        front(ii)
                    if ii - L >= 0:
                        back(ii - L)

    aC = Arena("C")
    WO = aC.take("WO", [128, 10, D], BF16)
    wso = [aC.take(f"wso{i}", [128, D], F32) for i in range(2)]
    Yb = [aC.take(f"Yb{i}", [128, 10, 512], BF16) for i in range(2)]
    SZ = [aC.take(f"SZ{i}", [128, 10, 512], BF16) for i in range(2)]
    YZ = [aC.take(f"YZ{i}", [128, 10, 512], BF16) for i in range(2)]
    INb = [aC.take(f"IN{i}", [128, 4, 520], BF16) for i in range(2)]
    GBb = [aC.take(f"GB{i}", [128, 4, 512], BF16) for i in range(2)]
    ctmp = [aC.take(f"ct{i}", [128, 512], F32) for i in range(2)]
    xc = [aC.take(f"xc{i}", [128, D], F32) for i in range(8)]
    xn = [aC.take(f"xn{i}", [128, D], F32) for i in range(3)]
    rc = {"w": 0, "xc": 0, "xn": 0, "bank": 0, "ct": 0}

    def phase_C(l):
        even = (l % 2 == 0)
        engs = ["act", "dve", "pool"]
        for c in range(10):
            s = wso[rc["w"] % 2]
            DMA(s.ap, w_out[l, c * 128:(c + 1) * 128, :], s.sem, [], [s.buf])
            CP(engs[rc["w"] % 3], WO.ap[:, c, :], s.ap, [s.buf], [WO.buf])
            rc["w"] += 1

        def loads(g):
            s = seg_of_group(g)
            t0, ln = segs[s]
            k = g % 2
            Y, Z = Yb[k], SZ[k]
            c0 = 4 if even else 0
            DMA(Y.ap[:, c0:10, :], YT[c0:10, :, g * 512:(g + 1) * 512].rearrange("c p t -> p c t"), Y.sem,
                [b_YT[g]], [Y.buf])
            DMA(Z.ap, FM[20:30, :, g * 512:(g + 1) * 512].rearrange("c p t -> p c t"), Z.sem, [b_FM[g]], [Z.buf])
            if even:
                I_, Gb = INb[k], GBb[k]
                lo = g * 512 - 1
                hi = g * 512 + 513
                a0 = 0
                if lo < t0:
                    MEMSET("pool", I_.ap[:, :, 0:1], 0.0, [I_.buf])
                    lo += 1
                    a0 = 1
                a1 = 514
                if hi > t0 + ln:
                    MEMSET("pool", I_.ap[:, :, 513:514], 0.0, [I_.buf])
                    hi -= 1
                    a1 = 513
                rb = [b_FM[g]]
                if g - 1 >= 0:
                    rb.append(b_FM[g - 1])
                if g + 1 < NG:
                    rb.append(b_FM[g + 1])
                DMA(I_.ap[:, :, a0:a1], FM[4:8, :, lo:hi].rearrange("c p t -> p c t"), I_.sem, rb, [I_.buf])
                DMA(Gb.ap, FM[0:4, :, g * 512:(g + 1) * 512].rearrange("c p t -> p c t"), Gb.sem, [b_FM[g]], [Gb.buf])
            xs = []
            for i in range(4):
                t = xc[rc["xc"] % 8]
                rc["xc"] += 1
                r0 = g * 512 - t0 + i * 128
                ap, rb = x_src(l, s, r0, 128)
                DMA(t.ap, ap, t.sem, rb, [t.buf])
                xs.append(t)
            return xs

        pend = loads(0)
        for g in range(NG):
            s = seg_of_group(g)
            t0, ln = segs[s]
            k = g % 2
            xs = pend
            Y, Z, yz = Yb[k], SZ[k], YZ[k]
            if even:
                e = l // 2
                I_, Gb = INb[k], GBb[k]
                cw = gcols[f"cw{e}"]
                for c in range(4):
                    ct = ctmp[rc["ct"] % 2]
                    rc["ct"] += 1
                    w0 = G.ap[:, cw + 0 * 4 + c:cw + 0 * 4 + c + 1]
                    w1 = G.ap[:, cw + 1 * 4 + c:cw + 1 * 4 + c + 1]
                    w2 = G.ap[:, cw + 2 * 4 + c:cw + 2 * 4 + c + 1]
                    TS("dve", ct.ap, I_.ap[:, c, 0:512], w0, ALU.mult, [I_.buf, G.buf], [ct.buf])
                    STT(ct.ap, I_.ap[:, c, 1:513], w1, ct.ap, ALU.mult, ALU.add, [I_.buf, G.buf, ct.buf], [ct.buf])
                    STT(ct.ap, I_.ap[:, c, 2:514], w2, ct.ap, ALU.mult, ALU.add, [I_.buf, G.buf, ct.buf], [ct.buf])
                    TT("dve", Y.ap[:, c, :], ct.ap, Gb.ap[:, c, :], ALU.mult, [ct.buf, Gb.buf], [Y.buf])
            for hf in range(2):
                TT("dve", yz.ap[:, 5 * hf:5 * hf + 5, :], Y.ap[:, 5 * hf:5 * hf + 5, :], Z.ap[:, 5 * hf:5 * hf + 5, :],
                   ALU.mult, [Y.buf, Z.buf], [yz.buf])
            if g + 1 < NG:
                pend = loads(g + 1)
            for i in range(4):
                xo = xn[rc["xn"] % 3]
                rc["xn"] += 1
                for nh in range(2):
                    bank = banks[rc["bank"] % 4]
                    rc["bank"] += 1
                    for c in range(10):
                        MM(bank.ap, yz.ap[:, c, i * 128:(i + 1) * 128], WO.ap[:, c, nh * 512:(nh + 1) * 512],
                           c == 0, c == 9, [yz.buf, WO.buf], [bank.buf])
                    TT("dve", xo.ap[:, nh * 512:(nh + 1) * 512], bank.ap, xs[i].ap[:, nh * 512:(nh + 1) * 512], ALU.add,
                       [bank.buf, xs[i].buf], [xo.buf])
                r0 = g * 512 - t0 + i * 128
                DMA(y_out[s][r0:r0 + 128, :], xo.ap, xo.sem, [xo.buf], [b_Y[g]])

    P.marks = []

    def mark(name):
        P.marks.append((name, dict(P.count)))

    for l in range(DEPTH):
        P.barrier()
        mark(f"L{l} P")
        phase_P(l)
        P.barrier()
        mark(f"L{l} Amem")
        phase_A_mem(l)
        mark(f"L{l} A")
        if l % 2 == 0:
            phase_A_even(l)
        else:
            phase_A_odd(l)
        P.barrier()
        mark(f"L{l} C")
        phase_C(l)
    mark("end")
    P.wait_all("sp", b_Y)
    P.barrier()

    with nc.Block() as block:
        P.replay(block)
    return nc, P


_CONST_CACHE = {}


def _consts(smax):
    if smax not in _CONST_CACHE:
        _CONST_CACHE[smax] = (_const_mats(), _masks(), _rope_tables(smax))
    return _CONST_CACHE[smax]


def run(x_prompt, x_sample, mem_prompt, mem_sample, norm_g, w_in, w_out, mem_norm_g, w_mem_kv, mem_qk_g,
        conv_w, swa_qk_g, swa_sink, ax_qk_g, diff_qk_g, diff_lambda, diff_subln_g, depth=None, n_cores=8):
    f = lambda a: np.ascontiguousarray(np.asarray(a), dtype=np.float32)
    x_prompt, x_sample, mem_prompt, mem_sample = f(x_prompt), f(x_sample), f(mem_prompt), f(mem_sample)
    B, SP, _ = x_prompt.shape
    SS = x_sample.shape[1]
    DEPTH = int(depth if depth is not None else np.asarray(norm_g).shape[0])
    nc, P = build_program(SP, SS, DEPTH)
    cm, mk, tabs = _consts(max(SP, SS))
    gp, _ = _gpack(DEPTH, f(norm_g), f(mem_norm_g), f(mem_qk_g), f(conv_w), f(swa_qk_g), f(swa_sink),
                   f(ax_qk_g), f(diff_qk_g), f(diff_lambda), f(diff_subln_g))
    w_in, w_out, w_mem = f(w_in)[:DEPTH], f(w_out)[:DEPTH], f(w_mem_kv)[:DEPTH]
    in_maps = []
    for c in range(n_cores):
        in_maps.append({
            "x_p": x_prompt[c], "x_s": x_sample[c],
            "mem": np.ascontiguousarray(np.concatenate([mem_prompt[c], mem_sample[c]], axis=0)),
            "w_in": w_in, "w_out": w_out, "w_mem": w_mem, "gpack": gp, "cmat": cm, "masks": mk, "tabs": tabs,
        })
    res = run_bass_kernel_spmd(nc, in_maps, core_ids=list(range(n_cores)))
    yp = np.stack([np.asarray(r["y_p"], dtype=np.float32) for r in res.results])
    ysm = np.stack([np.asarray(r["y_s"], dtype=np.float32) for r in res.results])
    return yp, ysm


def kernel(x_prompt, x_sample, mem_prompt, mem_sample, norm_g, w_in, w_out, mem_norm_g, w_mem_kv, mem_qk_g,
           conv_w, swa_qk_g, swa_sink, ax_qk_g, diff_qk_g, diff_lambda, diff_subln_g):
    return run(x_prompt, x_sample, mem_prompt, mem_sample, norm_g, w_in, w_out, mem_norm_g, w_mem_kv, mem_qk_g,
               conv_w, swa_qk_g, swa_sink, ax_qk_g, diff_qk_g, diff_lambda, diff_subln_g)
```

```python
import math
import numpy as np
import concourse.bass as bass
import concourse.mybir as mybir
from concourse.bass_utils import run_bass_kernel_spmd

F32 = mybir.dt.float32
BF16 = mybir.dt.bfloat16
U8 = mybir.dt.uint8
AF = mybir.ActivationFunctionType
ALU = mybir.AluOpType

D = 1024
INW = 3840
MIXW = 1280
NMEM = 256
EPS = 1e-6
NEGM = -30000.0
ENGS = ("pe", "act", "dve", "pool", "sp")


class Buf:
    __slots__ = ("name", "w", "r")

    def __init__(self, name):
        self.name = name
        self.w = {}
        self.r = {}


class DmaSem:
    __slots__ = ("handle", "count", "name")

    def __init__(self, handle, name):
        self.handle = handle
        self.count = 0
        self.name = name


class Plan:
    def __init__(self, nc):
        self.nc = nc
        self.items = {e: [] for e in ENGS}
        self.count = {e: 0 for e in ENGS}
        self.waited = {e: {} for e in ENGS}
        self.sems = {}
        for e in ENGS:
            if e != "sp":
                self.sems[e] = nc.alloc_semaphore(name=f"s_{e}")
        self.dmasems = []
        self.n_inst = 0

    def dma_sem(self, name):
        s = DmaSem(self.nc.alloc_semaphore(name=f"d_{name}"), name)
        self.dmasems.append(s)
        return s

    def _deps(self, eng, reads, writes):
        deps = {}
        for b in reads:
            for k, v in b.w.items():
                if deps.get(k, 0) < v:
                    deps[k] = v
        for b in writes:
            for k, v in b.w.items():
                if deps.get(k, 0) < v:
                    deps[k] = v
            for k, v in b.r.items():
                if deps.get(k, 0) < v:
                    deps[k] = v
        waits = []
        wd = self.waited[eng]
        for k, v in deps.items():
            if k == "pe" and eng == "pe":
                continue
            if wd.get(k, 0) >= v:
                continue
            wd[k] = v
            waits.append((k, v))
        return waits

    def _mark(self, key, val, reads, writes):
        for b in reads:
            if b.r.get(key, 0) < val:
                b.r[key] = val
        for b in writes:
            if b.r:
                b.w = {}
                b.r = {}
            b.w[key] = val

    def op(self, eng, fn, reads=(), writes=()):
        waits = self._deps(eng, reads, writes)
        self.count[eng] += 1
        self.items[eng].append((waits, fn, (eng, 1)))
        self._mark(eng, self.count[eng], reads, writes)
        self.n_inst += 1

    def dma(self, q, fn, sem, reads=(), writes=()):
        waits = self._deps(q, reads, writes)
        sem.count += 16
        self.items[q].append((waits, fn, (sem, 16)))
        self._mark(sem, sem.count, reads, writes)
        self.n_inst += 1

    def wait_all(self, eng, bufs):
        waits = self._deps(eng, bufs, ())
        self.items[eng].append((waits, None, None))

    def barrier(self):
        snap = [(e, self.count[e]) for e in ENGS if e != "sp"]
        snap += [(s, s.count) for s in self.dmasems]
        for e in ENGS:
            waits = []
            wd = self.waited[e]
            for k, v in snap:
                if v == 0 or k == e:
                    continue
                if wd.get(k, 0) >= v:
                    continue
                wd[k] = v
                waits.append((k, v))
            if waits:
                self.items[e].append((waits, None, None))

    def _semh(self, k):
        return k.handle if isinstance(k, DmaSem) else self.sems[k]

    def replay(self, block):
        plan = self

        def run(engine, name):
            for waits, fn, inc in plan.items[name]:
                for k, v in waits:
                    engine.wait_ge(plan._semh(k), v)
                if fn is None:
                    continue
                inst = fn(engine)
                inst.then_inc(plan._semh(inc[0]), inc[1])

        @block.tensor
        def _(e):
            run(e, "pe")

        @block.scalar
        def _(e):
            run(e, "act")

        @block.vector
        def _(e):
            run(e, "dve")

        @block.gpsimd
        def _(e):
            run(e, "pool")

        @block.sync
        def _(e):
            run(e, "sp")


class Tl:
    __slots__ = ("ap", "buf", "_sem", "plan", "name")

    def __init__(self, plan, name, ap):
        self.plan = plan
        self.name = name
        self.ap = ap
        self.buf = Buf(name)
        self._sem = None

    @property
    def sem(self):
        if self._sem is None:
            self._sem = self.plan.dma_sem(self.name)
        return self._sem


def _const_mats():
    ident = np.eye(128, dtype=np.float32)
    k = np.arange(128)
    bd = (k[:, None] // 64 == k[None, :] // 64).astype(np.float32) / 64.0
    onesm = np.full((128, 128), 1.0 / 128.0, np.float32)
    ones = np.ones((128, 128), np.float32)
    rot1 = np.zeros((128, 128), np.float32)
    rota = np.zeros((128, 128), np.float32)
    for m in range(128):
        if (m % 64) < 32:
            rot1[m + 32, m] = -1.0
        else:
            rot1[m - 32, m] = 1.0
        if (m % 32) < 16:
            rota[m + 16, m] = -1.0
        else:
            rota[m - 16, m] = 1.0
    return np.concatenate([ident, bd, onesm, ones, rot1, rota], axis=1)


def _masks():
    ki = np.arange(128)[:, None]
    qi = np.arange(128)[None, :]
    mp = np.where(qi <= ki, 0.0, NEGM).astype(np.float32)
    mn = np.where(ki <= qi, 0.0, NEGM).astype(np.float32)
    return np.concatenate([mp, mp, mn, mn], axis=1)


def _rope_tables(smax):
    theta = np.float32(10000.0)
    pos = np.arange(smax)
    f = np.arange(128) % 64
    inv1 = (theta ** (-(np.arange(0, 64, 2, dtype=np.float32)) / np.float32(64))).astype(np.float32)
    ang1 = (pos.astype(np.float32)[None, :] * inv1[f % 32][:, None]).astype(np.float32)
    inva = (theta ** (-(np.arange(0, 32, 2, dtype=np.float32)) / np.float32(32))).astype(np.float32)
    prow = (pos // 64).astype(np.float32)
    pcol = (pos % 64).astype(np.float32)
    fa = f % 32
    pa = np.where((f < 32)[:, None], prow[None, :], pcol[None, :]).astype(np.float32)
    anga = (pa * inva[fa % 16][:, None]).astype(np.float32)
    tabs = np.stack([np.cos(ang1.astype(np.float64)), np.sin(ang1.astype(np.float64)),
                     np.cos(anga.astype(np.float64)), np.sin(anga.astype(np.float64))]).astype(np.float32)
    return tabs


def _gpack(depth, norm_g, mem_norm_g, mem_qk_g, conv_w, swa_qk_g, swa_sink, ax_qk_g, diff_qk_g,
           diff_lambda, diff_subln_g):
    cols = {}
    parts = []
    pos = [0]

    def add(name, a):
        a = np.ascontiguousarray(a, dtype=np.float32)
        assert a.shape[0] == 128
        cols[name] = pos[0]
        parts.append(a)
        pos[0] += a.shape[1]

    dup = lambda v: np.tile(v, 2)[:, None]
    for l in range(depth):
        add(f"ng{l}", norm_g[l].reshape(8, 128).T)
        add(f"mng{l}", mem_norm_g[l].reshape(8, 128).T)
        add(f"mqg{l}", dup(mem_qk_g[l, 0]))
        add(f"mkg{l}", dup(mem_qk_g[l, 1]))
        if l % 2 == 0:
            e = l // 2
            add(f"sqg{e}", dup(swa_qk_g[e, 0]))
            add(f"skg{e}", dup(swa_qk_g[e, 1]))
            add(f"cw{e}", conv_w[e].reshape(3, 4, 128).transpose(2, 0, 1).reshape(128, 12))
            add(f"sink{e}", np.broadcast_to(swa_sink[e][None, :], (128, 8)))
        else:
            o = l // 2
            add(f"aqg{o}", dup(ax_qk_g[o, 0]))
            add(f"akg{o}", dup(ax_qk_g[o, 1]))
            add(f"dqg{o}", dup(diff_qk_g[o, 0]))
            add(f"dkg{o}", dup(diff_qk_g[o, 1]))
            add(f"sub{o}", diff_subln_g[o][:, None])
            add(f"lam{o}", np.broadcast_to(diff_lambda[o].reshape(1, 256), (128, 256)))
    return np.concatenate(parts, axis=1), cols


def _gpack_cols(depth):
    z = lambda *s: np.zeros(s, np.float32)
    _, cols = _gpack(depth, z(depth, D), z(depth, D), z(depth, 2, 64), z((depth + 1) // 2, 3, 512),
                     z((depth + 1) // 2, 2, 64), z((depth + 1) // 2, 8), z(max(depth // 2, 1), 2, 64),
                     z(max(depth // 2, 1), 2, 64), z(max(depth // 2, 1), 4, 64), z(max(depth // 2, 1), 128))
    g, _ = _gpack(depth, z(depth, D), z(depth, D), z(depth, 2, 64), z((depth + 1) // 2, 3, 512),
                  z((depth + 1) // 2, 2, 64), z((depth + 1) // 2, 8), z(max(depth // 2, 1), 2, 64),
                  z(max(depth // 2, 1), 2, 64), z(max(depth // 2, 1), 4, 64), z(max(depth // 2, 1), 128))
    return cols, g.shape[1]


def build_program(SP, SS, DEPTH, debug_layers=None):
    nc = bass.Bass("TRN2", target_bir_lowering=False)
    T = SP + SS
    NG = T // 512
    SMAX = max(SP, SS)
    segs = [(0, SP), (SP, SS)]
    gcols, NGC = _gpack_cols(DEPTH)

    x_p = nc.dram_tensor("x_p", [SP, D], F32, kind="ExternalInput").ap()
    x_s = nc.dram_tensor("x_s", [SS, D], F32, kind="ExternalInput").ap()
    mem = nc.dram_tensor("mem", [2 * NMEM, D], F32, kind="ExternalInput").ap()
    w_in = nc.dram_tensor("w_in", [DEPTH, D, INW], F32, kind="ExternalInput").ap()
    w_out = nc.dram_tensor("w_out", [DEPTH, MIXW, D], F32, kind="ExternalInput").ap()
    w_mem = nc.dram_tensor("w_mem", [DEPTH, D, 512], F32, kind="ExternalInput").ap()
    gpack_d = nc.dram_tensor("gpack", [128, NGC], F32, kind="ExternalInput").ap()
    cmat_d = nc.dram_tensor("cmat", [128, 768], F32, kind="ExternalInput").ap()
    masks_d = nc.dram_tensor("masks", [128, 512], F32, kind="ExternalInput").ap()
    tabs_d = nc.dram_tensor("tabs", [4, 128, SMAX], F32, kind="ExternalInput").ap()
    y_p = nc.dram_tensor("y_p", [SP, D], F32, kind="ExternalOutput").ap()
    y_s = nc.dram_tensor("y_s", [SS, D], F32, kind="ExternalOutput").ap()
    FM = nc.dram_tensor("FM", [30, 128, T], BF16, kind="Internal").ap()
    FMm = nc.dram_tensor("FMm", [2, 128, 512], BF16, kind="Internal").ap()
    VM = nc.dram_tensor("VM", [6, 128, T // 128, 128], BF16, kind="Internal").ap()
    VMm = nc.dram_tensor("VMm", [4, 128, 4, 128], BF16, kind="Internal").ap()
    YT = nc.dram_tensor("YT", [10, 128, T], BF16, kind="Internal").ap()

    x_in = [x_p, x_s]
    y_out = [y_p, y_s]

    P = Plan(nc)

    b_FM = [Buf(f"FM{g}") for g in range(NG)]
    b_VM = [Buf(f"VM{g}") for g in range(NG)]
    b_YT = [Buf(f"YT{g}") for g in range(NG)]
    b_Y = [Buf(f"Y{g}") for g in range(NG)]
    b_FMm = Buf("FMm")
    b_VMm = Buf("VMm")

    def seg_of_group(g):
        return 0 if g * 512 < SP else 1

    def groups_of_seg(s):
        t0, ln = segs[s]
        return list(range(t0 // 512, (t0 + ln) // 512))

    def sb(name, shape, dt):
        return Tl(P, name, nc.alloc_sbuf_tensor("sb_" + name, shape, dt).ap())

    G = sb("G", [128, NGC], F32)
    DV = sb("DV", [128, 64], F32)
    cmat_f = sb("cmat_f", [128, 768], F32)
    cmat = sb("cmat", [128, 768], BF16)
    masks_f = sb("masks_f", [128, 512], F32)
    masks = sb("masks", [128, 512], BF16)
    stat = sb("stat", [128, 64], F32)
    ident = cmat.ap[:, 0:128]
    bdm = cmat.ap[:, 128:256]
    onesm = cmat.ap[:, 256:384]
    ones = cmat.ap[:, 384:512]
    rot1 = cmat.ap[:, 512:640]
    rota = cmat.ap[:, 640:768]
    b_c = cmat.buf

    ARENA_BYTES = 172 * 1024
    arena = nc.alloc_sbuf_tensor("arena", [128, ARENA_BYTES], U8).ap()

    class Arena:
        def __init__(self, tag):
            self.off = 0
            self.tag = tag
            self.n = 0

        def take(self, name, shape, dt):
            esz = 4 if dt == F32 else 2
            n = int(np.prod(shape[1:])) * esz
            self.off = (self.off + 63) // 64 * 64
            assert self.off + n <= ARENA_BYTES, (self.tag, name, self.off, n)
            a = arena[0:shape[0], self.off:self.off + n].bitcast(dt)
            self.off += n
            if len(shape) == 3:
                a = a.rearrange("p (a b) -> p a b", a=shape[1])
            elif len(shape) == 4:
                a = a.rearrange("p (a b c) -> p a b c", a=shape[1], b=shape[2])
            return Tl(P, f"{self.tag}_{name}", a)

    psall = nc.alloc_psum_tensor("psall", [128, 4096], F32).ap()
    banks = [Tl(P, f"ps{i}", psall[:, i * 512:(i + 1) * 512]) for i in range(8)]
    ones_f = cmat_f.ap[:, 384:512]

    def MM(out, lhsT, rhs, start, stop, R, W):
        P.op("pe", lambda e: e.matmul(out, lhsT=lhsT, rhs=rhs, start=start, stop=stop), R, W)

    def TRN(out, in_, R, W):
        P.op("pe", lambda e: e.transpose(out, in_, ident), list(R) + [b_c], W)

    def ACT(out, in_, func, R, W, scale=None, bias=None, accum=None):
        kw = {}
        if scale is not None:
            kw["scale"] = scale
        if bias is not None:
            kw["bias"] = bias
        if accum is not None:
            kw["accum_out"] = accum
        P.op("act", lambda e: e.activation(out=out, in_=in_, func=func, **kw), R, W)

    def TT(eng, out, in0, in1, op, R, W):
        P.op(eng, lambda e: e.tensor_tensor(out=out, in0=in0, in1=in1, op=op), R, W)

    def TS(eng, out, in0, s1, op0, R, W, s2=None, op1=None):
        if op1 is None:
            P.op(eng, lambda e: e.tensor_scalar(out=out, in0=in0, scalar1=s1, scalar2=None, op0=op0), R, W)
        else:
            P.op(eng, lambda e: e.tensor_scalar(out=out, in0=in0, scalar1=s1, scalar2=s2, op0=op0, op1=op1), R, W)

    def STT(out, in0, scalar, in1, op0, op1, R, W):
        P.op("dve", lambda e: e.scalar_tensor_tensor(out=out, in0=in0, scalar=scalar, in1=in1, op0=op0, op1=op1), R, W)

    def CP(eng, out, in_, R, W):
        if eng == "act":
            P.op("act", lambda e: e.activation(out=out, in_=in_, func=AF.Copy), R, W)
        else:
            P.op(eng, lambda e: e.tensor_copy(out=out, in_=in_), R, W)

    def RECIP(out, in_, R, W):
        P.op("dve", lambda e: e.reciprocal(out=out, in_=in_), R, W)

    def REDUCE(out, in_, R, W):
        P.op("dve", lambda e: e.tensor_reduce(out=out, in_=in_, axis=mybir.AxisListType.X, op=ALU.add), R, W)

    def MEMSET(eng, ap, val, W):
        P.op(eng, lambda e: e.memset(ap, val), (), W)

    def DMA(out, in_, sem, R, W, q="sp"):
        P.dma(q, lambda e: e.dma_start(out=out, in_=in_), sem, R, W)

    DMA(G.ap, gpack_d, G.sem, [], [G.buf])
    DMA(cmat_f.ap, cmat_d, cmat_f.sem, [], [cmat_f.buf])
    DMA(masks_f.ap, masks_d, masks_f.sem, [], [masks_f.buf])
    CP("dve", cmat.ap, cmat_f.ap, [cmat_f.buf], [cmat.buf])
    CP("dve", masks.ap, masks_f.ap, [masks_f.buf], [masks.buf])
    MEMSET("dve", DV.ap[:, 0:1], EPS, [DV.buf])
    EPSB = DV.ap[:, 0:1]
    dvc = {}
    dvpos = [1]

    def dv_take(name, n=1):
        dvc[name] = dvpos[0]
        dvpos[0] += n
        return DV.ap[:, dvc[name]:dvc[name] + n]

    for l in range(DEPTH):
        if l % 2 == 0:
            e = l // 2
            es = dv_take(f"esink{e}", 8)
            ACT(es, G.ap[:, gcols[f"sink{e}"]:gcols[f"sink{e}"] + 8], AF.Exp, [G.buf], [DV.buf])
        else:
            o = l // 2
            lam_init = 0.8 - 0.6 * math.exp(-0.3 * l)
            lc = gcols[f"lam{o}"]
            tmpv = dv_take(f"lamtmp{o}", 4)
            prod = sb(f"lamprod{o}", [128, 128], F32)
            TT("dve", prod.ap[:, 0:64], G.ap[:, lc:lc + 64], G.ap[:, lc + 64:lc + 128], ALU.mult, [G.buf], [prod.buf])
            TT("dve", prod.ap[:, 64:128], G.ap[:, lc + 128:lc + 192], G.ap[:, lc + 192:lc + 256], ALU.mult,
               [G.buf], [prod.buf])
            REDUCE(tmpv[:, 0:2], prod.ap.rearrange("p (a b) -> p a b", a=2), [prod.buf], [DV.buf])
            ACT(tmpv[:, 2:4], tmpv[:, 0:2], AF.Exp, [DV.buf], [DV.buf])
            nl = dv_take(f"nlam{o}", 1)
            TS("dve", nl, tmpv[:, 3:4], -lam_init, ALU.add, [DV.buf], [DV.buf])
            TT("dve", nl, nl, tmpv[:, 2:3], ALU.subtract, [DV.buf], [DV.buf])
            gsv = dv_take(f"gs{o}", 1)
            TS("dve", gsv, G.ap[:, gcols[f"sub{o}"]:gcols[f"sub{o}"] + 1], 1.0 - lam_init, ALU.mult,
               [G.buf], [DV.buf])

    def gcol(name, n=1, off=0):
        c = gcols[name] + off
        return G.ap[:, c:c + n]

    aP = Arena("P")
    WI = aP.take("WI", [128, 8, INW], BF16)
    WM = aP.take("WM", [128, 8, 512], BF16)
    wst = [aP.take(f"wst{i}", [128, 960], F32) for i in range(2)]
    xt = [aP.take(f"xt{i}", [128, D], F32) for i in range(4)]
    xb = [aP.take(f"xb{i}", [128, D], BF16) for i in range(4)]
    xT = [aP.take(f"xT{i}", [128, 8, 512], BF16) for i in range(2)]
    tabs = [[aP.take(f"tab{s}_{i}", [128, 512], F32) for i in range(4)] for s in range(2)]
    u_sb = [aP.take(f"u{i}", [128, 512], F32) for i in range(4)]
    sq_sb = [aP.take(f"sq{i}", [128, 512], BF16) for i in range(2)]
    rs_sb = [aP.take(f"rs{i}", [128, 512], F32) for i in range(2)]
    t_sb = [aP.take(f"t{i}", [128, 512], BF16) for i in range(2)]
    a_sb = [aP.take(f"a{i}", [128, 512], F32) for i in range(3)]
    b_sb = [aP.take(f"b{i}", [128, 512], F32) for i in range(2)]
    oc_sb = [aP.take(f"oc{i}", [128, 512], BF16) for i in range(4)]
    vst = [aP.take(f"vst{i}", [128, 6, 128], BF16) for i in range(2)]
    sst = [aP.take(f"sst{i}", [128, 4], F32) for i in range(4)]

    pP_main = [banks[0], banks[1], banks[2]]
    pP_ms = [banks[3], banks[4]]
    pP_rot = [banks[5], banks[6]]
    pP_tp = banks[7]
    tp_bf = pP_tp.ap.bitcast(BF16)

    rr = {"cast": 0, "oc": 0, "xt": 0, "xb": 0, "main": 0, "c2": 0, "vst": 0, "ms": 0, "rot": 0, "xT": 0, "tab": 0, "u": 0, "sq": 0, "rs": 0, "t": 0, "a": 0, "b": 0}

    def load_cast_weights(l):
        engs = ["act", "dve", "act"]
        for k in range(8):
            for hf in range(4):
                s = wst[rr["cast"] % 2]
                DMA(s.ap, w_in[l, k * 128:(k + 1) * 128, hf * 960:(hf + 1) * 960], s.sem, [], [s.buf])
                eng = engs[rr["cast"] % 3]
                rr["cast"] += 1
                dst = WI.ap[:, k, hf * 960:(hf + 1) * 960]
                sc = gcol(f"ng{l}", 1, k)
                if eng == "act":
                    ACT(dst, s.ap, AF.Copy, [s.buf, G.buf], [WI.buf], scale=sc)
                else:
                    TS(eng, dst, s.ap, sc, ALU.mult, [s.buf, G.buf], [WI.buf])
        for k in range(8):
            s = wst[rr["cast"] % 2]
            DMA(s.ap[:, 0:512], w_mem[l, k * 128:(k + 1) * 128, :], s.sem, [], [s.buf])
            eng = engs[rr["cast"] % 3]
            rr["cast"] += 1
            dst = WM.ap[:, k, :]
            sc = gcol(f"mng{l}", 1, k)
            if eng == "act":
                ACT(dst, s.ap[:, 0:512], AF.Copy, [s.buf, G.buf], [WM.buf], scale=sc)
            else:
                TS(eng, dst, s.ap[:, 0:512], sc, ALU.mult, [s.buf, G.buf], [WM.buf])

    def x_src(l, s, r0, n):
        t0, _ = segs[s]
        src = x_in[s] if l == 0 else y_out[s]
        g = (t0 + r0) // 512
        return src[r0:r0 + n, :], ([] if l == 0 else [b_Y[g]])

    def emit_group_loads(l, gi):
        tiles = []
        for i in range(4):
            t = xt[rr["xt"] % 4]
            rr["xt"] += 1
            if gi == NG:
                DMA(t.ap, mem[i * 128:(i + 1) * 128, :], t.sem, [], [t.buf])
            else:
                s = seg_of_group(gi)
                r0 = gi * 512 - segs[s][0] + i * 128
                ap, rb = x_src(l, s, r0, 128)
                DMA(t.ap, ap, t.sem, rb, [t.buf])
            tiles.append(t)
        tb = None
        if gi < NG:
            s = seg_of_group(gi)
            p0 = gi * 512 - segs[s][0]
            tb = tabs[rr["tab"] % 2]
            rr["tab"] += 1
            which = [0, 1] if l % 2 == 0 else [0, 1, 2, 3]
            for w in which:
                DMA(tb[w].ap, tabs_d[w, :, p0:p0 + 512], tb[w].sem, [], [tb[w].buf])
        return tiles, tb

    def front_steps(l, gi, tiles):
        XT = xT[rr["xT"] % 2]
        rr["xT"] += 1
        steps = []
        for i, t in enumerate(tiles):
            st = sst[(gi * 4 + i) % 4]
            b = xb[(gi * 4 + i) % 4]

            def s1(t=t, st=st, b=b):
                ACT(b.ap, t.ap, AF.Square, [t.buf], [b.buf, st.buf], accum=st.ap[:, 0:1])
                ACT(st.ap[:, 1:2], st.ap[:, 0:1], AF.Ln, [st.buf, DV.buf], [st.buf], scale=1.0 / D, bias=EPSB)
                ACT(st.ap[:, 2:3], st.ap[:, 1:2], AF.Exp, [st.buf], [st.buf], scale=-0.5)
                TS("dve", b.ap, t.ap, st.ap[:, 2:3], ALU.mult, [t.buf, st.buf], [b.buf])

            def s2(i=i, b=b):
                for c in range(8):
                    TRN(tp_bf[:, c * 128:(c + 1) * 128], b.ap[:, c * 128:(c + 1) * 128], [b.buf], [pP_tp.buf])
                CP("act" if i % 2 == 0 else "dve", XT.ap[:, :, i * 128:(i + 1) * 128],
                   tp_bf.rearrange("p (c t) -> p c t", c=8), [pP_tp.buf], [XT.buf])
            steps.append(s1)
            steps.append(s2)
        return XT, steps

    def fm_store(l, gi, chunk, oc):
        if gi == NG:
            DMA(FMm[chunk, :, :], oc.ap, oc.sem, [oc.buf], [b_FMm])
        else:
            DMA(FM[chunk, :, gi * 512:(gi + 1) * 512], oc.ap, oc.sem, [oc.buf], [b_FM[gi]])

    def next_oc():
        o = oc_sb[rr["oc"] % 4]
        rr["oc"] += 1
        return o

    def proj_group(l, gi, XT, tb, W, jobs, tokjobs, hooks):
        n = len(jobs)
        state = [dict() for _ in range(n)]

        def main(j):
            jb = jobs[j]
            bank = pP_main[rr["main"] % 3]
            rr["main"] += 1
            state[j]["bank"] = bank
            c0 = jb["col"]
            for k in range(8):
                MM(bank.ap, W.ap[:, k, c0:c0 + 128], XT.ap[:, k, :], k == 0, k == 7, [W.buf, XT.buf], [bank.buf])

        NORM = ("hn", "rope1", "ropeA")
        ROPE = ("rope1", "ropeA")

        def T1(j):
            jb = jobs[j]
            st = state[j]
            bank = st["bank"]
            kind = jb["kind"]
            if kind == "plain":
                oc = next_oc()
                CP("act", oc.ap, bank.ap, [bank.buf], [oc.buf])
                fm_store(l, gi, jb["chunk"], oc)
            elif kind == "silu":
                oc = next_oc()
                ACT(oc.ap, bank.ap, AF.Silu, [bank.buf], [oc.buf])
                fm_store(l, gi, jb["chunk"], oc)
            elif kind == "gc":
                u = u_sb[rr["u"] % 4]
                rr["u"] += 1
                st["u"] = u
                CP("act", u.ap, bank.ap, [bank.buf], [u.buf])
            elif kind == "hc":
                u = state[j - 1]["u"]
                oc = next_oc()
                TT("dve", oc.ap, bank.ap, u.ap, ALU.mult, [bank.buf, u.buf], [oc.buf])
                fm_store(l, gi, jb["chunk"], oc)
            else:
                sq = sq_sb[rr["sq"] % 2]
                rr["sq"] += 1
                st["sq"] = sq
                ACT(sq.ap, bank.ap, AF.Square, [bank.buf], [sq.buf])

        def T2(j):
            jb, st = jobs[j], state[j]
            if jb["kind"] not in NORM:
                return
            bank = st["bank"]
            u = u_sb[rr["u"] % 4]
            rr["u"] += 1
            st["u"] = u
            CP("dve", u.ap, bank.ap, [bank.buf, st["sq"].buf], [u.buf])
            st["ms"] = pP_ms[rr["ms"] % 2]
            rr["ms"] += 1
            MM(st["ms"].ap, bdm, st["sq"].ap, True, True, [b_c, st["sq"].buf], [st["ms"].buf])

        def T3(j):
            jb, st = jobs[j], state[j]
            if jb["kind"] not in NORM:
                return
            rs = rs_sb[rr["rs"] % 2]
            rr["rs"] += 1
            st["rs"] = rs
            ACT(rs.ap, st["ms"].ap, AF.Ln, [st["ms"].buf, DV.buf], [rs.buf], bias=EPSB)
            ACT(rs.ap, rs.ap, AF.Exp, [rs.buf], [rs.buf], scale=-0.5)

        def T4(j):
            jb, st = jobs[j], state[j]
            if jb["kind"] not in NORM:
                return
            u, rs = st["u"], st["rs"]
            if jb["kind"] == "hn":
                oc = next_oc()
                STT(oc.ap, u.ap, jb["gain"], rs.ap, ALU.mult, ALU.mult, [u.buf, rs.buf, G.buf], [oc.buf])
                fm_store(l, gi, jb["chunk"], oc)
            else:
                t = t_sb[rr["t"] % 2]
                rr["t"] += 1
                st["t"] = t
                STT(t.ap, u.ap, jb["gain"], rs.ap, ALU.mult, ALU.mult, [u.buf, rs.buf, G.buf], [t.buf])

        def T5(j):
            jb, st = jobs[j], state[j]
            if jb["kind"] not in ROPE:
                return
            t = st["t"]
            rm = rot1 if jb["kind"] == "rope1" else rota
            st["rot"] = pP_rot[rr["rot"] % 2]
            rr["rot"] += 1
            MM(st["rot"].ap, rm, t.ap, True, True, [b_c, t.buf], [st["rot"].buf])
            a = a_sb[rr["a"] % 3]
            rr["a"] += 1
            st["a"] = a
            ct = tb[0] if jb["kind"] == "rope1" else tb[2]
            TT("pool", a.ap, t.ap, ct.ap, ALU.mult, [t.buf, ct.buf], [a.buf])

        def T6(j):
            jb, st = jobs[j], state[j]
            if jb["kind"] not in ROPE:
                return
            b = b_sb[rr["b"] % 2]
            rr["b"] += 1
            st["b"] = b
            sn = tb[1] if jb["kind"] == "rope1" else tb[3]
            TT("dve", b.ap, st["rot"].ap, sn.ap, ALU.mult, [st["rot"].buf, sn.buf], [b.buf])

        def T7(j):
            jb, st = jobs[j], state[j]
            if jb["kind"] not in ROPE:
                return
            oc = next_oc()
            TT("pool", oc.ap, st["a"].ap, st["b"].ap, ALU.add, [st["a"].buf, st["b"].buf], [oc.buf])
            fm_store(l, gi, jb["chunk"], oc)

        stages = [T1, T2, T3, T4, T5, T6, T7]
        NST = len(stages)
        nh = len(hooks)
        done_h = set()
        if n >= 20:
            hook_steps = [2, 11, 4, 13, 6, 15, 8, 17]
        else:
            hook_steps = [0, 1, 0, 1, 1, 2, 1, 2]
        for step in range(n + NST):
            if step < n:
                main(step)
            if nh:
                for hi_, hstep in enumerate(hook_steps):
                    if hstep == step and hi_ < nh:
                        hooks[hi_]()
                        done_h.add(hi_)
            for d in range(NST, 0, -1):
                if 0 <= step - d < n:
                    stages[d - 1](step - d)
        for hi_ in range(nh):
            if hi_ not in done_h:
                hooks[hi_]()

        for i in range(4):
            vs = vst[rr["vst"] % 2]
            rr["vst"] += 1
            nslots = 0
            for tj_i, tj in enumerate(tokjobs):
                bank = pP_main[rr["main"] % 3]
                rr["main"] += 1
                nco = tj["ncols"]
                for k in range(8):
                    MM(bank.ap[:, 0:nco], XT.ap[:, k, i * 128:(i + 1) * 128], W.ap[:, k, tj["col"]:tj["col"] + nco],
                       k == 0, k == 7, [W.buf, XT.buf], [bank.buf])
                nh = tj["nhead"]
                wd = nco // nh
                dst = vs.ap[:, tj["slot0"]:tj["slot0"] + nh, 0:wd]
                srcv = bank.ap[:, 0:nco].rearrange("p (h d) -> p h d", h=nh)
                CP("dve" if tj_i % 2 == 0 else "act", dst, srcv, [bank.buf], [vs.buf])
                nslots = max(nslots, tj["slot0"] + nh)
            if gi == NG:
                DMA(VMm[:, :, i, :].rearrange("c p d -> p c d"), vs.ap[:, 0:4, :], vs.sem, [vs.buf], [b_VMm])
            else:
                j = gi * 4 + i
                DMA(VM[0:nslots, :, j, :].rearrange("c p d -> p c d"), vs.ap[:, 0:nslots, :], vs.sem,
                    [vs.buf], [b_VM[gi]])

    def phase_P(l):
        even = (l % 2 == 0)
        load_cast_weights(l)
        for v in vst:
            MEMSET("pool", v.ap, 1.0, [v.buf])
        if even:
            e = l // 2
            jobs = []
            for c in range(4):
                jobs.append(dict(col=c * 128, kind="plain", chunk=c))
            for c in range(4):
                jobs.append(dict(col=512 + c * 128, kind="gc", chunk=None))
                jobs.append(dict(col=1024 + c * 128, kind="hc", chunk=4 + c))
            for c in range(4):
                jobs.append(dict(col=1536 + c * 128, kind="rope1", chunk=12 + c, gain=gcol(f"sqg{e}")))
            jobs.append(dict(col=2048, kind="rope1", chunk=16, gain=gcol(f"skg{e}")))
            for c in range(2):
                jobs.append(dict(col=2304 + c * 128, kind="hn", chunk=18 + c, gain=gcol(f"mqg{l}")))
            for c in range(10):
                jobs.append(dict(col=2560 + c * 128, kind="silu", chunk=20 + c))
            tokjobs = [dict(col=2176, ncols=128, slot0=0, nhead=2)]
        else:
            o = l // 2
            jobs = []
            for c in range(4):
                jobs.append(dict(col=c * 128, kind="ropeA", chunk=c, gain=gcol(f"aqg{o}")))
            jobs.append(dict(col=512, kind="ropeA", chunk=4, gain=gcol(f"akg{o}")))
            for c in range(4):
                jobs.append(dict(col=768 + c * 128, kind="rope1", chunk=6 + c, gain=gcol(f"dqg{o}")))
            for c in range(4):
                jobs.append(dict(col=1280 + c * 128, kind="rope1", chunk=10 + c, gain=gcol(f"dkg{o}")))
            for c in range(2):
                jobs.append(dict(col=2304 + c * 128, kind="hn", chunk=18 + c, gain=gcol(f"mqg{l}")))
            for c in range(10):
                jobs.append(dict(col=2560 + c * 128, kind="silu", chunk=20 + c))
            tokjobs = [dict(col=640, ncols=128, slot0=0, nhead=2), dict(col=1792, ncols=512, slot0=2, nhead=4)]
        heavy = [j for j in jobs if j["kind"] in ("rope1", "ropeA")]
        light = []
        i_ = 0
        while i_ < len(jobs):
            j = jobs[i_]
            if j["kind"] in ("rope1", "ropeA"):
                i_ += 1
                continue
            if j["kind"] == "gc":
                light.append([j, jobs[i_ + 1]])
                i_ += 2
            else:
                light.append([j])
                i_ += 1
        merged = []
        hi_, li_ = 0, 0
        while hi_ < len(heavy) or li_ < len(light):
            if hi_ < len(heavy):
                merged.append(heavy[hi_])
                hi_ += 1
            take = 1 if len(heavy) - hi_ >= len(light) - li_ else 2
            for _ in range(take):
                if li_ < len(light):
                    merged.extend(light[li_])
                    li_ += 1
        assert len(merged) == len(jobs)
        jobs = merged
        memjobs = [dict(col=c * 128, kind="hn", chunk=c, gain=gcol(f"mkg{l}")) for c in range(2)]
        memtok = [dict(col=256, ncols=256, slot0=0, nhead=4)]

        order = [NG] + list(range(NG))
        tiles, tb = emit_group_loads(l, order[0])
        XT, steps = front_steps(l, order[0], tiles)
        for st_ in steps:
            st_()
        for idx, gi in enumerate(order):
            cur_XT, cur_tb = XT, tb
            hooks = []
            if idx + 1 < len(order):
                tiles, tb = emit_group_loads(l, order[idx + 1])
                XT, hooks = front_steps(l, order[idx + 1], tiles)
            if gi == NG:
                proj_group(l, gi, cur_XT, cur_tb, WM, memjobs, memtok, hooks)
                for v in vst:
                    MEMSET("pool", v.ap, 1.0, [v.buf])
            else:
                proj_group(l, gi, cur_XT, cur_tb, WI, jobs, tokjobs, hooks)

    aA = Arena("A")
    NJ = SMAX // 128
    KTa = aA.take("KTa", [128, SMAX], BF16)
    KTb = aA.take("KTb", [128, SMAX], BF16)
    VA = [aA.take(f"VA{i}", [128, NJ, 128], BF16) for i in range(2)]
    Qs = [aA.take(f"Qs{i}", [128, 2, 512], BF16) for i in range(2)]
    pt = [aA.take(f"pt{i}", [128, 512], BF16) for i in range(8)]
    rz = [aA.take(f"rz{i}", [128, 512], F32) for i in range(2)]
    zs = [aA.take(f"zs{i}", [128, 512], F32) for i in range(2)]
    ys = [aA.take(f"ys{i}", [128, 512], BF16) for i in range(4)]
    ysw = [aA.take(f"ysw{i}", [64, 4, 512], BF16) for i in range(2)]
    d_t = [aA.take(f"dt{i}", [128, 512], F32) for i in range(2)]
    d_o = aA.take("do", [128, 512], F32)
    d_sq = aA.take("dsq", [128, 512], BF16)
    d_ln = aA.take("dln", [128, 512], F32)
    d_rs = aA.take("drs", [128, 512], F32)
    accD = [aA.take(f"accD{i}", [128, 1024], F32) for i in range(2)]
    accP = [aA.take(f"accP{i}", [128, 1024], F32) for i in range(2)]
    zsD = [aA.take(f"zsD{i}", [128, 512], F32) for i in range(2)]
    zsP = [aA.take(f"zsP{i}", [128, 512], F32) for i in range(2)]
    ESR = aA.take("ESR", [128, 4, 256], F32)

    pA_sc = [banks[0], banks[1], banks[2]]
    pA_O = [banks[3], banks[4]]
    pA_acc = [banks[5], banks[5]]
    pA_Zb = banks[6]
    pA_ms = banks[7]
    scw = pA_sc
    ra = {"sc": 0, "pt": 0, "O": 0, "ys": 0, "Qs": 0, "rz": 0, "ysw": 0, "acc": 0}

    def load_KT(src_a, src_b, n, rb):
        MEMSET("pool", KTa.ap[64:128, 0:n], 0.0, [KTa.buf])
        MEMSET("pool", KTb.ap[0:64, 0:n], 0.0, [KTb.buf])
        DMA(KTa.ap[0:64, 0:n], src_a, KTa.sem, rb, [KTa.buf])
        DMA(KTb.ap[64:128, 0:n], src_b, KTb.sem, rb, [KTb.buf])

    def dense_units(maps, nkt, qts, mode, l, recip):
        units = [(qi, mi, j) for qi in range(len(qts)) for mi in range(len(maps)) for j in range(nkt)]
        L = 2
        qtile = {}
        cur = {}
        pending = []

        def front(u):
            qi, mi, j = units[u]
            m = maps[mi]
            if mi == 0 and j == 0:
                q = Qs[ra["Qs"] % 2]
                ra["Qs"] += 1
                qts[qi]["load"](q)
                qtile[qi] = q
            q = qtile[qi]
            sc = pA_sc[ra["sc"] % 3]
            ra["sc"] += 1
            pp = pt[ra["pt"] % 8]
            ra["pt"] += 1
            kt = KTa if m["half"] == 0 else KTb
            MM(sc.ap, kt.ap[:, j * 128:(j + 1) * 128], q.ap[:, m["qc"], :], True, True, [kt.buf, q.buf], [sc.buf])
            ACT(pp.ap[:, 0:512], sc.ap, AF.Exp, [sc.buf], [pp.buf], scale=0.125)
            cur[u] = pp

        def back(u):
            qi, mi, j = units[u]
            m = maps[mi]
            pp = cur.pop(u)
            if j == 0:
                m["_O"] = pA_O[ra["O"] % 2]
                ra["O"] += 1
                if mode == "diff":
                    k = ra["acc"] % 2
                    ra["acc"] += 1
                    m["_acc"] = {"dve": pA_acc[k], "pool": accP[k]}
                    m["_zs"] = zsD[k]
                    m["_init"] = {"dve": False, "pool": False, "pe": False}
            O = m["_O"]
            va = VA[m["va"]]
            MM(O.ap, va.ap[:, j, :], pp.ap[:, 0:512], j == 0, j == nkt - 1, [va.buf, pp.buf], [O.buf])
            if mode == "diff":
                r8 = j % 8
                eng = "pe" if r8 == 7 else ("pool" if r8 in (1, 3, 5) else "dve")
                if eng == "pe":
                    MM(pA_Zb.ap, ones, pp.ap[:, 0:512], not m["_init"]["pe"], False, [b_c, pp.buf], [pA_Zb.buf])
                    m["_init"]["pe"] = True
                else:
                    acc = m["_acc"][eng]
                    aap = acc.ap[:, 0:512]
                    if not m["_init"][eng]:
                        CP(eng, aap, pp.ap[:, 0:512], [pp.buf], [acc.buf])
                        m["_init"][eng] = True
                    else:
                        TT(eng, aap, aap, pp.ap[:, 0:512], ALU.add, [acc.buf, pp.buf], [acc.buf])
            if j == nkt - 1:
                if mode == "aug":
                    epi_aug(qi, mi)
                else:
                    epi_diff(qi, mi)

        def epi_aug(qi, mi):
            m = maps[mi]
            O = m["_O"]
            tok0 = qts[qi]["tok0"]
            g = tok0 // 512
            r = rz[ra["rz"] % 2]
            ra["rz"] += 1
            y = ys[ra["ys"] % 4]
            ra["ys"] += 1
            if recip == "dve":
                RECIP(r.ap[64:128, :], O.ap[64:128, :], [O.buf], [r.buf])
            else:
                ACT(r.ap[64:128, :], O.ap[64:128, :], AF.Ln, [O.buf], [r.buf])
                ACT(r.ap[64:128, :], r.ap[64:128, :], AF.Exp, [r.buf], [r.buf], scale=-1.0)
            TT("dve", y.ap[0:64, :], O.ap[0:64, :], r.ap[64:128, :], ALU.mult, [O.buf, r.buf], [y.buf])
            ch, hf = m["out"]
            DMA(YT[ch, hf * 64:(hf + 1) * 64, tok0:tok0 + 512], y.ap[0:64, :], y.sem, [y.buf], [b_YT[g]])

        def epi_diff(qi, mi):
            m = maps[mi]
            O = m["_O"]
            acc = m["_acc"]
            sD = m["_zs"]
            tok0 = qts[qi]["tok0"]
            g = tok0 // 512
            o = l // 2
            t = d_t[mi]
            r = rz[ra["rz"] % 2]
            ra["rz"] += 1
            has_pool = m["_init"]["pool"]
            has_pe = m["_init"]["pe"]
            CP("dve", sD.ap, acc["dve"].ap[:, 0:512], [acc["dve"].buf], [sD.buf])

            def st1():
                MM(pA_Zb.ap, ones_f, sD.ap, not has_pe, not has_pool, [cmat_f.buf, sD.buf], [pA_Zb.buf])
                if has_pool:
                    MM(pA_Zb.ap, ones_f, acc["pool"].ap[:, 0:512], False, True, [cmat_f.buf, acc["pool"].buf],
                       [pA_Zb.buf])
                ACT(r.ap, pA_Zb.ap, AF.Ln, [pA_Zb.buf], [r.buf])
                ACT(r.ap, r.ap, AF.Exp, [r.buf], [r.buf], scale=-1.0)
                TT("dve", t.ap, O.ap, r.ap, ALU.mult, [O.buf, r.buf], [t.buf])

            def st2():
                nl = DV.ap[:, dvc[f"nlam{o}"]:dvc[f"nlam{o}"] + 1]
                STT(d_o.ap, d_t[1].ap, nl, d_t[0].ap, ALU.mult, ALU.add, [d_t[0].buf, d_t[1].buf, DV.buf], [d_o.buf])
                ACT(d_sq.ap, d_o.ap, AF.Square, [d_o.buf], [d_sq.buf])

            def st3():
                gsv = DV.ap[:, dvc[f"gs{o}"]:dvc[f"gs{o}"] + 1]
                MM(pA_ms.ap, onesm, d_sq.ap, True, True, [b_c, d_sq.buf], [pA_ms.buf])
                ACT(d_ln.ap, pA_ms.ap, AF.Ln, [pA_ms.buf, DV.buf], [d_ln.buf], bias=EPSB)
                ACT(d_rs.ap, d_ln.ap, AF.Exp, [d_ln.buf], [d_rs.buf], scale=-0.5)
                y = ys[ra["ys"] % 4]
                ra["ys"] += 1
                STT(y.ap, d_o.ap, gsv, d_rs.ap, ALU.mult, ALU.mult, [d_o.buf, d_rs.buf, DV.buf], [y.buf])
                DMA(YT[m["out"], :, tok0:tok0 + 512], y.ap, y.sem, [y.buf], [b_YT[g]])

            pending.append([3, st1])
            if mi == 1:
                pending.append([6, st2])
                pending.append([9, st3])

        def tick(flush=False):
            keep = []
            for it in pending:
                it[0] -= 1
                if it[0] <= 0 or flush:
                    it[1]()
                else:
                    keep.append(it)
            pending[:] = keep

        n = len(units)
        for i in range(n + L):
            if i < n:
                front(i)
            if i - L >= 0:
                back(i - L)
            tick()
        while pending:
            tick(flush=True)

    def q_loader(chunks, tok0, rb):
        def load(q):
            for ci, ch in enumerate(chunks):
                DMA(q.ap[:, ci, :], FM[ch, :, tok0:tok0 + 512], q.sem, rb, [q.buf])
        return load

    def phase_A_mem(l):
        for s in range(2):
            t0, ln = segs[s]
            gl = groups_of_seg(s)
            for pr in range(2):
                load_KT(FMm[pr, 0:64, s * 256:(s + 1) * 256], FMm[pr, 64:128, s * 256:(s + 1) * 256], 256, [b_FMm])
                for hh in range(2):
                    DMA(VA[hh].ap[:, 0:2, :], VMm[2 * pr + hh, :, 2 * s:2 * s + 2, :], VA[hh].sem,
                        [b_VMm], [VA[hh].buf])
                maps = [dict(qc=0, half=0, va=0, out=(8 + pr, 0)), dict(qc=0, half=1, va=1, out=(8 + pr, 1))]
                qts = [dict(load=q_loader([18 + pr], g * 512, [b_FM[g]]), tok0=g * 512) for g in gl]
                dense_units(maps, 2, qts, "aug", l, "act")

    def phase_A_odd(l):
        for s in range(2):
            t0, ln = segs[s]
            gl = groups_of_seg(s)
            rbs = [b_FM[g] for g in gl]
            rvs = [b_VM[g] for g in gl]
            nkt = ln // 128
            j0 = t0 // 128
            for kv in range(2):
                srck = FM[4, kv * 64:(kv + 1) * 64, t0:t0 + ln]
                load_KT(srck, srck, ln, rbs)
                DMA(VA[0].ap[:, 0:nkt, :], VM[kv, :, j0:j0 + nkt, :], VA[0].sem, rvs, [VA[0].buf])
                maps = []
                for hh in range(4):
                    h = 4 * kv + hh
                    maps.append(dict(qc=hh // 2, half=hh % 2, va=0, out=(h // 2, h % 2)))
                qts = [dict(load=q_loader([2 * kv, 2 * kv + 1], g * 512, [b_FM[g]]), tok0=g * 512) for g in gl]
                dense_units(maps, nkt, qts, "aug", l, "dve")
            for h in range(4):
                load_KT(FM[10 + h, 0:64, t0:t0 + ln], FM[10 + h, 64:128, t0:t0 + ln], ln, rbs)
                DMA(VA[0].ap[:, 0:nkt, :], VM[2 + h, :, j0:j0 + nkt, :], VA[0].sem, rvs, [VA[0].buf])
                maps = [dict(qc=0, half=0, va=0, out=4 + h), dict(qc=0, half=1, va=0, out=4 + h)]
                qts = [dict(load=q_loader([6 + h], g * 512, [b_FM[g]]), tok0=g * 512) for g in gl]
                dense_units(maps, nkt, qts, "diff", l, "act")

    def phase_A_even(l):
        e = l // 2
        esc = dvc[f"esink{e}"]
        maskP = masks.ap[:, 0:256]
        maskN = masks.ap[:, 256:512]
        MEMSET("dve", ESR.ap, 0.0, [ESR.buf])
        for kv in range(2):
            for hf in range(2):
                for ci in range(2):
                    h = 4 * kv + 2 * ci + hf
                    dst = ESR.ap[:, kv * 2 + hf, ci * 128:(ci + 1) * 128]
                    TS("dve", dst, dst, DV.ap[:, esc + h:esc + h + 1], ALU.add, [ESR.buf, DV.buf], [ESR.buf])
        for s in range(2):
            t0, ln = segs[s]
            gl = groups_of_seg(s)
            rbs = [b_FM[g] for g in gl]
            rvs = [b_VM[g] for g in gl]
            nb = ln // 128
            j0 = t0 // 128
            for kv in range(2):
                srck = FM[16, kv * 64:(kv + 1) * 64, t0:t0 + ln]
                load_KT(srck, srck, ln, rbs)
                DMA(VA[0].ap[:, 0:nb, :], VM[kv, :, j0:j0 + nb, :], VA[0].sem, rvs, [VA[0].buf])
                units = []
                for g in gl:
                    for bi in range(4):
                        i = (g * 512 - t0) // 128 + bi
                        for hf in range(2):
                            kbs = [jj for jj in (i - 1, i, i + 1) if 0 <= jj < nb]
                            for jj in kbs:
                                units.append((g, bi, i, hf, jj, jj == kbs[0], jj == kbs[-1]))
                qtile = {}
                cur = {}
                ost = {}

                def front(u, kv=kv):
                    g, bi, i, hf, jj, first, last = units[u]
                    if bi == 0 and hf == 0 and first:
                        q = Qs[ra["Qs"] % 2]
                        ra["Qs"] += 1
                        for ci in range(2):
                            DMA(q.ap[:, ci, :], FM[12 + 2 * kv + ci, :, g * 512:(g + 1) * 512], q.sem,
                                [b_FM[g]], [q.buf])
                        qtile[g] = q
                    q = qtile[g]
                    sc = pA_sc[ra["sc"] % 3]
                    ra["sc"] += 1
                    p = pt[ra["pt"] % 8]
                    ra["pt"] += 1
                    kt = KTa if hf == 0 else KTb
                    scv = sc.ap[:, 0:256]
                    rhs = q.ap[:, :, bi * 128:(bi + 1) * 128]
                    MM(scv, kt.ap[:, jj * 128:(jj + 1) * 128], rhs, True, jj == i, [kt.buf, q.buf], [sc.buf])
                    if jj != i:
                        MM(scv, ident, maskP if jj < i else maskN, False, True, [b_c, masks.buf], [sc.buf])
                    ACT(p.ap[:, 0:256], scv, AF.Exp, [sc.buf], [p.buf], scale=0.125)
                    cur[u] = p

                def back(u, kv=kv):
                    g, bi, i, hf, jj, first, last = units[u]
                    p = cur.pop(u)
                    if first:
                        ost[(i, hf)] = pA_O[ra["O"] % 2]
                        ra["O"] += 1
                    O = ost[(i, hf)]
                    Ov = O.ap[:, 0:256]
                    MM(Ov, VA[0].ap[:, jj, :], p.ap[:, 0:256], first, last, [VA[0].buf, p.buf], [O.buf])
                    if last:
                        if bi == 0 and hf == 0:
                            ost["ysw"] = ysw[ra["ysw"] % 2]
                            ra["ysw"] += 1
                        yw = ost["ysw"]
                        z = zs[ra["rz"] % 2]
                        ra["rz"] += 1
                        TT("dve", z.ap[64:128, 0:256], O.ap[64:128, 0:256], ESR.ap[64:128, kv * 2 + hf, :], ALU.add,
                           [O.buf, ESR.buf], [z.buf])
                        ACT(z.ap[64:128, 0:256], z.ap[64:128, 0:256], AF.Ln, [z.buf], [z.buf])
                        ACT(z.ap[64:128, 0:256], z.ap[64:128, 0:256], AF.Exp, [z.buf], [z.buf], scale=-1.0)
                        TT("dve", yw.ap[:, 2 * hf:2 * hf + 2, bi * 128:(bi + 1) * 128],
                           O.ap[0:64, 0:256].rearrange("p (c q) -> p c q", c=2),
                           z.ap[64:128, 0:256].rearrange("p (c q) -> p c q", c=2), ALU.mult,
                           [O.buf, z.buf], [yw.buf])
                        if bi == 3 and hf == 1:
                            for hf2 in range(2):
                                for ci in range(2):
                                    h = 4 * kv + 2 * ci + hf2
                                    DMA(YT[4 + h // 2, (h % 2) * 64:(h % 2) * 64 + 64, g * 512:(g + 1) * 512],
                                        yw.ap[:, 2 * hf2 + ci, :], yw.sem, [yw.buf], [b_YT[g]])

                n = len(units)
                L = 2
                for ii in range(n + L):
                    if ii < n:
                        front(ii)
                    if ii - L >= 0:
                        back(ii - L)

    aC = Arena("C")
    WO = aC.take("WO", [128, 10, D], BF16)
    wso = [aC.take(f"wso{i}", [128, D], F32) for i in range(2)]
    Yb = [aC.take(f"Yb{i}", [128, 10, 512], BF16) for i in range(2)]
    SZ = [aC.take(f"SZ{i}", [128, 10, 512], BF16) for i in range(2)]
    YZ = [aC.take(f"YZ{i}", [128, 10, 512], BF16) for i in range(2)]
    INb = [aC.take(f"IN{i}", [128, 4, 520], BF16) for i in range(2)]
    GBb = [aC.take(f"GB{i}", [128, 4, 512], BF16) for i in range(2)]
    ctmp = [aC.take(f"ct{i}", [128, 512], F32) for i in range(2)]
    xc = [aC.take(f"xc{i}", [128, D], F32) for i in range(8)]
    xn = [aC.take(f"xn{i}", [128, D], F32) for i in range(3)]
    rc = {"w": 0, "xc": 0, "xn": 0, "bank": 0, "ct": 0}

    def phase_C(l):
        even = (l % 2 == 0)
        engs = ["act", "dve", "act"]
        for c in range(10):
            s = wso[rc["w"] % 2]
            DMA(s.ap, w_out[l, c * 128:(c + 1) * 128, :], s.sem, [], [s.buf])
            CP(engs[rc["w"] % 3], WO.ap[:, c, :], s.ap, [s.buf], [WO.buf])
            rc["w"] += 1

        def loads(g):
            s = seg_of_group(g)
            t0, ln = segs[s]
            k = g % 2
            Y, Z = Yb[k], SZ[k]
            c0 = 4 if even else 0
            DMA(Y.ap[:, c0:10, :], YT[c0:10, :, g * 512:(g + 1) * 512].rearrange("c p t -> p c t"), Y.sem,
                [b_YT[g]], [Y.buf])
            DMA(Z.ap, FM[20:30, :, g * 512:(g + 1) * 512].rearrange("c p t -> p c t"), Z.sem, [b_FM[g]], [Z.buf])
            if even:
                I_, Gb = INb[k], GBb[k]
                lo = g * 512 - 1
                hi = g * 512 + 513
                a0 = 0
                if lo < t0:
                    MEMSET("pool", I_.ap[:, :, 0:1], 0.0, [I_.buf])
                    lo += 1
                    a0 = 1
                a1 = 514
                if hi > t0 + ln:
                    MEMSET("pool", I_.ap[:, :, 513:514], 0.0, [I_.buf])
                    hi -= 1
                    a1 = 513
                rb = [b_FM[g]]
                if g - 1 >= 0:
                    rb.append(b_FM[g - 1])
                if g + 1 < NG:
                    rb.append(b_FM[g + 1])
                DMA(I_.ap[:, :, a0:a1], FM[4:8, :, lo:hi].rearrange("c p t -> p c t"), I_.sem, rb, [I_.buf])
                DMA(Gb.ap, FM[0:4, :, g * 512:(g + 1) * 512].rearrange("c p t -> p c t"), Gb.sem, [b_FM[g]], [Gb.buf])
            xs = []
            for i in range(4):
                t = xc[rc["xc"] % 8]
                rc["xc"] += 1
                r0 = g * 512 - t0 + i * 128
                ap, rb = x_src(l, s, r0, 128)
                DMA(t.ap, ap, t.sem, rb, [t.buf])
                xs.append(t)
            return xs

        def prep(g, xs):
            k = g % 2
            Y, Z, yz = Yb[k], SZ[k], YZ[k]
            if even:
                e = l // 2
                I_, Gb = INb[k], GBb[k]
                cw = gcols[f"cw{e}"]
                for c in range(4):
                    ct = ctmp[rc["ct"] % 2]
                    rc["ct"] += 1
                    w0 = G.ap[:, cw + 0 * 4 + c:cw + 0 * 4 + c + 1]
                    w1 = G.ap[:, cw + 1 * 4 + c:cw + 1 * 4 + c + 1]
                    w2 = G.ap[:, cw + 2 * 4 + c:cw + 2 * 4 + c + 1]
                    TS("dve", ct.ap, I_.ap[:, c, 0:512], w0, ALU.mult, [I_.buf, G.buf], [ct.buf])
                    STT(ct.ap, I_.ap[:, c, 1:513], w1, ct.ap, ALU.mult, ALU.add, [I_.buf, G.buf, ct.buf], [ct.buf])
                    STT(ct.ap, I_.ap[:, c, 2:514], w2, ct.ap, ALU.mult, ALU.add, [I_.buf, G.buf, ct.buf], [ct.buf])
                    TT("dve", Y.ap[:, c, :], ct.ap, Gb.ap[:, c, :], ALU.mult, [ct.buf, Gb.buf], [Y.buf])
            for hf in range(2):
                TT("dve", yz.ap[:, 5 * hf:5 * hf + 5, :], Y.ap[:, 5 * hf:5 * hf + 5, :], Z.ap[:, 5 * hf:5 * hf + 5, :],
                   ALU.mult, [Y.buf, Z.buf], [yz.buf])

        def outproj(g, xs):
            s = seg_of_group(g)
            t0, ln = segs[s]
            yz = YZ[g % 2]
            for i in range(4):
                xo = xn[rc["xn"] % 3]
                rc["xn"] += 1
                for nh in range(2):
                    bank = banks[rc["bank"] % 4]
                    rc["bank"] += 1
                    for c in range(10):
                        MM(bank.ap, yz.ap[:, c, i * 128:(i + 1) * 128], WO.ap[:, c, nh * 512:(nh + 1) * 512],
                           c == 0, c == 9, [yz.buf, WO.buf], [bank.buf])
                    TT("dve", xo.ap[:, nh * 512:(nh + 1) * 512], bank.ap, xs[i].ap[:, nh * 512:(nh + 1) * 512], ALU.add,
                       [bank.buf, xs[i].buf], [xo.buf])
                r0 = g * 512 - t0 + i * 128
                DMA(y_out[s][r0:r0 + 128, :], xo.ap, xo.sem, [xo.buf], [b_Y[g]])

        xs_of = {}
        xs_of[0] = loads(0)
        if NG > 1:
            xs_of[1] = loads(1)
        prep(0, xs_of[0])
        for g in range(NG):
            if g + 1 < NG:
                prep(g + 1, xs_of[g + 1])
            outproj(g, xs_of.pop(g))
            if g + 2 < NG:
                xs_of[g + 2] = loads(g + 2)

    P.marks = []

    def mark(name):
        P.marks.append((name, dict(P.count)))

    for l in range(DEPTH):
        P.barrier()
        mark(f"L{l} P")
        phase_P(l)
        P.barrier()
        mark(f"L{l} Amem")
        phase_A_mem(l)
        mark(f"L{l} A")
        if l % 2 == 0:
            phase_A_even(l)
        else:
            phase_A_odd(l)
        P.barrier()
        mark(f"L{l} C")
        phase_C(l)
    mark("end")
    P.wait_all("sp", b_Y)
    P.barrier()

    with nc.Block() as block:
        P.replay(block)
    return nc, P


_CONST_CACHE = {}


def _consts(smax):
    if smax not in _CONST_CACHE:
        _CONST_CACHE[smax] = (_const_mats(), _masks(), _rope_tables(smax))
    return _CONST_CACHE[smax]


def run(x_prompt, x_sample, mem_prompt, mem_sample, norm_g, w_in, w_out, mem_norm_g, w_mem_kv, mem_qk_g,
        conv_w, swa_qk_g, swa_sink, ax_qk_g, diff_qk_g, diff_lambda, diff_subln_g, depth=None, n_cores=8):
    f = lambda a: np.ascontiguousarray(np.asarray(a), dtype=np.float32)
    x_prompt, x_sample, mem_prompt, mem_sample = f(x_prompt), f(x_sample), f(mem_prompt), f(mem_sample)
    B, SP, _ = x_prompt.shape
    SS = x_sample.shape[1]
    DEPTH = int(depth if depth is not None else np.asarray(norm_g).shape[0])
    nc, P = build_program(SP, SS, DEPTH)
    cm, mk, tabs = _consts(max(SP, SS))
    gp, _ = _gpack(DEPTH, f(norm_g), f(mem_norm_g), f(mem_qk_g), f(conv_w), f(swa_qk_g), f(swa_sink),
                   f(ax_qk_g), f(diff_qk_g), f(diff_lambda), f(diff_subln_g))
    w_in, w_out, w_mem = f(w_in)[:DEPTH], f(w_out)[:DEPTH], f(w_mem_kv)[:DEPTH]
    in_maps = []
    for c in range(n_cores):
        in_maps.append({
            "x_p": x_prompt[c], "x_s": x_sample[c],
            "mem": np.ascontiguousarray(np.concatenate([mem_prompt[c], mem_sample[c]], axis=0)),
            "w_in": w_in, "w_out": w_out, "w_mem": w_mem, "gpack": gp, "cmat": cm, "masks": mk, "tabs": tabs,
        })
    res = run_bass_kernel_spmd(nc, in_maps, core_ids=list(range(n_cores)))
    yp = np.stack([np.asarray(r["y_p"], dtype=np.float32) for r in res.results])
    ysm = np.stack([np.asarray(r["y_s"], dtype=np.float32) for r in res.results])
    return yp, ysm


def kernel(x_prompt, x_sample, mem_prompt, mem_sample, norm_g, w_in, w_out, mem_norm_g, w_mem_kv, mem_qk_g,
           conv_w, swa_qk_g, swa_sink, ax_qk_g, diff_qk_g, diff_lambda, diff_subln_g):
    return run(x_prompt, x_sample, mem_prompt, mem_sample, norm_g, w_in, w_out, mem_norm_g, w_mem_kv, mem_qk_g,
               conv_w, swa_qk_g, swa_sink, ax_qk_g, diff_qk_g, diff_lambda, diff_subln_g)
```

```python
import math
import numpy as np
import concourse.bass as bass
import concourse.mybir as mybir
from concourse.bass_utils import run_bass_kernel_spmd

F32 = mybir.dt.float32
BF16 = mybir.dt.bfloat16
U8 = mybir.dt.uint8
AF = mybir.ActivationFunctionType
ALU = mybir.AluOpType

D = 1024
INW = 3840
MIXW = 1280
NMEM = 256
EPS = 1e-6
NEGM = -30000.0
ENGS = ("pe", "act", "dve", "pool", "sp")


class Buf:
    __slots__ = ("name", "w", "r")

    def __init__(self, name):
        self.name = name
        self.w = {}
        self.r = {}


class DmaSem:
    __slots__ = ("handle", "count", "name")

    def __init__(self, handle, name):
        self.handle = handle
        self.count = 0
        self.name = name


class Plan:
    def __init__(self, nc):
        self.nc = nc
        self.items = {e: [] for e in ENGS}
        self.count = {e: 0 for e in ENGS}
        self.waited = {e: {} for e in ENGS}
        self.sems = {}
        for e in ENGS:
            if e != "sp":
                self.sems[e] = nc.alloc_semaphore(name=f"s_{e}")
        self.dmasems = []
        self.n_inst = 0

    def dma_sem(self, name):
        s = DmaSem(self.nc.alloc_semaphore(name=f"d_{name}"), name)
        self.dmasems.append(s)
        return s

    def _deps(self, eng, reads, writes):
        deps = {}
        for b in reads:
            for k, v in b.w.items():
                if deps.get(k, 0) < v:
                    deps[k] = v
        for b in writes:
            for k, v in b.w.items():
                if deps.get(k, 0) < v:
                    deps[k] = v
            for k, v in b.r.items():
                if deps.get(k, 0) < v:
                    deps[k] = v
        waits = []
        wd = self.waited[eng]
        for k, v in deps.items():
            if k == "pe" and eng == "pe":
                continue
            if wd.get(k, 0) >= v:
                continue
            wd[k] = v
            waits.append((k, v))
        return waits

    def _mark(self, key, val, reads, writes):
        for b in reads:
            if b.r.get(key, 0) < val:
                b.r[key] = val
        for b in writes:
            if b.r:
                b.w = {}
                b.r = {}
            b.w[key] = val

    def op(self, eng, fn, reads=(), writes=()):
        waits = self._deps(eng, reads, writes)
        self.count[eng] += 1
        self.items[eng].append((waits, fn, (eng, 1)))
        self._mark(eng, self.count[eng], reads, writes)
        self.n_inst += 1

    def dma(self, q, fn, sem, reads=(), writes=()):
        waits = self._deps(q, reads, writes)
        sem.count += 16
        self.items[q].append((waits, fn, (sem, 16)))
        self._mark(sem, sem.count, reads, writes)
        self.n_inst += 1

    def wait_all(self, eng, bufs):
        waits = self._deps(eng, bufs, ())
        self.items[eng].append((waits, None, None))

    def barrier(self):
        snap = [(e, self.count[e]) for e in ENGS if e != "sp"]
        snap += [(s, s.count) for s in self.dmasems]
        for e in ENGS:
            waits = []
            wd = self.waited[e]
            for k, v in snap:
                if v == 0 or k == e:
                    continue
                if wd.get(k, 0) >= v:
                    continue
                wd[k] = v
                waits.append((k, v))
            if waits:
                self.items[e].append((waits, None, None))

    def _semh(self, k):
        return k.handle if isinstance(k, DmaSem) else self.sems[k]

    def replay(self, block):
        plan = self

        def run(engine, name):
            for waits, fn, inc in plan.items[name]:
                for k, v in waits:
                    engine.wait_ge(plan._semh(k), v)
                if fn is None:
                    continue
                inst = fn(engine)
                inst.then_inc(plan._semh(inc[0]), inc[1])

        @block.tensor
        def _(e):
            run(e, "pe")

        @block.scalar
        def _(e):
            run(e, "act")

        @block.vector
        def _(e):
            run(e, "dve")

        @block.gpsimd
        def _(e):
            run(e, "pool")

        @block.sync
        def _(e):
            run(e, "sp")


class Tl:
    __slots__ = ("ap", "buf", "_sem", "plan", "name")

    def __init__(self, plan, name, ap):
        self.plan = plan
        self.name = name
        self.ap = ap
        self.buf = Buf(name)
        self._sem = None

    @property
    def sem(self):
        if self._sem is None:
            self._sem = self.plan.dma_sem(self.name)
        return self._sem


def _const_mats():
    ident = np.eye(128, dtype=np.float32)
    k = np.arange(128)
    bd = (k[:, None] // 64 == k[None, :] // 64).astype(np.float32) / 64.0
    onesm = np.full((128, 128), 1.0 / 128.0, np.float32)
    ones = np.ones((128, 128), np.float32)
    rot1 = np.zeros((128, 128), np.float32)
    rota = np.zeros((128, 128), np.float32)
    for m in range(128):
        if (m % 64) < 32:
            rot1[m + 32, m] = -1.0
        else:
            rot1[m - 32, m] = 1.0
        if (m % 32) < 16:
            rota[m + 16, m] = -1.0
        else:
            rota[m - 16, m] = 1.0
    return np.concatenate([ident, bd, onesm, ones, rot1, rota], axis=1)


def _masks():
    ki = np.arange(128)[:, None]
    qi = np.arange(128)[None, :]
    mp = np.where(qi <= ki, 0.0, NEGM).astype(np.float32)
    mn = np.where(ki <= qi, 0.0, NEGM).astype(np.float32)
    return np.concatenate([mp, mp, mn, mn], axis=1)


def _rope_tables(smax):
    theta = np.float32(10000.0)
    pos = np.arange(smax)
    f = np.arange(128) % 64
    inv1 = (theta ** (-(np.arange(0, 64, 2, dtype=np.float32)) / np.float32(64))).astype(np.float32)
    ang1 = (pos.astype(np.float32)[None, :] * inv1[f % 32][:, None]).astype(np.float32)
    inva = (theta ** (-(np.arange(0, 32, 2, dtype=np.float32)) / np.float32(32))).astype(np.float32)
    prow = (pos // 64).astype(np.float32)
    pcol = (pos % 64).astype(np.float32)
    fa = f % 32
    pa = np.where((f < 32)[:, None], prow[None, :], pcol[None, :]).astype(np.float32)
    anga = (pa * inva[fa % 16][:, None]).astype(np.float32)
    tabs = np.stack([np.cos(ang1.astype(np.float64)), np.sin(ang1.astype(np.float64)),
                     np.cos(anga.astype(np.float64)), np.sin(anga.astype(np.float64))]).astype(np.float32)
    return tabs


def _gpack(depth, norm_g, mem_norm_g, mem_qk_g, conv_w, swa_qk_g, swa_sink, ax_qk_g, diff_qk_g,
           diff_lambda, diff_subln_g):
    cols = {}
    parts = []
    pos = [0]

    def add(name, a):
        a = np.ascontiguousarray(a, dtype=np.float32)
        assert a.shape[0] == 128
        cols[name] = pos[0]
        parts.append(a)
        pos[0] += a.shape[1]

    dup = lambda v: np.tile(v, 2)[:, None]
    for l in range(depth):
        add(f"ng{l}", norm_g[l].reshape(8, 128).T)
        add(f"mng{l}", mem_norm_g[l].reshape(8, 128).T)
        add(f"mqg{l}", dup(mem_qk_g[l, 0]))
        add(f"mkg{l}", dup(mem_qk_g[l, 1]))
        if l % 2 == 0:
            e = l // 2
            add(f"sqg{e}", dup(swa_qk_g[e, 0]))
            add(f"skg{e}", dup(swa_qk_g[e, 1]))
            add(f"cw{e}", conv_w[e].reshape(3, 4, 128).transpose(2, 0, 1).reshape(128, 12))
            add(f"sink{e}", np.broadcast_to(swa_sink[e][None, :], (128, 8)))
        else:
            o = l // 2
            add(f"aqg{o}", dup(ax_qk_g[o, 0]))
            add(f"akg{o}", dup(ax_qk_g[o, 1]))
            add(f"dqg{o}", dup(diff_qk_g[o, 0]))
            add(f"dkg{o}", dup(diff_qk_g[o, 1]))
            add(f"sub{o}", diff_subln_g[o][:, None])
            add(f"lam{o}", np.broadcast_to(diff_lambda[o].reshape(1, 256), (128, 256)))
    return np.concatenate(parts, axis=1), cols


def _gpack_cols(depth):
    z = lambda *s: np.zeros(s, np.float32)
    _, cols = _gpack(depth, z(depth, D), z(depth, D), z(depth, 2, 64), z((depth + 1) // 2, 3, 512),
                     z((depth + 1) // 2, 2, 64), z((depth + 1) // 2, 8), z(max(depth // 2, 1), 2, 64),
                     z(max(depth // 2, 1), 2, 64), z(max(depth // 2, 1), 4, 64), z(max(depth // 2, 1), 128))
    g, _ = _gpack(depth, z(depth, D), z(depth, D), z(depth, 2, 64), z((depth + 1) // 2, 3, 512),
                  z((depth + 1) // 2, 2, 64), z((depth + 1) // 2, 8), z(max(depth // 2, 1), 2, 64),
                  z(max(depth // 2, 1), 2, 64), z(max(depth // 2, 1), 4, 64), z(max(depth // 2, 1), 128))
    return cols, g.shape[1]


def build_program(SP, SS, DEPTH, debug_layers=None):
    nc = bass.Bass("TRN2", target_bir_lowering=False)
    T = SP + SS
    NG = T // 512
    SMAX = max(SP, SS)
    segs = [(0, SP), (SP, SS)]
    gcols, NGC = _gpack_cols(DEPTH)

    x_p = nc.dram_tensor("x_p", [SP, D], F32, kind="ExternalInput").ap()
    x_s = nc.dram_tensor("x_s", [SS, D], F32, kind="ExternalInput").ap()
    mem = nc.dram_tensor("mem", [2 * NMEM, D], F32, kind="ExternalInput").ap()
    w_in = nc.dram_tensor("w_in", [DEPTH, D, INW], F32, kind="ExternalInput").ap()
    w_out = nc.dram_tensor("w_out", [DEPTH, MIXW, D], F32, kind="ExternalInput").ap()
    w_mem = nc.dram_tensor("w_mem", [DEPTH, D, 512], F32, kind="ExternalInput").ap()
    gpack_d = nc.dram_tensor("gpack", [128, NGC], F32, kind="ExternalInput").ap()
    cmat_d = nc.dram_tensor("cmat", [128, 768], F32, kind="ExternalInput").ap()
    masks_d = nc.dram_tensor("masks", [128, 512], F32, kind="ExternalInput").ap()
    tabs_d = nc.dram_tensor("tabs", [4, 128, SMAX], F32, kind="ExternalInput").ap()
    y_p = nc.dram_tensor("y_p", [SP, D], F32, kind="ExternalOutput").ap()
    y_s = nc.dram_tensor("y_s", [SS, D], F32, kind="ExternalOutput").ap()
    FM = nc.dram_tensor("FM", [30, 128, T], BF16, kind="Internal").ap()
    FMm = nc.dram_tensor("FMm", [2, 128, 512], BF16, kind="Internal").ap()
    VM = nc.dram_tensor("VM", [6, 128, T // 128, 128], BF16, kind="Internal").ap()
    VMm = nc.dram_tensor("VMm", [4, 128, 4, 128], BF16, kind="Internal").ap()
    YT = nc.dram_tensor("YT", [10, 128, T], BF16, kind="Internal").ap()

    x_in = [x_p, x_s]
    y_out = [y_p, y_s]

    P = Plan(nc)

    b_FM = [Buf(f"FM{g}") for g in range(NG)]
    b_VM = [Buf(f"VM{g}") for g in range(NG)]
    b_YT = [Buf(f"YT{g}") for g in range(NG)]
    b_Y = [Buf(f"Y{g}") for g in range(NG)]
    b_FMm = Buf("FMm")
    b_VMm = Buf("VMm")

    def seg_of_group(g):
        return 0 if g * 512 < SP else 1

    def groups_of_seg(s):
        t0, ln = segs[s]
        return list(range(t0 // 512, (t0 + ln) // 512))

    def sb(name, shape, dt):
        return Tl(P, name, nc.alloc_sbuf_tensor("sb_" + name, shape, dt).ap())

    G = sb("G", [128, NGC], F32)
    DV = sb("DV", [128, 64], F32)
    cmat_f = sb("cmat_f", [128, 768], F32)
    cmat = sb("cmat", [128, 768], BF16)
    masks_f = sb("masks_f", [128, 512], F32)
    masks = sb("masks", [128, 512], BF16)
    stat = sb("stat", [128, 64], F32)
    ident = cmat.ap[:, 0:128]
    bdm = cmat.ap[:, 128:256]
    onesm = cmat.ap[:, 256:384]
    ones = cmat.ap[:, 384:512]
    rot1 = cmat.ap[:, 512:640]
    rota = cmat.ap[:, 640:768]
    b_c = cmat.buf

    ARENA_BYTES = 172 * 1024
    arena = nc.alloc_sbuf_tensor("arena", [128, ARENA_BYTES], U8).ap()

    class Arena:
        def __init__(self, tag):
            self.off = 0
            self.tag = tag
            self.n = 0

        def take(self, name, shape, dt):
            esz = 4 if dt == F32 else 2
            n = int(np.prod(shape[1:])) * esz
            self.off = (self.off + 63) // 64 * 64
            assert self.off + n <= ARENA_BYTES, (self.tag, name, self.off, n)
            a = arena[0:shape[0], self.off:self.off + n].bitcast(dt)
            self.off += n
            if len(shape) == 3:
                a = a.rearrange("p (a b) -> p a b", a=shape[1])
            elif len(shape) == 4:
                a = a.rearrange("p (a b c) -> p a b c", a=shape[1], b=shape[2])
            return Tl(P, f"{self.tag}_{name}", a)

    psall = nc.alloc_psum_tensor("psall", [128, 4096], F32).ap()
    banks = [Tl(P, f"ps{i}", psall[:, i * 512:(i + 1) * 512]) for i in range(8)]
    ones_f = cmat_f.ap[:, 384:512]

    def MM(out, lhsT, rhs, start, stop, R, W):
        P.op("pe", lambda e: e.matmul(out, lhsT=lhsT, rhs=rhs, start=start, stop=stop), R, W)

    def TRN(out, in_, R, W):
        P.op("pe", lambda e: e.transpose(out, in_, ident), list(R) + [b_c], W)

    def ACT(out, in_, func, R, W, scale=None, bias=None, accum=None):
        kw = {}
        if scale is not None:
            kw["scale"] = scale
        if bias is not None:
            kw["bias"] = bias
        if accum is not None:
            kw["accum_out"] = accum
        P.op("act", lambda e: e.activation(out=out, in_=in_, func=func, **kw), R, W)

    def TT(eng, out, in0, in1, op, R, W):
        P.op(eng, lambda e: e.tensor_tensor(out=out, in0=in0, in1=in1, op=op), R, W)

    def TS(eng, out, in0, s1, op0, R, W, s2=None, op1=None):
        if op1 is None:
            P.op(eng, lambda e: e.tensor_scalar(out=out, in0=in0, scalar1=s1, scalar2=None, op0=op0), R, W)
        else:
            P.op(eng, lambda e: e.tensor_scalar(out=out, in0=in0, scalar1=s1, scalar2=s2, op0=op0, op1=op1), R, W)

    def STT(out, in0, scalar, in1, op0, op1, R, W):
        P.op("dve", lambda e: e.scalar_tensor_tensor(out=out, in0=in0, scalar=scalar, in1=in1, op0=op0, op1=op1), R, W)

    def CP(eng, out, in_, R, W):
        if eng == "act":
            P.op("act", lambda e: e.activation(out=out, in_=in_, func=AF.Copy), R, W)
        else:
            P.op(eng, lambda e: e.tensor_copy(out=out, in_=in_), R, W)

    def RECIP(out, in_, R, W):
        P.op("dve", lambda e: e.reciprocal(out=out, in_=in_), R, W)

    def REDUCE(out, in_, R, W):
        P.op("dve", lambda e: e.tensor_reduce(out=out, in_=in_, axis=mybir.AxisListType.X, op=ALU.add), R, W)

    def MEMSET(eng, ap, val, W):
        P.op(eng, lambda e: e.memset(ap, val), (), W)

    def DMA(out, in_, sem, R, W, q="sp"):
        P.dma(q, lambda e: e.dma_start(out=out, in_=in_), sem, R, W)

    DMA(G.ap, gpack_d, G.sem, [], [G.buf])
    DMA(cmat_f.ap, cmat_d, cmat_f.sem, [], [cmat_f.buf])
    DMA(masks_f.ap, masks_d, masks_f.sem, [], [masks_f.buf])
    CP("dve", cmat.ap, cmat_f.ap, [cmat_f.buf], [cmat.buf])
    CP("dve", masks.ap, masks_f.ap, [masks_f.buf], [masks.buf])
    MEMSET("dve", DV.ap[:, 0:1], EPS, [DV.buf])
    EPSB = DV.ap[:, 0:1]
    dvc = {}
    dvpos = [1]

    def dv_take(name, n=1):
        dvc[name] = dvpos[0]
        dvpos[0] += n
        return DV.ap[:, dvc[name]:dvc[name] + n]

    for l in range(DEPTH):
        if l % 2 == 0:
            e = l // 2
            es = dv_take(f"esink{e}", 8)
            ACT(es, G.ap[:, gcols[f"sink{e}"]:gcols[f"sink{e}"] + 8], AF.Exp, [G.buf], [DV.buf])
        else:
            o = l // 2
            lam_init = 0.8 - 0.6 * math.exp(-0.3 * l)
            lc = gcols[f"lam{o}"]
            tmpv = dv_take(f"lamtmp{o}", 4)
            prod = sb(f"lamprod{o}", [128, 128], F32)
            TT("dve", prod.ap[:, 0:64], G.ap[:, lc:lc + 64], G.ap[:, lc + 64:lc + 128], ALU.mult, [G.buf], [prod.buf])
            TT("dve", prod.ap[:, 64:128], G.ap[:, lc + 128:lc + 192], G.ap[:, lc + 192:lc + 256], ALU.mult,
               [G.buf], [prod.buf])
            REDUCE(tmpv[:, 0:2], prod.ap.rearrange("p (a b) -> p a b", a=2), [prod.buf], [DV.buf])
            ACT(tmpv[:, 2:4], tmpv[:, 0:2], AF.Exp, [DV.buf], [DV.buf])
            nl = dv_take(f"nlam{o}", 1)
            TS("dve", nl, tmpv[:, 3:4], -lam_init, ALU.add, [DV.buf], [DV.buf])
            TT("dve", nl, nl, tmpv[:, 2:3], ALU.subtract, [DV.buf], [DV.buf])
            gsv = dv_take(f"gs{o}", 1)
            TS("dve", gsv, G.ap[:, gcols[f"sub{o}"]:gcols[f"sub{o}"] + 1], 1.0 - lam_init, ALU.mult,
               [G.buf], [DV.buf])

    def gcol(name, n=1, off=0):
        c = gcols[name] + off
        return G.ap[:, c:c + n]

    aP = Arena("P")
    WI = aP.take("WI", [128, 8, INW], BF16)
    WM = aP.take("WM", [128, 8, 512], BF16)
    wst = [aP.take(f"wst{i}", [128, 960], F32) for i in range(2)]
    xt = [aP.take(f"xt{i}", [128, D], F32) for i in range(4)]
    xb = [aP.take(f"xb{i}", [128, D], BF16) for i in range(4)]
    xT = [aP.take(f"xT{i}", [128, 8, 512], BF16) for i in range(2)]
    tabs = [[aP.take(f"tab{s}_{i}", [128, 512], F32) for i in range(4)] for s in range(2)]
    u_sb = [aP.take(f"u{i}", [128, 512], F32) for i in range(4)]
    sq_sb = [aP.take(f"sq{i}", [128, 512], BF16) for i in range(2)]
    rs_sb = [aP.take(f"rs{i}", [128, 512], F32) for i in range(2)]
    t_sb = [aP.take(f"t{i}", [128, 512], BF16) for i in range(2)]
    a_sb = [aP.take(f"a{i}", [128, 512], F32) for i in range(3)]
    b_sb = [aP.take(f"b{i}", [128, 512], F32) for i in range(2)]
    oc_sb = [aP.take(f"oc{i}", [128, 512], BF16) for i in range(4)]
    vst = [aP.take(f"vst{i}", [128, 6, 128], BF16) for i in range(2)]
    sst = [aP.take(f"sst{i}", [128, 4], F32) for i in range(4)]

    pP_main = [banks[0], banks[1], banks[2]]
    pP_ms = [banks[3], banks[4]]
    pP_rot = [banks[5], banks[6]]
    pP_tp = banks[7]
    tp_bf = pP_tp.ap.bitcast(BF16)

    rr = {"cast": 0, "oc": 0, "xt": 0, "xb": 0, "main": 0, "c2": 0, "vst": 0, "ms": 0, "rot": 0, "xT": 0, "tab": 0, "u": 0, "sq": 0, "rs": 0, "t": 0, "a": 0, "b": 0}

    def load_cast_weights(l):
        engs = ["act", "dve", "act"]
        for k in range(8):
            for hf in range(4):
                s = wst[rr["cast"] % 2]
                DMA(s.ap, w_in[l, k * 128:(k + 1) * 128, hf * 960:(hf + 1) * 960], s.sem, [], [s.buf])
                eng = engs[rr["cast"] % 3]
                rr["cast"] += 1
                dst = WI.ap[:, k, hf * 960:(hf + 1) * 960]
                sc = gcol(f"ng{l}", 1, k)
                if eng == "act":
                    ACT(dst, s.ap, AF.Copy, [s.buf, G.buf], [WI.buf], scale=sc)
                else:
                    TS(eng, dst, s.ap, sc, ALU.mult, [s.buf, G.buf], [WI.buf])
        for k in range(8):
            s = wst[rr["cast"] % 2]
            DMA(s.ap[:, 0:512], w_mem[l, k * 128:(k + 1) * 128, :], s.sem, [], [s.buf])
            eng = engs[rr["cast"] % 3]
            rr["cast"] += 1
            dst = WM.ap[:, k, :]
            sc = gcol(f"mng{l}", 1, k)
            if eng == "act":
                ACT(dst, s.ap[:, 0:512], AF.Copy, [s.buf, G.buf], [WM.buf], scale=sc)
            else:
                TS(eng, dst, s.ap[:, 0:512], sc, ALU.mult, [s.buf, G.buf], [WM.buf])

    def x_src(l, s, r0, n):
        t0, _ = segs[s]
        src = x_in[s] if l == 0 else y_out[s]
        g = (t0 + r0) // 512
        return src[r0:r0 + n, :], ([] if l == 0 else [b_Y[g]])

    def emit_group_loads(l, gi):
        tiles = []
        for i in range(4):
            t = xt[rr["xt"] % 4]
            rr["xt"] += 1
            if gi == NG:
                DMA(t.ap, mem[i * 128:(i + 1) * 128, :], t.sem, [], [t.buf])
            else:
                s = seg_of_group(gi)
                r0 = gi * 512 - segs[s][0] + i * 128
                ap, rb = x_src(l, s, r0, 128)
                DMA(t.ap, ap, t.sem, rb, [t.buf])
            tiles.append(t)
        tb = None
        if gi < NG:
            s = seg_of_group(gi)
            p0 = gi * 512 - segs[s][0]
            tb = tabs[rr["tab"] % 2]
            rr["tab"] += 1
            which = [0, 1] if l % 2 == 0 else [0, 1, 2, 3]
            for w in which:
                DMA(tb[w].ap, tabs_d[w, :, p0:p0 + 512], tb[w].sem, [], [tb[w].buf])
        return tiles, tb

    def front_steps(l, gi, tiles):
        XT = xT[rr["xT"] % 2]
        rr["xT"] += 1
        steps = []
        for i, t in enumerate(tiles):
            st = sst[(gi * 4 + i) % 4]
            b = xb[(gi * 4 + i) % 4]

            def s1(t=t, st=st, b=b):
                ACT(b.ap, t.ap, AF.Square, [t.buf], [b.buf, st.buf], accum=st.ap[:, 0:1])
                ACT(st.ap[:, 1:2], st.ap[:, 0:1], AF.Ln, [st.buf, DV.buf], [st.buf], scale=1.0 / D, bias=EPSB)
                ACT(st.ap[:, 2:3], st.ap[:, 1:2], AF.Exp, [st.buf], [st.buf], scale=-0.5)
                TS("dve", b.ap, t.ap, st.ap[:, 2:3], ALU.mult, [t.buf, st.buf], [b.buf])

            def s2(i=i, b=b):
                for c in range(8):
                    TRN(tp_bf[:, c * 128:(c + 1) * 128], b.ap[:, c * 128:(c + 1) * 128], [b.buf], [pP_tp.buf])
                CP("act" if i % 2 == 0 else "dve", XT.ap[:, :, i * 128:(i + 1) * 128],
                   tp_bf.rearrange("p (c t) -> p c t", c=8), [pP_tp.buf], [XT.buf])
            steps.append(s1)
            steps.append(s2)
        return XT, steps

    def fm_store(l, gi, chunk, oc):
        if gi == NG:
            DMA(FMm[chunk, :, :], oc.ap, oc.sem, [oc.buf], [b_FMm])
        else:
            DMA(FM[chunk, :, gi * 512:(gi + 1) * 512], oc.ap, oc.sem, [oc.buf], [b_FM[gi]])

    def next_oc():
        o = oc_sb[rr["oc"] % 4]
        rr["oc"] += 1
        return o

    def proj_group(l, gi, XT, tb, W, jobs, tokjobs, hooks):
        n = len(jobs)
        state = [dict() for _ in range(n)]

        def main(j):
            jb = jobs[j]
            bank = pP_main[rr["main"] % 3]
            rr["main"] += 1
            state[j]["bank"] = bank
            c0 = jb["col"]
            for k in range(8):
                MM(bank.ap, W.ap[:, k, c0:c0 + 128], XT.ap[:, k, :], k == 0, k == 7, [W.buf, XT.buf], [bank.buf])

        NORM = ("hn", "rope1", "ropeA")
        ROPE = ("rope1", "ropeA")

        def T1(j):
            jb = jobs[j]
            st = state[j]
            bank = st["bank"]
            kind = jb["kind"]
            if kind == "plain":
                oc = next_oc()
                CP("act", oc.ap, bank.ap, [bank.buf], [oc.buf])
                fm_store(l, gi, jb["chunk"], oc)
            elif kind == "silu":
                oc = next_oc()
                ACT(oc.ap, bank.ap, AF.Silu, [bank.buf], [oc.buf])
                fm_store(l, gi, jb["chunk"], oc)
            elif kind == "gc":
                u = u_sb[rr["u"] % 4]
                rr["u"] += 1
                st["u"] = u
                CP("act", u.ap, bank.ap, [bank.buf], [u.buf])
            elif kind == "hc":
                u = state[j - 1]["u"]
                oc = next_oc()
                TT("dve", oc.ap, bank.ap, u.ap, ALU.mult, [bank.buf, u.buf], [oc.buf])
                fm_store(l, gi, jb["chunk"], oc)
            else:
                sq = sq_sb[rr["sq"] % 2]
                rr["sq"] += 1
                st["sq"] = sq
                ACT(sq.ap, bank.ap, AF.Square, [bank.buf], [sq.buf])

        def T2(j):
            jb, st = jobs[j], state[j]
            if jb["kind"] not in NORM:
                return
            bank = st["bank"]
            u = u_sb[rr["u"] % 4]
            rr["u"] += 1
            st["u"] = u
            CP("dve", u.ap, bank.ap, [bank.buf, st["sq"].buf], [u.buf])
            st["ms"] = pP_ms[rr["ms"] % 2]
            rr["ms"] += 1
            MM(st["ms"].ap, bdm, st["sq"].ap, True, True, [b_c, st["sq"].buf], [st["ms"].buf])

        def T3(j):
            jb, st = jobs[j], state[j]
            if jb["kind"] not in NORM:
                return
            rs = rs_sb[rr["rs"] % 2]
            rr["rs"] += 1
            st["rs"] = rs
            ACT(rs.ap, st["ms"].ap, AF.Ln, [st["ms"].buf, DV.buf], [rs.buf], bias=EPSB)
            ACT(rs.ap, rs.ap, AF.Exp, [rs.buf], [rs.buf], scale=-0.5)

        def T4(j):
            jb, st = jobs[j], state[j]
            if jb["kind"] not in NORM:
                return
            u, rs = st["u"], st["rs"]
            if jb["kind"] == "hn":
                oc = next_oc()
                STT(oc.ap, u.ap, jb["gain"], rs.ap, ALU.mult, ALU.mult, [u.buf, rs.buf, G.buf], [oc.buf])
                fm_store(l, gi, jb["chunk"], oc)
            else:
                t = t_sb[rr["t"] % 2]
                rr["t"] += 1
                st["t"] = t
                STT(t.ap, u.ap, jb["gain"], rs.ap, ALU.mult, ALU.mult, [u.buf, rs.buf, G.buf], [t.buf])

        def T5(j):
            jb, st = jobs[j], state[j]
            if jb["kind"] not in ROPE:
                return
            t = st["t"]
            rm = rot1 if jb["kind"] == "rope1" else rota
            st["rot"] = pP_rot[rr["rot"] % 2]
            rr["rot"] += 1
            MM(st["rot"].ap, rm, t.ap, True, True, [b_c, t.buf], [st["rot"].buf])
            a = a_sb[rr["a"] % 3]
            rr["a"] += 1
            st["a"] = a
            ct = tb[0] if jb["kind"] == "rope1" else tb[2]
            TT("pool", a.ap, t.ap, ct.ap, ALU.mult, [t.buf, ct.buf], [a.buf])

        def T6(j):
            jb, st = jobs[j], state[j]
            if jb["kind"] not in ROPE:
                return
            b = b_sb[rr["b"] % 2]
            rr["b"] += 1
            st["b"] = b
            sn = tb[1] if jb["kind"] == "rope1" else tb[3]
            TT("dve", b.ap, st["rot"].ap, sn.ap, ALU.mult, [st["rot"].buf, sn.buf], [b.buf])

        def T7(j):
            jb, st = jobs[j], state[j]
            if jb["kind"] not in ROPE:
                return
            oc = next_oc()
            TT("pool", oc.ap, st["a"].ap, st["b"].ap, ALU.add, [st["a"].buf, st["b"].buf], [oc.buf])
            fm_store(l, gi, jb["chunk"], oc)

        stages = [T1, T2, T3, T4, T5, T6, T7]
        NST = len(stages)
        nh = len(hooks)
        done_h = set()
        if n >= 20:
            hook_steps = [2, 11, 4, 13, 6, 15, 8, 17]
        else:
            hook_steps = [0, 1, 0, 1, 1, 2, 1, 2]
        for step in range(n + NST):
            if step < n:
                main(step)
            if nh:
                for hi_, hstep in enumerate(hook_steps):
                    if hstep == step and hi_ < nh:
                        hooks[hi_]()
                        done_h.add(hi_)
            for d in range(NST, 0, -1):
                if 0 <= step - d < n:
                    stages[d - 1](step - d)
        for hi_ in range(nh):
            if hi_ not in done_h:
                hooks[hi_]()

        for i in range(4):
            vs = vst[rr["vst"] % 2]
            rr["vst"] += 1
            nslots = 0
            for tj_i, tj in enumerate(tokjobs):
                bank = pP_main[rr["main"] % 3]
                rr["main"] += 1
                nco = tj["ncols"]
                for k in range(8):
                    MM(bank.ap[:, 0:nco], XT.ap[:, k, i * 128:(i + 1) * 128], W.ap[:, k, tj["col"]:tj["col"] + nco],
                       k == 0, k == 7, [W.buf, XT.buf], [bank.buf])
                nh = tj["nhead"]
                wd = nco // nh
                dst = vs.ap[:, tj["slot0"]:tj["slot0"] + nh, 0:wd]
                srcv = bank.ap[:, 0:nco].rearrange("p (h d) -> p h d", h=nh)
                CP("dve" if tj_i % 2 == 0 else "act", dst, srcv, [bank.buf], [vs.buf])
                nslots = max(nslots, tj["slot0"] + nh)
            if gi == NG:
                DMA(VMm[:, :, i, :].rearrange("c p d -> p c d"), vs.ap[:, 0:4, :], vs.sem, [vs.buf], [b_VMm])
            else:
                j = gi * 4 + i
                DMA(VM[0:nslots, :, j, :].rearrange("c p d -> p c d"), vs.ap[:, 0:nslots, :], vs.sem,
                    [vs.buf], [b_VM[gi]])

    def phase_P(l):
        even = (l % 2 == 0)
        load_cast_weights(l)
        for v in vst:
            MEMSET("pool", v.ap, 1.0, [v.buf])
        if even:
            e = l // 2
            jobs = []
            for c in range(4):
                jobs.append(dict(col=c * 128, kind="plain", chunk=c))
            for c in range(4):
                jobs.append(dict(col=512 + c * 128, kind="gc", chunk=None))
                jobs.append(dict(col=1024 + c * 128, kind="hc", chunk=4 + c))
            for c in range(4):
                jobs.append(dict(col=1536 + c * 128, kind="rope1", chunk=12 + c, gain=gcol(f"sqg{e}")))
            jobs.append(dict(col=2048, kind="rope1", chunk=16, gain=gcol(f"skg{e}")))
            for c in range(2):
                jobs.append(dict(col=2304 + c * 128, kind="hn", chunk=18 + c, gain=gcol(f"mqg{l}")))
            for c in range(10):
                jobs.append(dict(col=2560 + c * 128, kind="plain", chunk=20 + c))
            tokjobs = [dict(col=2176, ncols=128, slot0=0, nhead=2)]
        else:
            o = l // 2
            jobs = []
            for c in range(4):
                jobs.append(dict(col=c * 128, kind="ropeA", chunk=c, gain=gcol(f"aqg{o}")))
            jobs.append(dict(col=512, kind="ropeA", chunk=4, gain=gcol(f"akg{o}")))
            for c in range(4):
                jobs.append(dict(col=768 + c * 128, kind="rope1", chunk=6 + c, gain=gcol(f"dqg{o}")))
            for c in range(4):
                jobs.append(dict(col=1280 + c * 128, kind="rope1", chunk=10 + c, gain=gcol(f"dkg{o}")))
            for c in range(2):
                jobs.append(dict(col=2304 + c * 128, kind="hn", chunk=18 + c, gain=gcol(f"mqg{l}")))
            for c in range(10):
                jobs.append(dict(col=2560 + c * 128, kind="plain", chunk=20 + c))
            tokjobs = [dict(col=640, ncols=128, slot0=0, nhead=2), dict(col=1792, ncols=512, slot0=2, nhead=4)]
        heavy = [j for j in jobs if j["kind"] in ("rope1", "ropeA")]
        light = []
        i_ = 0
        while i_ < len(jobs):
            j = jobs[i_]
            if j["kind"] in ("rope1", "ropeA"):
                i_ += 1
                continue
            if j["kind"] == "gc":
                light.append([j, jobs[i_ + 1]])
                i_ += 2
            else:
                light.append([j])
                i_ += 1
        merged = []
        hi_, li_ = 0, 0
        while hi_ < len(heavy) or li_ < len(light):
            if hi_ < len(heavy):
                merged.append(heavy[hi_])
                hi_ += 1
            take = 1 if len(heavy) - hi_ >= len(light) - li_ else 2
            for _ in range(take):
                if li_ < len(light):
                    merged.extend(light[li_])
                    li_ += 1
        assert len(merged) == len(jobs)
        jobs = merged
        memjobs = [dict(col=c * 128, kind="hn", chunk=c, gain=gcol(f"mkg{l}")) for c in range(2)]
        memtok = [dict(col=256, ncols=256, slot0=0, nhead=4)]

        order = [NG] + list(range(NG))
        tiles, tb = emit_group_loads(l, order[0])
        XT, steps = front_steps(l, order[0], tiles)
        for st_ in steps:
            st_()
        for idx, gi in enumerate(order):
            cur_XT, cur_tb = XT, tb
            hooks = []
            if idx + 1 < len(order):
                tiles, tb = emit_group_loads(l, order[idx + 1])
                XT, hooks = front_steps(l, order[idx + 1], tiles)
            if gi == NG:
                proj_group(l, gi, cur_XT, cur_tb, WM, memjobs, memtok, hooks)
                for v in vst:
                    MEMSET("pool", v.ap, 1.0, [v.buf])
            else:
                proj_group(l, gi, cur_XT, cur_tb, WI, jobs, tokjobs, hooks)

    aA = Arena("A")
    NJ = SMAX // 128
    KTa = aA.take("KTa", [128, SMAX], BF16)
    KTb = aA.take("KTb", [128, SMAX], BF16)
    VA = [aA.take(f"VA{i}", [128, NJ, 128], BF16) for i in range(2)]
    Qs = [aA.take(f"Qs{i}", [128, 2, 512], BF16) for i in range(2)]
    pt = [aA.take(f"pt{i}", [128, 512], BF16) for i in range(8)]
    rz = [aA.take(f"rz{i}", [128, 512], F32) for i in range(2)]
    zs = [aA.take(f"zs{i}", [128, 512], F32) for i in range(2)]
    ys = [aA.take(f"ys{i}", [128, 512], BF16) for i in range(4)]
    ysw = [aA.take(f"ysw{i}", [64, 4, 512], BF16) for i in range(2)]
    d_t = [aA.take(f"dt{i}", [128, 512], F32) for i in range(2)]
    d_o = aA.take("do", [128, 512], F32)
    d_sq = aA.take("dsq", [128, 512], BF16)
    d_ln = aA.take("dln", [128, 512], F32)
    d_rs = aA.take("drs", [128, 512], F32)
    accD = [aA.take(f"accD{i}", [128, 1024], F32) for i in range(2)]
    accP = [aA.take(f"accP{i}", [128, 1024], F32) for i in range(2)]
    zsD = [aA.take(f"zsD{i}", [128, 512], F32) for i in range(2)]
    zsP = [aA.take(f"zsP{i}", [128, 512], F32) for i in range(2)]
    ESR = aA.take("ESR", [128, 4, 256], F32)

    pA_sc = [banks[0], banks[1], banks[2]]
    pA_O = [banks[3], banks[4]]
    pA_acc = [banks[5], banks[5]]
    pA_Zb = banks[6]
    pA_ms = banks[7]
    scw = pA_sc
    ra = {"sc": 0, "pt": 0, "O": 0, "ys": 0, "Qs": 0, "rz": 0, "ysw": 0, "acc": 0}

    def load_KT(src_a, src_b, n, rb):
        MEMSET("pool", KTa.ap[64:128, 0:n], 0.0, [KTa.buf])
        MEMSET("pool", KTb.ap[0:64, 0:n], 0.0, [KTb.buf])
        DMA(KTa.ap[0:64, 0:n], src_a, KTa.sem, rb, [KTa.buf])
        DMA(KTb.ap[64:128, 0:n], src_b, KTb.sem, rb, [KTb.buf])

    def dense_units(maps, nkt, qts, mode, l, recip):
        units = [(qi, mi, j) for qi in range(len(qts)) for mi in range(len(maps)) for j in range(nkt)]
        L = 2
        qtile = {}
        cur = {}
        pending = []

        def front(u):
            qi, mi, j = units[u]
            m = maps[mi]
            if mi == 0 and j == 0:
                q = Qs[ra["Qs"] % 2]
                ra["Qs"] += 1
                qts[qi]["load"](q)
                qtile[qi] = q
            q = qtile[qi]
            sc = pA_sc[ra["sc"] % 3]
            ra["sc"] += 1
            pp = pt[ra["pt"] % 8]
            ra["pt"] += 1
            kt = KTa if m["half"] == 0 else KTb
            MM(sc.ap, kt.ap[:, j * 128:(j + 1) * 128], q.ap[:, m["qc"], :], True, True, [kt.buf, q.buf], [sc.buf])
            ACT(pp.ap[:, 0:512], sc.ap, AF.Exp, [sc.buf], [pp.buf], scale=0.125)
            cur[u] = pp

        def back(u):
            qi, mi, j = units[u]
            m = maps[mi]
            pp = cur.pop(u)
            if j == 0:
                m["_O"] = pA_O[ra["O"] % 2]
                ra["O"] += 1
                if mode == "diff":
                    k = ra["acc"] % 2
                    ra["acc"] += 1
                    m["_acc"] = {"dve": pA_acc[k], "pool": accP[k]}
                    m["_zs"] = zsD[k]
                    m["_init"] = {"dve": False, "pool": False, "pe": False}
            O = m["_O"]
            va = VA[m["va"]]
            MM(O.ap, va.ap[:, j, :], pp.ap[:, 0:512], j == 0, j == nkt - 1, [va.buf, pp.buf], [O.buf])
            if mode == "diff":
                r8 = j % 8
                eng = "pe" if r8 == 7 else ("pool" if r8 in (1, 3, 5) else "dve")
                if eng == "pe":
                    MM(pA_Zb.ap, ones, pp.ap[:, 0:512], not m["_init"]["pe"], False, [b_c, pp.buf], [pA_Zb.buf])
                    m["_init"]["pe"] = True
                else:
                    acc = m["_acc"][eng]
                    aap = acc.ap[:, 0:512]
                    if not m["_init"][eng]:
                        CP(eng, aap, pp.ap[:, 0:512], [pp.buf], [acc.buf])
                        m["_init"][eng] = True
                    else:
                        TT(eng, aap, aap, pp.ap[:, 0:512], ALU.add, [acc.buf, pp.buf], [acc.buf])
            if j == nkt - 1:
                if mode == "aug":
                    epi_aug(qi, mi)
                else:
                    epi_diff(qi, mi)

        def epi_aug(qi, mi):
            m = maps[mi]
            O = m["_O"]
            tok0 = qts[qi]["tok0"]
            g = tok0 // 512
            r = rz[ra["rz"] % 2]
            ra["rz"] += 1
            y = ys[ra["ys"] % 4]
            ra["ys"] += 1
            if recip == "dve":
                RECIP(r.ap[64:128, :], O.ap[64:128, :], [O.buf], [r.buf])
            else:
                ACT(r.ap[64:128, :], O.ap[64:128, :], AF.Ln, [O.buf], [r.buf])
                ACT(r.ap[64:128, :], r.ap[64:128, :], AF.Exp, [r.buf], [r.buf], scale=-1.0)
            TT("dve", y.ap[0:64, :], O.ap[0:64, :], r.ap[64:128, :], ALU.mult, [O.buf, r.buf], [y.buf])
            ch, hf = m["out"]
            DMA(YT[ch, hf * 64:(hf + 1) * 64, tok0:tok0 + 512], y.ap[0:64, :], y.sem, [y.buf], [b_YT[g]])

        def epi_diff(qi, mi):
            m = maps[mi]
            O = m["_O"]
            acc = m["_acc"]
            sD = m["_zs"]
            tok0 = qts[qi]["tok0"]
            g = tok0 // 512
            o = l // 2
            t = d_t[mi]
            r = rz[ra["rz"] % 2]
            ra["rz"] += 1
            has_pool = m["_init"]["pool"]
            has_pe = m["_init"]["pe"]
            CP("dve", sD.ap, acc["dve"].ap[:, 0:512], [acc["dve"].buf], [sD.buf])

            def st1():
                MM(pA_Zb.ap, ones_f, sD.ap, not has_pe, not has_pool, [cmat_f.buf, sD.buf], [pA_Zb.buf])
                if has_pool:
                    MM(pA_Zb.ap, ones_f, acc["pool"].ap[:, 0:512], False, True, [cmat_f.buf, acc["pool"].buf],
                       [pA_Zb.buf])
                ACT(r.ap, pA_Zb.ap, AF.Ln, [pA_Zb.buf], [r.buf])
                ACT(r.ap, r.ap, AF.Exp, [r.buf], [r.buf], scale=-1.0)
                TT("dve", t.ap, O.ap, r.ap, ALU.mult, [O.buf, r.buf], [t.buf])

            def st2():
                nl = DV.ap[:, dvc[f"nlam{o}"]:dvc[f"nlam{o}"] + 1]
                STT(d_o.ap, d_t[1].ap, nl, d_t[0].ap, ALU.mult, ALU.add, [d_t[0].buf, d_t[1].buf, DV.buf], [d_o.buf])
                ACT(d_sq.ap, d_o.ap, AF.Square, [d_o.buf], [d_sq.buf])

            def st3():
                gsv = DV.ap[:, dvc[f"gs{o}"]:dvc[f"gs{o}"] + 1]
                MM(pA_ms.ap, onesm, d_sq.ap, True, True, [b_c, d_sq.buf], [pA_ms.buf])
                ACT(d_ln.ap, pA_ms.ap, AF.Ln, [pA_ms.buf, DV.buf], [d_ln.buf], bias=EPSB)
                ACT(d_rs.ap, d_ln.ap, AF.Exp, [d_ln.buf], [d_rs.buf], scale=-0.5)
                y = ys[ra["ys"] % 4]
                ra["ys"] += 1
                STT(y.ap, d_o.ap, gsv, d_rs.ap, ALU.mult, ALU.mult, [d_o.buf, d_rs.buf, DV.buf], [y.buf])
                DMA(YT[m["out"], :, tok0:tok0 + 512], y.ap, y.sem, [y.buf], [b_YT[g]])

            pending.append([3, st1])
            if mi == 1:
                pending.append([6, st2])
                pending.append([9, st3])

        def tick(flush=False):
            keep = []
            for it in pending:
                it[0] -= 1
                if it[0] <= 0 or flush:
                    it[1]()
                else:
                    keep.append(it)
            pending[:] = keep

        n = len(units)
        for i in range(n + L):
            if i < n:
                front(i)
            if i - L >= 0:
                back(i - L)
            tick()
        while pending:
            tick(flush=True)

    def q_loader(chunks, tok0, rb):
        def load(q):
            for ci, ch in enumerate(chunks):
                DMA(q.ap[:, ci, :], FM[ch, :, tok0:tok0 + 512], q.sem, rb, [q.buf])
        return load

    def phase_A_mem(l):
        for s in range(2):
            t0, ln = segs[s]
            gl = groups_of_seg(s)
            for pr in range(2):
                load_KT(FMm[pr, 0:64, s * 256:(s + 1) * 256], FMm[pr, 64:128, s * 256:(s + 1) * 256], 256, [b_FMm])
                for hh in range(2):
                    DMA(VA[hh].ap[:, 0:2, :], VMm[2 * pr + hh, :, 2 * s:2 * s + 2, :], VA[hh].sem,
                        [b_VMm], [VA[hh].buf])
                maps = [dict(qc=0, half=0, va=0, out=(8 + pr, 0)), dict(qc=0, half=1, va=1, out=(8 + pr, 1))]
                qts = [dict(load=q_loader([18 + pr], g * 512, [b_FM[g]]), tok0=g * 512) for g in gl]
                dense_units(maps, 2, qts, "aug", l, "act")

    def phase_A_odd(l):
        for s in range(2):
            t0, ln = segs[s]
            gl = groups_of_seg(s)
            rbs = [b_FM[g] for g in gl]
            rvs = [b_VM[g] for g in gl]
            nkt = ln // 128
            j0 = t0 // 128
            for kv in range(2):
                srck = FM[4, kv * 64:(kv + 1) * 64, t0:t0 + ln]
                load_KT(srck, srck, ln, rbs)
                DMA(VA[0].ap[:, 0:nkt, :], VM[kv, :, j0:j0 + nkt, :], VA[0].sem, rvs, [VA[0].buf])
                maps = []
                for hh in range(4):
                    h = 4 * kv + hh
                    maps.append(dict(qc=hh // 2, half=hh % 2, va=0, out=(h // 2, h % 2)))
                qts = [dict(load=q_loader([2 * kv, 2 * kv + 1], g * 512, [b_FM[g]]), tok0=g * 512) for g in gl]
                dense_units(maps, nkt, qts, "aug", l, "dve")
            for h in range(4):
                load_KT(FM[10 + h, 0:64, t0:t0 + ln], FM[10 + h, 64:128, t0:t0 + ln], ln, rbs)
                DMA(VA[0].ap[:, 0:nkt, :], VM[2 + h, :, j0:j0 + nkt, :], VA[0].sem, rvs, [VA[0].buf])
                maps = [dict(qc=0, half=0, va=0, out=4 + h), dict(qc=0, half=1, va=0, out=4 + h)]
                qts = [dict(load=q_loader([6 + h], g * 512, [b_FM[g]]), tok0=g * 512) for g in gl]
                dense_units(maps, nkt, qts, "diff", l, "act")

    def phase_A_even(l):
        e = l // 2
        esc = dvc[f"esink{e}"]
        maskP = masks.ap[:, 0:256]
        maskN = masks.ap[:, 256:512]
        MEMSET("dve", ESR.ap, 0.0, [ESR.buf])
        for kv in range(2):
            for hf in range(2):
                for ci in range(2):
                    h = 4 * kv + 2 * ci + hf
                    dst = ESR.ap[:, kv * 2 + hf, ci * 128:(ci + 1) * 128]
                    TS("dve", dst, dst, DV.ap[:, esc + h:esc + h + 1], ALU.add, [ESR.buf, DV.buf], [ESR.buf])
        for s in range(2):
            t0, ln = segs[s]
            gl = groups_of_seg(s)
            rbs = [b_FM[g] for g in gl]
            rvs = [b_VM[g] for g in gl]
            nb = ln // 128
            j0 = t0 // 128
            for kv in range(2):
                srck = FM[16, kv * 64:(kv + 1) * 64, t0:t0 + ln]
                load_KT(srck, srck, ln, rbs)
                DMA(VA[0].ap[:, 0:nb, :], VM[kv, :, j0:j0 + nb, :], VA[0].sem, rvs, [VA[0].buf])
                units = []
                for g in gl:
                    for bi in range(4):
                        i = (g * 512 - t0) // 128 + bi
                        for hf in range(2):
                            kbs = [jj for jj in (i - 1, i, i + 1) if 0 <= jj < nb]
                            for jj in kbs:
                                units.append((g, bi, i, hf, jj, jj == kbs[0], jj == kbs[-1]))
                qtile = {}
                cur = {}
                ost = {}

                def front(u, kv=kv):
                    g, bi, i, hf, jj, first, last = units[u]
                    if bi == 0 and hf == 0 and first:
                        q = Qs[ra["Qs"] % 2]
                        ra["Qs"] += 1
                        for ci in range(2):
                            DMA(q.ap[:, ci, :], FM[12 + 2 * kv + ci, :, g * 512:(g + 1) * 512], q.sem,
                                [b_FM[g]], [q.buf])
                        qtile[g] = q
                    q = qtile[g]
                    sc = pA_sc[ra["sc"] % 3]
                    ra["sc"] += 1
                    p = pt[ra["pt"] % 8]
                    ra["pt"] += 1
                    kt = KTa if hf == 0 else KTb
                    scv = sc.ap[:, 0:256]
                    rhs = q.ap[:, :, bi * 128:(bi + 1) * 128]
                    MM(scv, kt.ap[:, jj * 128:(jj + 1) * 128], rhs, True, jj == i, [kt.buf, q.buf], [sc.buf])
                    if jj != i:
                        MM(scv, ident, maskP if jj < i else maskN, False, True, [b_c, masks.buf], [sc.buf])
                    ACT(p.ap[:, 0:256], scv, AF.Exp, [sc.buf], [p.buf], scale=0.125)
                    cur[u] = p

                def back(u, kv=kv):
                    g, bi, i, hf, jj, first, last = units[u]
                    p = cur.pop(u)
                    if first:
                        ost[(i, hf)] = pA_O[ra["O"] % 2]
                        ra["O"] += 1
                    O = ost[(i, hf)]
                    Ov = O.ap[:, 0:256]
                    MM(Ov, VA[0].ap[:, jj, :], p.ap[:, 0:256], first, last, [VA[0].buf, p.buf], [O.buf])
                    if last:
                        if bi == 0 and hf == 0:
                            ost["ysw"] = ysw[ra["ysw"] % 2]
                            ra["ysw"] += 1
                        yw = ost["ysw"]
                        z = zs[ra["rz"] % 2]
                        ra["rz"] += 1
                        TT("dve", z.ap[64:128, 0:256], O.ap[64:128, 0:256], ESR.ap[64:128, kv * 2 + hf, :], ALU.add,
                           [O.buf, ESR.buf], [z.buf])
                        ACT(z.ap[64:128, 0:256], z.ap[64:128, 0:256], AF.Ln, [z.buf], [z.buf])
                        ACT(z.ap[64:128, 0:256], z.ap[64:128, 0:256], AF.Exp, [z.buf], [z.buf], scale=-1.0)
                        TT("dve", yw.ap[:, 2 * hf:2 * hf + 2, bi * 128:(bi + 1) * 128],
                           O.ap[0:64, 0:256].rearrange("p (c q) -> p c q", c=2),
                           z.ap[64:128, 0:256].rearrange("p (c q) -> p c q", c=2), ALU.mult,
                           [O.buf, z.buf], [yw.buf])
                        if bi == 3 and hf == 1:
                            for hf2 in range(2):
                                for ci in range(2):
                                    h = 4 * kv + 2 * ci + hf2
                                    DMA(YT[4 + h // 2, (h % 2) * 64:(h % 2) * 64 + 64, g * 512:(g + 1) * 512],
                                        yw.ap[:, 2 * hf2 + ci, :], yw.sem, [yw.buf], [b_YT[g]])

                n = len(units)
                L = 2
                for ii in range(n + L):
                    if ii < n:
                        front(ii)
                    if ii - L >= 0:
                        back(ii - L)

    aC = Arena("C")
    WO = aC.take("WO", [128, 10, D], BF16)
    wso = [aC.take(f"wso{i}", [128, D], F32) for i in range(2)]
    Yb = [aC.take(f"Yb{i}", [128, 10, 512], BF16) for i in range(2)]
    SZ = [aC.take(f"SZ{i}", [128, 10, 512], BF16) for i in range(2)]
    YZ = [aC.take(f"YZ{i}", [128, 10, 512], BF16) for i in range(2)]
    INb = [aC.take(f"IN{i}", [128, 4, 520], BF16) for i in range(2)]
    GBb = [aC.take(f"GB{i}", [128, 4, 512], BF16) for i in range(2)]
    ctmp = [aC.take(f"ct{i}", [128, 512], F32) for i in range(2)]
    xc = [aC.take(f"xc{i}", [128, D], F32) for i in range(8)]
    xn = [aC.take(f"xn{i}", [128, D], F32) for i in range(3)]
    rc = {"w": 0, "xc": 0, "xn": 0, "bank": 0, "ct": 0}

    def phase_C(l):
        even = (l % 2 == 0)
        engs = ["act", "dve", "act"]
        for c in range(10):
            s = wso[rc["w"] % 2]
            DMA(s.ap, w_out[l, c * 128:(c + 1) * 128, :], s.sem, [], [s.buf])
            CP(engs[rc["w"] % 3], WO.ap[:, c, :], s.ap, [s.buf], [WO.buf])
            rc["w"] += 1

        def loads(g):
            s = seg_of_group(g)
            t0, ln = segs[s]
            k = g % 2
            Y, Z = Yb[k], SZ[k]
            c0 = 4 if even else 0
            DMA(Y.ap[:, c0:10, :], YT[c0:10, :, g * 512:(g + 1) * 512].rearrange("c p t -> p c t"), Y.sem,
                [b_YT[g]], [Y.buf])
            DMA(Z.ap, FM[20:30, :, g * 512:(g + 1) * 512].rearrange("c p t -> p c t"), Z.sem, [b_FM[g]], [Z.buf])
            if even:
                I_, Gb = INb[k], GBb[k]
                lo = g * 512 - 1
                hi = g * 512 + 513
                a0 = 0
                if lo < t0:
                    MEMSET("pool", I_.ap[:, :, 0:1], 0.0, [I_.buf])
                    lo += 1
                    a0 = 1
                a1 = 514
                if hi > t0 + ln:
                    MEMSET("pool", I_.ap[:, :, 513:514], 0.0, [I_.buf])
                    hi -= 1
                    a1 = 513
                rb = [b_FM[g]]
                if g - 1 >= 0:
                    rb.append(b_FM[g - 1])
                if g + 1 < NG:
                    rb.append(b_FM[g + 1])
                DMA(I_.ap[:, :, a0:a1], FM[4:8, :, lo:hi].rearrange("c p t -> p c t"), I_.sem, rb, [I_.buf])
                DMA(Gb.ap, FM[0:4, :, g * 512:(g + 1) * 512].rearrange("c p t -> p c t"), Gb.sem, [b_FM[g]], [Gb.buf])
            xs = []
            for i in range(4):
                t = xc[rc["xc"] % 8]
                rc["xc"] += 1
                r0 = g * 512 - t0 + i * 128
                ap, rb = x_src(l, s, r0, 128)
                DMA(t.ap, ap, t.sem, rb, [t.buf])
                xs.append(t)
            return xs

        def prep_parts(g):
            k = g % 2
            Y, Z, yz = Yb[k], SZ[k], YZ[k]

            def silu_():
                for hf in range(2):
                    ACT(Z.ap[:, 5 * hf:5 * hf + 5, :], Z.ap[:, 5 * hf:5 * hf + 5, :], AF.Silu, [Z.buf], [Z.buf])

            def yz_(hf):
                TT("dve", yz.ap[:, 5 * hf:5 * hf + 5, :], Y.ap[:, 5 * hf:5 * hf + 5, :], Z.ap[:, 5 * hf:5 * hf + 5, :],
                   ALU.mult, [Y.buf, Z.buf], [yz.buf])

            def conv_(c):
                e = l // 2
                I_, Gb = INb[k], GBb[k]
                cw = gcols[f"cw{e}"]
                ct = ctmp[rc["ct"] % 2]
                rc["ct"] += 1
                w0 = G.ap[:, cw + 0 * 4 + c:cw + 0 * 4 + c + 1]
                w1 = G.ap[:, cw + 1 * 4 + c:cw + 1 * 4 + c + 1]
                w2 = G.ap[:, cw + 2 * 4 + c:cw + 2 * 4 + c + 1]
                TS("dve", ct.ap, I_.ap[:, c, 0:512], w0, ALU.mult, [I_.buf, G.buf], [ct.buf])
                STT(ct.ap, I_.ap[:, c, 1:513], w1, ct.ap, ALU.mult, ALU.add, [I_.buf, G.buf, ct.buf], [ct.buf])
                STT(ct.ap, I_.ap[:, c, 2:514], w2, ct.ap, ALU.mult, ALU.add, [I_.buf, G.buf, ct.buf], [ct.buf])
                TT("dve", Y.ap[:, c, :], ct.ap, Gb.ap[:, c, :], ALU.mult, [ct.buf, Gb.buf], [Y.buf])

            def p0():
                silu_()
                if even:
                    conv_(0)
                yz_(1)

            def p1():
                if even:
                    conv_(1)

            def p2():
                if even:
                    conv_(2)

            def p3():
                if even:
                    conv_(3)
                yz_(0)
            return [p0, p1, p2, p3]

        def outproj_tile(g, xs, i):
            s = seg_of_group(g)
            t0, ln = segs[s]
            yz = YZ[g % 2]
            xo = xn[rc["xn"] % 3]
            rc["xn"] += 1
            for nh in range(2):
                bank = banks[rc["bank"] % 4]
                rc["bank"] += 1
                for c in range(10):
                    MM(bank.ap, yz.ap[:, c, i * 128:(i + 1) * 128], WO.ap[:, c, nh * 512:(nh + 1) * 512],
                       c == 0, c == 9, [yz.buf, WO.buf], [bank.buf])
                TT("dve", xo.ap[:, nh * 512:(nh + 1) * 512], bank.ap, xs[i].ap[:, nh * 512:(nh + 1) * 512], ALU.add,
                   [bank.buf, xs[i].buf], [xo.buf])
            r0 = g * 512 - t0 + i * 128
            DMA(y_out[s][r0:r0 + 128, :], xo.ap, xo.sem, [xo.buf], [b_Y[g]])

        xs_of = {}
        xs_of[0] = loads(0)
        if NG > 1:
            xs_of[1] = loads(1)
        for p_ in prep_parts(0):
            p_()
        for g in range(NG):
            parts = prep_parts(g + 1) if g + 1 < NG else [None] * 4
            xs = xs_of.pop(g)
            for i in range(4):
                if parts[i] is not None:
                    parts[i]()
                outproj_tile(g, xs, i)
            if g + 2 < NG:
                xs_of[g + 2] = loads(g + 2)

    P.marks = []

    def mark(name):
        P.marks.append((name, dict(P.count)))

    for l in range(DEPTH):
        P.barrier()
        mark(f"L{l} P")
        phase_P(l)
        P.barrier()
        mark(f"L{l} Amem")
        phase_A_mem(l)
        mark(f"L{l} A")
        if l % 2 == 0:
            phase_A_even(l)
        else:
            phase_A_odd(l)
        P.barrier()
        mark(f"L{l} C")
        phase_C(l)
    mark("end")
    P.wait_all("sp", b_Y)
    P.barrier()

    with nc.Block() as block:
        P.replay(block)
    return nc, P


_CONST_CACHE = {}


def _consts(smax):
    if smax not in _CONST_CACHE:
        _CONST_CACHE[smax] = (_const_mats(), _masks(), _rope_tables(smax))
    return _CONST_CACHE[smax]


def run(x_prompt, x_sample, mem_prompt, mem_sample, norm_g, w_in, w_out, mem_norm_g, w_mem_kv, mem_qk_g,
        conv_w, swa_qk_g, swa_sink, ax_qk_g, diff_qk_g, diff_lambda, diff_subln_g, depth=None, n_cores=8):
    f = lambda a: np.ascontiguousarray(np.asarray(a), dtype=np.float32)
    x_prompt, x_sample, mem_prompt, mem_sample = f(x_prompt), f(x_sample), f(mem_prompt), f(mem_sample)
    B, SP, _ = x_prompt.shape
    SS = x_sample.shape[1]
    DEPTH = int(depth if depth is not None else np.asarray(norm_g).shape[0])
    nc, P = build_program(SP, SS, DEPTH)
    cm, mk, tabs = _consts(max(SP, SS))
    gp, _ = _gpack(DEPTH, f(norm_g), f(mem_norm_g), f(mem_qk_g), f(conv_w), f(swa_qk_g), f(swa_sink),
                   f(ax_qk_g), f(diff_qk_g), f(diff_lambda), f(diff_subln_g))
    w_in, w_out, w_mem = f(w_in)[:DEPTH], f(w_out)[:DEPTH], f(w_mem_kv)[:DEPTH]
    in_maps = []
    for c in range(n_cores):
        in_maps.append({
            "x_p": x_prompt[c], "x_s": x_sample[c],
            "mem": np.ascontiguousarray(np.concatenate([mem_prompt[c], mem_sample[c]], axis=0)),
            "w_in": w_in, "w_out": w_out, "w_mem": w_mem, "gpack": gp, "cmat": cm, "masks": mk, "tabs": tabs,
        })
    res = run_bass_kernel_spmd(nc, in_maps, core_ids=list(range(n_cores)))
    yp = np.stack([np.asarray(r["y_p"], dtype=np.float32) for r in res.results])
    ysm = np.stack([np.asarray(r["y_s"], dtype=np.float32) for r in res.results])
    return yp, ysm


def kernel(x_prompt, x_sample, mem_prompt, mem_sample, norm_g, w_in, w_out, mem_norm_g, w_mem_kv, mem_qk_g,
           conv_w, swa_qk_g, swa_sink, ax_qk_g, diff_qk_g, diff_lambda, diff_subln_g):
    return run(x_prompt, x_sample, mem_prompt, mem_sample, norm_g, w_in, w_out, mem_norm_g, w_mem_kv, mem_qk_g,
               conv_w, swa_qk_g, swa_sink, ax_qk_g, diff_qk_g, diff_lambda, diff_subln_g)
```
